# Optimizing a Trainium2 kernel written in Bass

```python
import math
import jax, jax.numpy as jnp
from jax import lax
import numpy as np

D_MODEL = 1024
BATCH = 32
SEQ = 256
DEPTH = 4
DEC_BATCH = 2
DEC_SEQ = 4096
PAST_LEN = 256

GRID_W = 64
D_MIX = D_MODEL
D_GROUP = D_MIX // 4
HEAD_DIM = 64
H_A = D_GROUP // HEAD_DIM
H_D = D_GROUP // HEAD_DIM
N_DIR = 2
RANK_W = 64
RANK_A = 64
RANK_G = 128
POOL_WINDOWS = (2, 4, 8, 16)
N_POOL = len(POOL_WINDOWS)
POOL_CH = D_GROUP // N_POOL
CONV_W = 31
QK_CONV_W = 3
MLSTM_CHUNK = 64
D_FF = 4 * D_MODEL
EPS = 1e-6
RWKV_LN_EPS = 64e-5
CONV_LN_EPS = 1e-5

A_W = 3 * D_GROUP + RANK_W + RANK_A + RANK_G
B_W = D_GROUP
C_W = 2 * D_GROUP
D_W = 4 * D_GROUP + 2 * N_DIR * H_D
D_IN = A_W + B_W + C_W + D_W
IN_SPLITS = [A_W, A_W + B_W, A_W + B_W + C_W]
A_SPLITS = [D_GROUP, 2 * D_GROUP, 3 * D_GROUP, 3 * D_GROUP + RANK_W, 3 * D_GROUP + RANK_W + RANK_A]
D_SPLITS = [2 * D_GROUP, 3 * D_GROUP, 3 * D_GROUP + N_DIR * H_D, 3 * D_GROUP + 2 * N_DIR * H_D]

kernel_name = 'hybrid_rwkv7_pool_conformer_mlstm_dit_step'

F32 = jnp.float32


def rms_norm(x, g):
    xf = x.astype(F32)
    y = xf * lax.rsqrt(jnp.mean(xf * xf, axis=-1, keepdims=True) + EPS)
    return (y * g).astype(x.dtype)


def layer_norm(x, g, b, eps):
    xf = x.astype(F32)
    mu = jnp.mean(xf, axis=-1, keepdims=True)
    var = jnp.mean(jnp.square(xf - mu), axis=-1, keepdims=True)
    return ((xf - mu) * lax.rsqrt(var + eps) * g + b).astype(x.dtype)


def dwconv(x, w):
    k, ch = w.shape
    return lax.conv_general_dilated(x, w[:, None, :], window_strides=(1,), padding=[(k // 2, k // 2)],
                                    dimension_numbers=('NWC', 'WIO', 'NWC'), feature_group_count=ch)


def token_shift(x):
    zero = jnp.zeros_like(x[:, :1])
    prev = jnp.concatenate([zero, x[:, :-1]], axis=1)
    nxt = jnp.concatenate([x[:, 1:], zero], axis=1)
    return 0.5 * (prev + nxt)


def window_bounds(n, win):
    t = np.arange(n)
    return np.clip(t - win // 2, 0, n), np.clip(t + win // 2, 0, n)


def pool_seq(x, win):
    b, t, ch = x.shape
    cs = jnp.concatenate([jnp.zeros((b, 1, ch), x.dtype), jnp.cumsum(x, axis=1)], axis=1)
    lo, hi = window_bounds(t, win)
    cnt = (hi - lo).astype(np.float32)
    return (cs[:, hi] - cs[:, lo]) / cnt[None, :, None]


def pool_grid(x, win):
    b, t, ch = x.shape
    rows = t // GRID_W
    g = x.reshape(b, rows, GRID_W, ch)
    sat = jnp.pad(jnp.cumsum(jnp.cumsum(g, axis=1), axis=2), ((0, 0), (1, 0), (1, 0), (0, 0)))
    r_lo, r_hi = window_bounds(rows, win)
    c_lo, c_hi = window_bounds(GRID_W, win)
    band = sat[:, r_hi] - sat[:, r_lo]
    box = band[:, :, c_hi] - band[:, :, c_lo]
    cnt = np.outer(r_hi - r_lo, c_hi - c_lo).astype(np.float32)
    return (box / cnt[None, :, :, None]).reshape(b, t, ch)


def rwkv_scan(r, w, k, v, kk, a, s0, reverse):
    def step(s, inp):
        r_t, w_t, k_t, v_t, kk_t, a_t = inp
        sk = jnp.einsum('bhvk,bhk->bhv', s, kk_t)
        s = (s * w_t[:, :, None, :] - sk[..., None] * (kk_t * a_t)[:, :, None, :]
             + v_t[..., None] * k_t[:, :, None, :])
        return s, jnp.einsum('bhvk,bhk->bhv', s, r_t)
    xs = tuple(jnp.moveaxis(z, 1, 0) for z in (r, w, k, v, kk, a))
    s_final, y = lax.scan(step, s0, xs, reverse=reverse)
    return jnp.moveaxis(y, 0, 1), s_final


def rwkv_mix(z, s0, P, l):
    dt = z.dtype
    z = z + (token_shift(z) - z) * P['rwkv_mu'][l]
    r, k, v, wd, ad, gd = jnp.split(z, A_SPLITS, axis=-1)
    b, t, _ = r.shape
    hd = lambda u: u.astype(F32).reshape(b, t, H_A, HEAD_DIM)
    kk = hd(k * P['rwkv_k_k'][l])
    kk = kk * lax.rsqrt(jnp.sum(kk * kk, axis=-1, keepdims=True) + 1e-12)
    rh, vh = hd(r), hd(v)
    wt = jnp.tanh(wd)
    g = jax.nn.sigmoid(gd) @ P['rwkv_g_up'][l]
    ys, finals = [], []
    for d in range(N_DIR):
        w_log = -jax.nn.softplus(-(P['rwkv_w0'][l, d] + wt @ P['rwkv_w_up'][l, d])) - 0.5
        decay = jnp.exp(-jnp.exp(w_log.astype(F32)))
        a = jax.nn.sigmoid(P['rwkv_a0'][l, d] + ad @ P['rwkv_a_up'][l, d])
        kd = hd(k * (1.0 + (a - 1.0) * P['rwkv_k_a'][l]))
        yd, sd = rwkv_scan(rh, hd(decay), kd, vh, kk, hd(a), s0[:, d].astype(F32), reverse=(d == 1))
        bonus = jnp.sum(rh * kd * P['rwkv_r_k'][l], axis=-1, keepdims=True) * vh
        ys.append(yd + bonus)
        finals.append(sd)
    y = layer_norm(ys[0] + ys[1], P['rwkv_ln_g'][l].reshape(H_A, HEAD_DIM),
                   P['rwkv_ln_b'][l].reshape(H_A, HEAD_DIM), RWKV_LN_EPS)
    return (y.reshape(b, t, D_GROUP) * g).astype(dt), jnp.stack(finals, axis=1)


def pool_mix(z, P, l, grid):
    zf = z.astype(F32)
    outs = []
    for gi, win in enumerate(POOL_WINDOWS):
        zg = zf[..., gi * POOL_CH:(gi + 1) * POOL_CH]
        pooled = pool_grid(zg, win) if grid else pool_seq(zg, win)
        outs.append((pooled - zg).astype(z.dtype) @ P['pool_w'][l, gi])
    return jnp.concatenate(outs, axis=-1) * P['pool_scale'][l]


def conv_mix(z, P, l):
    val, gate = jnp.split(z, 2, axis=-1)
    u = val * jax.nn.sigmoid(gate)
    u = dwconv(u, P['conv_dw'][l]) + P['conv_b'][l]
    u = jax.nn.silu(layer_norm(u, P['conv_ln_g'][l], P['conv_ln_b'][l], CONV_LN_EPS))
    return u @ P['conv_pw'][l]


def mlstm_chunkwise(q, k, v, logi, logf, c0, n0, m0):
    b, t, h, dh = q.shape
    L = MLSTM_CHUNK
    nc = t // L
    chunks = lambda z: jnp.moveaxis(z.reshape((b, nc, L) + z.shape[2:]), 1, 0)
    causal = jnp.tril(jnp.ones((L, L), dtype=bool))[None, :, :, None]

    def step(carry, inp):
        c, n, m = carry
        qc, kc, vc, li, lf = inp
        bcum = jnp.cumsum(lf, axis=1)
        log_w = jnp.where(causal, bcum[:, :, None] - bcum[:, None] + li[:, None], -jnp.inf)
        m_t = jnp.maximum(bcum + m[:, None], jnp.max(log_w, axis=2))
        w = jnp.exp(log_w - m_t[:, :, None])
        inter = jnp.exp(bcum + m[:, None] - m_t)
        s = jnp.einsum('bthd,bshd->btsh', qc, kc) * w
        num = jnp.einsum('btsh,bshd->bthd', s, vc) + inter[..., None] * jnp.einsum('bhvk,bthk->bthv', c, qc)
        den = jnp.sum(s, axis=2) + inter * jnp.einsum('bhk,bthk->bth', n, qc)
        hc = num / jnp.maximum(jnp.abs(den), jnp.exp(-m_t))[..., None]
        m_new = m_t[:, -1]
        w_end = jnp.exp(bcum[:, -1:] - bcum + li - m_new[:, None])
        carry_decay = jnp.exp(bcum[:, -1] + m - m_new)
        c = carry_decay[..., None, None] * c + jnp.einsum('bsh,bshv,bshk->bhvk', w_end, vc, kc)
        n = carry_decay[..., None] * n + jnp.einsum('bsh,bshk->bhk', w_end, kc)
        return (c, n, m_new), hc

    (c, n, m), hs = lax.scan(step, (c0, n0, m0), tuple(chunks(z) for z in (q, k, v, logi, logf)))
    return jnp.moveaxis(hs, 0, 1).reshape(b, t, h, dh), (c, n, m)


def mlstm_mix(z, c0, n0, m0, P, l):
    dt = z.dtype
    qk, v, ipre, fpre, opre = jnp.split(z, D_SPLITS, axis=-1)
    qk = jax.nn.silu(dwconv(qk, P['mlstm_qk_conv'][l]))
    q, k = jnp.split(qk, 2, axis=-1)
    b, t, _ = q.shape
    hd = lambda u: u.astype(F32).reshape(b, t, H_D, HEAD_DIM)
    logi = (ipre + P['mlstm_i_bias'][l]).astype(F32).reshape(b, t, N_DIR, H_D)
    logf = jax.nn.log_sigmoid((fpre + P['mlstm_f_bias'][l]).astype(F32)).reshape(b, t, N_DIR, H_D)
    qh, kh, vh = hd(q), hd(k) * (1.0 / math.sqrt(HEAD_DIM)), hd(v)
    hs, cs, ns, ms = [], [], [], []
    for d in range(N_DIR):
        seqs = (qh, kh, vh, logi[:, :, d], logf[:, :, d])
        if d == 1:
            seqs = tuple(jnp.flip(u, axis=1) for u in seqs)
        hdir, (cd, nd, md) = mlstm_chunkwise(*seqs, c0[:, d].astype(F32), n0[:, d].astype(F32), m0[:, d].astype(F32))
        if d == 1:
            hdir = jnp.flip(hdir, axis=1)
        hs.append(hdir)
        cs.append(cd)
        ns.append(nd)
        ms.append(md)
    hsum = (hs[0] + hs[1]) * jax.nn.sigmoid(hd(opre))
    hsum = rms_norm(hsum, P['mlstm_hn_g'][l].reshape(H_D, HEAD_DIM)).reshape(b, t, D_GROUP)
    return hsum.astype(dt), (jnp.stack(cs, axis=1), jnp.stack(ns, axis=1), jnp.stack(ms, axis=1))


def trunk_layer(x, mod, P, l, states, grid):
    sh1, sc1, g1, sh2, sc2, g2 = jnp.split(mod[:, None, :], 6, axis=-1)
    h = rms_norm(x, P['norm1_g'][l]) * (1.0 + sc1) + sh1
    z = h @ P['w_in'][l]
    za, zb, zc, zd = jnp.split(z, IN_SPLITS, axis=-1)
    s_rwkv, s_c, s_n, s_m = states
    ya, f_rwkv = rwkv_mix(za, s_rwkv, P, l)
    yb = pool_mix(zb, P, l, grid)
    yc = conv_mix(zc, P, l)
    yd, (f_c, f_n, f_m) = mlstm_mix(zd, s_c, s_n, s_m, P, l)
    y = jnp.concatenate([ya, yb, yc, yd], axis=-1) @ P['w_out'][l]
    x = x + g1 * y
    h = rms_norm(x, P['norm2_g'][l]) * (1.0 + sc2) + sh2
    f = jnp.square(jax.nn.relu(h @ P['mlp_w1'][l] + P['mlp_b1'][l])) @ P['mlp_w2'][l] + P['mlp_b2'][l]
    x = x + g2 * f
    return x, (f_rwkv, f_c, f_n, f_m)


def setup_inputs(seed: int = 0) -> dict:
    key = jax.random.key(seed)
    ks = iter(jax.random.split(key, 64))
    nrm = lambda shape, s: s * jax.random.normal(next(ks), shape, F32)
    unif = lambda shape, lo, hi: jax.random.uniform(next(ks), shape, F32, lo, hi)
    gain = lambda shape: 1.0 + nrm(shape, 0.01)
    L, D = DEPTH, D_MODEL
    return {
        'x_prompt': nrm((BATCH, SEQ, D), 1.0),
        'x_sample': nrm((DEC_BATCH, DEC_SEQ, D), 1.0),
        'c': nrm((DEC_BATCH, D), 1.0),
        'state_rwkv': nrm((DEC_BATCH, L, N_DIR, H_A, HEAD_DIM, HEAD_DIM), 0.1),
        'state_mlstm_C': nrm((DEC_BATCH, L, N_DIR, H_D, HEAD_DIM, HEAD_DIM), 0.1),
        'state_mlstm_n': nrm((DEC_BATCH, L, N_DIR, H_D, HEAD_DIM), 0.1),
        'state_mlstm_m': nrm((DEC_BATCH, L, N_DIR, H_D), 1.0),
        'c_ctx': nrm((D,), 1.0),
        'w_mod': nrm((L, D, 6 * D), 0.5 * D ** -0.5),
        'b_mod': nrm((L, 6 * D), 0.01),
        'norm1_g': gain((L, D)),
        'norm2_g': gain((L, D)),
        'w_in': nrm((L, D, D_IN), D ** -0.5),
        'w_out': nrm((L, D_MIX, D), D_MIX ** -0.5),
        'rwkv_mu': unif((L, A_W), 0.0, 1.0),
        'rwkv_w0': unif((L, N_DIR, D_GROUP), -6.0, 1.0),
        'rwkv_w_up': nrm((L, N_DIR, RANK_W, D_GROUP), 0.1 * RANK_W ** -0.5),
        'rwkv_a0': nrm((L, N_DIR, D_GROUP), 0.1),
        'rwkv_a_up': nrm((L, N_DIR, RANK_A, D_GROUP), 0.1 * RANK_A ** -0.5),
        'rwkv_g_up': nrm((L, RANK_G, D_GROUP), RANK_G ** -0.5),
        'rwkv_k_k': 0.85 + nrm((L, D_GROUP), 0.02),
        'rwkv_k_a': gain((L, D_GROUP)),
        'rwkv_r_k': nrm((L, H_A, HEAD_DIM), 0.1),
        'rwkv_ln_g': gain((L, D_GROUP)),
        'rwkv_ln_b': nrm((L, D_GROUP), 0.01),
        'pool_w': nrm((L, N_POOL, POOL_CH, POOL_CH), POOL_CH ** -0.5),
        'pool_scale': gain((L, D_GROUP)),
        'conv_dw': nrm((L, CONV_W, D_GROUP), CONV_W ** -0.5),
        'conv_b': nrm((L, D_GROUP), 0.01),
        'conv_ln_g': gain((L, D_GROUP)),
        'conv_ln_b': nrm((L, D_GROUP), 0.01),
        'conv_pw': nrm((L, D_GROUP, D_GROUP), D_GROUP ** -0.5),
        'mlstm_qk_conv': nrm((L, QK_CONV_W, 2 * D_GROUP), QK_CONV_W ** -0.5),
        'mlstm_i_bias': nrm((L, N_DIR * H_D), 0.1),
        'mlstm_f_bias': jnp.linspace(3.0, 6.0, N_DIR * H_D, dtype=F32)[None, :] + nrm((L, N_DIR * H_D), 0.1),
        'mlstm_hn_g': gain((L, D_GROUP)),
        'mlp_w1': nrm((L, D, D_FF), D ** -0.5),
        'mlp_b1': nrm((L, D_FF), 0.01),
        'mlp_w2': nrm((L, D_FF, D), D_FF ** -0.5),
        'mlp_b2': nrm((L, D), 0.01),
        'final_g': gain((D,)),
    }


def reference(x_prompt, x_sample, c, state_rwkv, state_mlstm_C, state_mlstm_n, state_mlstm_m, c_ctx,
              w_mod, b_mod, norm1_g, norm2_g, w_in, w_out,
              rwkv_mu, rwkv_w0, rwkv_w_up, rwkv_a0, rwkv_a_up, rwkv_g_up, rwkv_k_k, rwkv_k_a, rwkv_r_k,
              rwkv_ln_g, rwkv_ln_b, pool_w, pool_scale, conv_dw, conv_b, conv_ln_g, conv_ln_b, conv_pw,
              mlstm_qk_conv, mlstm_i_bias, mlstm_f_bias, mlstm_hn_g,
              mlp_w1, mlp_b1, mlp_w2, mlp_b2, final_g):
    P = dict(norm1_g=norm1_g, norm2_g=norm2_g, w_in=w_in, w_out=w_out,
             rwkv_mu=rwkv_mu, rwkv_w0=rwkv_w0, rwkv_w_up=rwkv_w_up, rwkv_a0=rwkv_a0, rwkv_a_up=rwkv_a_up,
             rwkv_g_up=rwkv_g_up, rwkv_k_k=rwkv_k_k, rwkv_k_a=rwkv_k_a, rwkv_r_k=rwkv_r_k,
             rwkv_ln_g=rwkv_ln_g, rwkv_ln_b=rwkv_ln_b, pool_w=pool_w, pool_scale=pool_scale,
             conv_dw=conv_dw, conv_b=conv_b, conv_ln_g=conv_ln_g, conv_ln_b=conv_ln_b, conv_pw=conv_pw,
             mlstm_qk_conv=mlstm_qk_conv, mlstm_i_bias=mlstm_i_bias, mlstm_f_bias=mlstm_f_bias,
             mlstm_hn_g=mlstm_hn_g, mlp_w1=mlp_w1, mlp_b1=mlp_b1, mlp_w2=mlp_w2, mlp_b2=mlp_b2)
    bp = x_prompt.shape[0]
    zero_states = (jnp.zeros((bp, N_DIR, H_A, HEAD_DIM, HEAD_DIM), F32),
                   jnp.zeros((bp, N_DIR, H_D, HEAD_DIM, HEAD_DIM), F32),
                   jnp.zeros((bp, N_DIR, H_D, HEAD_DIM), F32),
                   jnp.zeros((bp, N_DIR, H_D), F32))
    xp, xs = x_prompt, x_sample
    new_r, new_c, new_n, new_m = [], [], [], []
    for l in range(DEPTH):
        mod_ctx = (jax.nn.silu(c_ctx) @ w_mod[l] + b_mod[l])[None, :]
        xp, (sr, sc, sn, sm) = trunk_layer(xp, mod_ctx, P, l, zero_states, grid=False)
        new_r.append(sr)
        new_c.append(sc)
        new_n.append(sn)
        new_m.append(sm)
        mod_lat = jax.nn.silu(c) @ w_mod[l] + b_mod[l]
        cached = (state_rwkv[:, l], state_mlstm_C[:, l], state_mlstm_n[:, l], state_mlstm_m[:, l])
        xs, _ = trunk_layer(xs, mod_lat, P, l, cached, grid=True)
    y_prompt = rms_norm(xp, final_g)
    y_sample = rms_norm(xs, final_g)
    dt = x_prompt.dtype
    return (y_prompt, y_sample,
            jnp.stack(new_r, axis=1).astype(dt), jnp.stack(new_c, axis=1).astype(dt),
            jnp.stack(new_n, axis=1).astype(dt), jnp.stack(new_m, axis=1).astype(dt))
```

```python
import contextlib
import math
import numpy as np
import ml_dtypes
import concourse.bass as bass
import concourse.mybir as mybir
from concourse.bass_utils import run_bass_kernel_spmd

F32, BF16 = mybir.dt.float32, mybir.dt.bfloat16
AF = mybir.ActivationFunctionType
ALU = mybir.AluOpType
AX = mybir.AxisListType

D = 1024
L = 4
T = 4096
NSEG = 16
DIN = 2832
DFF = 4096
NB = 8
ARENA = 52000
DEBUG = False
RW_STAGE = 3
RW_CUT = 9
EPS = 1e-6
RWKV_LN_EPS = 64e-5
CONV_LN_EPS = 1e-5
POOL_WINS = (2, 4, 8, 16)
POOL_DELTAS = {0: (-1, 0, 1), 1: (-1, 0, 1), 2: (-2, -1, 0, 1, 2), 3: (-4, -3, -2, -1, 0, 1, 2, 3, 4)}

C_A, C_B, C_C, C_QK, C_V, C_I, C_F, C_O = 0, 1024, 1280, 1792, 2304, 2560, 2568, 2576

COLS = {}
_off = 0
for _n, _w in [("mu", 8), ("w0", 4), ("a0", 4), ("kk", 2), ("ka", 2), ("rk", 2), ("lng", 2), ("lnb", 2),
               ("pscale", 2), ("convb", 2), ("clng", 2), ("clnb", 2), ("cdw", 62), ("qkc", 12), ("hng", 2),
               ("b1", 32)]:
    COLS[_n] = _off
    _off += _w
NCOL = _off


class Prog:
    def __init__(self, esem, dsems):
        self.q = {e: [] for e in ("pe", "act", "dve", "pool", "sp")}
        self.cnt = dict.fromkeys(self.q, 0)
        self.esem, self.dsems = esem, dsems
        self.duse = {e: [0] * len(dsems[e]) for e in dsems}
        self.dnext = {e: 0 for e in dsems}
        self.lastw, self.readers = {}, {}
        self.seen = {e: {} for e in self.q}
        self.alltok = {}

    def sem(self, sid):
        return self.esem[sid[1]] if sid[0] == "e" else self.dsems[sid[1]][sid[2]]

    def op(self, eng, fn, reads=(), writes=(), dma=False):
        need = {}

        def add(tok):
            if tok is not None and need.get(tok[0], 0) < tok[1]:
                need[tok[0]] = tok[1]

        for k in reads:
            add(self.lastw.get(k))
        for k in writes:
            add(self.lastw.get(k))
            for sid, val in self.readers.get(k, {}).items():
                add((sid, val))
        if dma:
            i = self.dnext[eng]
            self.dnext[eng] = (i + 1) % len(self.dsems[eng])
            sid = ("d", eng, i)
            if self.duse[eng][i] > 0:
                need[sid] = max(need.get(sid, 0), 16 * self.duse[eng][i])
            self.duse[eng][i] += 1
            tok = (sid, 16 * self.duse[eng][i])
        else:
            self.cnt[eng] += 1
            tok = (("e", eng), self.cnt[eng])
        waits = []
        for sid, val in need.items():
            if sid == ("e", "pe") and eng == "pe":
                continue
            if self.seen[eng].get(sid, 0) >= val:
                continue
            self.seen[eng][sid] = val
            waits.append((sid, val))
        self.q[eng].append((waits, fn, tok))
        for k in reads:
            r = self.readers.setdefault(k, {})
            if r.get(tok[0], 0) < tok[1]:
                r[tok[0]] = tok[1]
        for k in writes:
            self.lastw[k] = tok
            self.readers[k] = {}
        self.alltok[tok[0]] = tok[1]

    def barrier(self, engines=None):
        for eng in (engines or self.q):
            waits = []
            for sid, val in self.alltok.items():
                if self.seen[eng].get(sid, 0) >= val:
                    continue
                self.seen[eng][sid] = val
                waits.append((sid, val))
            self.q[eng].append((waits, None, None))

    def emit(self, eng, e):
        for waits, fn, tok in self.q[eng]:
            for sid, val in waits:
                e.wait_ge(self.sem(sid), val)
            if fn is not None:
                ins = fn(e)
                ins.then_inc(self.sem(tok[0]), 16 if tok[0][0] == "d" else 1)


def _colify(v):
    v = np.asarray(v, np.float32).reshape(-1, 128)
    return np.ascontiguousarray(v.T)


def _window_bounds(n, win):
    t = np.arange(n)
    return np.clip(t - win // 2, 0, n), np.clip(t + win // 2, 0, n)


def _pool_consts(grid):
    mats = np.zeros((4, 9, 2, 128, 128), np.float32)
    inv = np.zeros((128, 32, 4), np.float32)
    for g, win in enumerate(POOL_WINS):
        if grid:
            rlo, rhi = _window_bounds(64, win)
            clo, chi = _window_bounds(64, win)
            Mfull = None
            R = np.zeros((64, 64), np.float32)
            for r in range(64):
                R[r, rlo[r]:rhi[r]] = 1
            Cm = np.zeros((64, 64), np.float32)
            for c in range(64):
                Cm[c, clo[c]:chi[c]] = 1
            Mfull = np.kron(R, Cm)
            cnt = Mfull.sum(1)
        else:
            lo, hi = _window_bounds(256, win)
            M1 = np.zeros((256, 256), np.float32)
            for t in range(256):
                M1[t, lo[t]:hi[t]] = 1
            Mfull = np.kron(np.eye(16, dtype=np.float32), M1)
            cnt = Mfull.sum(1)
        inv[:, :, g] = (1.0 / cnt).reshape(32, 128).T
        for di, dl in enumerate(POOL_DELTAS[g]):
            for par in range(2):
                acc = None
                for i in range(par, 32, 2):
                    j = i + dl
                    if j < 0 or j >= 32:
                        continue
                    blk = Mfull[i * 128:(i + 1) * 128, j * 128:(j + 1) * 128]
                    if acc is None:
                        acc = blk
                    else:
                        assert np.array_equal(acc, blk), (g, dl, par, i)
                if acc is not None:
                    mats[g, di, par] = acc.T
    return mats, inv


def _prep_core(kind, xs, cvec, inp, sr=None, sc=None, sn=None, sm=None):
    m = {}
    m["x"] = np.ascontiguousarray(xs.reshape(T, D), np.float32)
    m["cvcol"] = _colify(cvec)
    cmv = 0.0 if kind == "p" else 1.0
    m["cm"] = np.full((128, 1), cmv, np.float32)
    mp = np.ones((128, 512), np.float32)
    mn = np.ones((128, 512), np.float32)
    if kind == "p":
        mp[:, 0] = 0; mp[:, 256] = 0
        mn[:, 255] = 0; mn[:, 511] = 0
    m["tsm"] = np.stack([mp, mn], 1).reshape(128, 1024)
    mats, inv = _pool_consts(kind == "s")
    m["poolm"] = np.ascontiguousarray(mats.transpose(3, 0, 1, 2, 4).reshape(128, 4 * 9 * 2 * 128))
    m["poolinv"] = np.ascontiguousarray(inv.reshape(128, 128))
    s0r = np.zeros((L, NSEG, 2, 128, 4, 64), np.float32)
    s0c = np.zeros((L, NSEG, 2, 128, 4, 65), np.float32)
    s0m = np.zeros((L, 2, 4, NSEG), np.float32)
    if kind == "s":
        for l in range(L):
            for d in range(2):
                seg = 0 if d == 0 else NSEG - 1
                for h in range(4):
                    hh = h % 2
                    s0r[l, seg, d, hh * 64:(hh + 1) * 64, h, :] = sr[l, d, h].T
                    s0c[l, seg, d, hh * 64:(hh + 1) * 64, h, :64] = sc[l, d, h].T
                    s0c[l, seg, d, hh * 64:(hh + 1) * 64, h, 64] = sn[l, d, h]
                    s0m[l, d, h, seg] = sm[l, d, h]
    m["s0r"] = s0r.reshape(L * NSEG * 2 * 128, 256)
    m["s0c"] = s0c.reshape(L * NSEG * 2 * 128, 260)
    m["s0m"] = s0m.reshape(L * 8, NSEG)
    return m


def _shared_consts(inp):
    m = {}
    f = lambda a: np.ascontiguousarray(a, np.float32)
    for k in ("w_mod", "w_in", "w_out", "mlp_w1", "mlp_w2", "conv_pw", "rwkv_g_up"):
        m[k] = f(inp[k]).reshape(-1, inp[k].shape[-1])
    m["b_mod"] = f(inp["b_mod"])
    m["rowp"] = f(np.concatenate([inp["norm1_g"], inp["norm2_g"], inp["mlp_b2"],
                                  np.broadcast_to(inp["final_g"], (L, D))], 1))
    cols = np.zeros((128, L, NCOL), np.float32)
    for l in range(L):
        def put(name, v):
            c = _colify(v)
            cols[:, l, COLS[name]:COLS[name] + c.shape[1]] = c
        put("mu", inp["rwkv_mu"][l])
        put("w0", inp["rwkv_w0"][l].reshape(-1))
        put("a0", inp["rwkv_a0"][l].reshape(-1))
        put("kk", inp["rwkv_k_k"][l]); put("ka", inp["rwkv_k_a"][l]); put("rk", inp["rwkv_r_k"][l].reshape(-1))
        put("lng", inp["rwkv_ln_g"][l]); put("lnb", inp["rwkv_ln_b"][l])
        put("pscale", inp["pool_scale"][l]); put("convb", inp["conv_b"][l])
        put("clng", inp["conv_ln_g"][l]); put("clnb", inp["conv_ln_b"][l])
        put("cdw", inp["conv_dw"][l].reshape(-1))
        put("qkc", inp["mlstm_qk_conv"][l].reshape(-1))
        put("hng", inp["mlstm_hn_g"][l]); put("b1", inp["mlp_b1"][l])
    m["cols"] = cols.reshape(128, L * NCOL)
    gb = np.zeros((128, L, 2), np.float32)
    for l in range(L):
        for d in range(2):
            for h in range(4):
                gb[d * 64 + h * 16:(d * 64 + h * 16 + 16), l, 0] = inp["mlstm_i_bias"][l, d * 4 + h]
                gb[d * 64 + h * 16:(d * 64 + h * 16 + 16), l, 1] = inp["mlstm_f_bias"][l, d * 4 + h]
    m["gbias"] = gb.reshape(128, L * 2)
    wa = np.concatenate([inp["rwkv_w_up"], inp["rwkv_a_up"]], 2)
    m["waup"] = f(wa).reshape(L * 2 * 128, 256)
    pw = np.zeros((L, 2, 128, 128), np.float32)
    for l in range(L):
        for g in range(4):
            p, o = g // 2, (g % 2) * 64
            pw[l, p, o:o + 64, o:o + 64] = inp["pool_w"][l, g]
    m["poolw"] = pw.reshape(L * 2 * 128, 128)
    c = {}
    c["ident"] = np.eye(128, dtype=np.float32)
    bo = np.zeros((128, 128), np.float32); bo[:64, :64] = 1; bo[64:, 64:] = 1
    c["bones"] = bo
    c["ones"] = np.ones((128, 128), np.float32)
    s = np.arange(64)
    ms = (s[:, None] < s[None, :]).astype(np.float32)
    mi = (s[:, None] <= s[None, :]).astype(np.float32)
    ml = (s[:, None] > s[None, :]).astype(np.float32)
    pad = lambda a: np.concatenate([np.tile(a, (1, 8)), np.zeros((64, 512), np.float32)], 0)
    blk = (s[:, None] // 16 == s[None, :] // 16).astype(np.float32)
    c["ms"] = pad(ms); c["mi"] = pad(mi); c["ml"] = pad(ml * blk)
    c["msd"] = pad(ms * blk); c["mll"] = pad(ml * (1 - blk))
    c["i8"] = pad(np.eye(64, dtype=np.float32))
    rm = np.ones((128, 512), np.float32); rm[:, ::64] = 0
    c["rmask"] = rm
    rn = np.zeros((128, 512), np.float32); rn[:, ::64] = -1e30
    c["rneg"] = rn
    pk = np.zeros((128, 16, 8), np.float32)
    for d in range(2):
        for h in range(4):
            for sg in range(16):
                pk[d * 64 + h * 16 + sg, sg, d * 4 + h] = 1
    c["pick"] = pk.reshape(128, 128)
    c["mi8"] = c["mi"] * 0.125
    p2 = np.zeros((128, 64), np.float32)
    for d in range(2):
        for h in range(4):
            p2[d * 32 + h, d * 4 + h] = 1
    c["pk2"] = p2
    jr = np.zeros((128, 64), np.float32)
    jr[np.arange(64), 63 - np.arange(64)] = 1
    c["jrev"] = jr
    order = ["ident", "bones", "ones", "ms", "mi", "ml", "i8", "rmask", "rneg", "pick", "mi8", "pk2", "jrev", "msd", "mll"]
    m["consts"] = np.concatenate([c[k] for k in order], 1)
    return m


CONST_OFF = {}
_o = 0
for _k, _w in [("ident", 128), ("bones", 128), ("ones", 128), ("ms", 512), ("mi", 512), ("ml", 512), ("i8", 512), ("rmask", 512),
               ("rneg", 512), ("pick", 128), ("mi8", 512), ("pk2", 64), ("jrev", 64), ("msd", 512), ("mll", 512)]:
    CONST_OFF[_k] = (_o, _w)
    _o += _w
NCONST = _o


def build(nlayers=L):
    nc = bass.Bass("TRN2", target_bir_lowering=False)
    din = lambda n, s, dt=F32: nc.dram_tensor(n, list(s), dt, kind="ExternalInput").ap()
    dout = lambda n, s, dt=F32: nc.dram_tensor(n, list(s), dt, kind="ExternalOutput").ap()
    dscr = lambda n, s, dt=F32: nc.dram_tensor(n, list(s), dt, kind="ExternalOutput" if DEBUG else "Internal").ap()
    x_in = din("x", [T, D]); cvcol = din("cvcol", [128, 8]); cm_d = din("cm", [128, 1])
    tsm_d = din("tsm", [128, 1024]); poolm_d = din("poolm", [128, 4 * 9 * 2 * 128]); poolinv_d = din("poolinv", [128, 128])
    s0r_d = din("s0r", [L * NSEG * 2 * 128, 256]); s0c_d = din("s0c", [L * NSEG * 2 * 128, 260]); s0m_d = din("s0m", [L * 8, NSEG])
    w_mod = din("w_mod", [L * D, 6 * D]); b_mod = din("b_mod", [L, 6 * D])
    w_in = din("w_in", [L * D, DIN]); w_out = din("w_out", [L * D, D])
    w1 = din("mlp_w1", [L * D, DFF]); w2 = din("mlp_w2", [L * DFF, D])
    conv_pw = din("conv_pw", [L * 256, 256]); g_up = din("rwkv_g_up", [L * 128, 256])
    rowp = din("rowp", [L, 4 * D]); cols_d = din("cols", [128, L * NCOL]); gbias_d = din("gbias", [128, L * 2])
    waup_d = din("waup", [L * 2 * 128, 256]); poolw_d = din("poolw", [L * 2 * 128, 128]); consts_d = din("consts", [128, NCONST])
    y_out = dout("y", [T, D]); st_r = dout("st_r", [L * NSEG * 2 * 2 * 128, 64])
    st_c = dout("st_c", [L * NSEG * 2 * 2 * 128, 65]); st_m = dout("st_m", [L * 8, NSEG])
    modrow = dscr("modrow", [L, 6 * D]); xs = dscr("xs", [T, D])
    zAT = dscr("zAT", [1024, T], BF16); zB = dscr("zB", [T, 256], BF16); zCT = dscr("zCT", [512, T], BF16)
    zQKT = dscr("zQKT", [512, T], BF16); zV = dscr("zV", [T, 256], BF16); zO = dscr("zO", [T, 256], BF16)
    zG = dscr("zG", [16, T]); yT = dscr("yT", [2 * 256, T]); bonT = dscr("bonT", [2 * 256, T])
    catT = dscr("catT", [1024, T], BF16); hm = dscr("hm", [2 * T, 256]); gtab = dscr("gtab", [4 * 128, 256])

    es = contextlib.ExitStack()
    with es:
        arena = es.enter_context(nc.sbuf_tensor("arena", [128, ARENA], F32))
        psall = es.enter_context(nc.psum_tensor("psall", [128, 4096], F32))
        esem = {e: es.enter_context(nc.semaphore("e_" + e)) for e in ("pe", "act", "dve", "pool", "sp")}
        dsems = {e: [es.enter_context(nc.semaphore("d_%s%d" % (e, i))) for i in range(12)] for e in ("sp", "pool")}
        block = es.enter_context(nc.Block())
        P = Prog(esem, dsems)
        PS = [psall[:, i * 512:(i + 1) * 512] for i in range(8)]
        PSK = ["ps%d" % i for i in range(8)]
        pstate = {"i": 0}

        def bank():
            i = pstate["i"]; pstate["i"] = (i + 1) % 8
            return PS[i], PSK[i]

        aoff = {"o": 0}

        def af(n):
            o = aoff["o"]; aoff["o"] = o + n
            aoff["max"] = max(aoff.get("max", 0), aoff["o"])
            assert aoff["o"] <= ARENA, aoff["o"]
            return arena[:, o:o + n]

        def ab(n):
            n32 = (n + 1) // 2
            return af(n32).bitcast(BF16)[:, 0:n]

        def mm(out, lhsT, rhs, st, sp_, r, w):
            P.op("pe", lambda e: e.matmul(out, lhsT, rhs, start=st, stop=sp_), r, w)

        def tr(out, in_, ident, r, w):
            P.op("pe", lambda e: e.transpose(out, in_, ident), r, w)

        def act(out, in_, func, r, w, bias=None, scale=None, accum=None):
            kw = {}
            if bias is not None: kw["bias"] = bias
            if scale is not None: kw["scale"] = scale
            if accum is not None: kw["accum_out"] = accum
            P.op("act", lambda e: e.activation(out=out, in_=in_, func=func, **kw), r, w)

        def tt(out, a, b, op, r, w, eng="dve"):
            P.op(eng, lambda e: e.tensor_tensor(out=out, in0=a, in1=b, op=op), r, w)

        def ts(out, a, s1, s2, op0, op1, r, w, eng="dve"):
            if op1 is None:
                P.op(eng, lambda e: e.tensor_scalar(out, a, s1, None, op0), r, w)
            else:
                P.op(eng, lambda e: e.tensor_scalar(out, a, s1, s2, op0, op1), r, w)

        def stt(out, a, s, b, op0, op1, r, w):
            P.op("dve", lambda e: e.scalar_tensor_tensor(out=out, in0=a, scalar=s, in1=b, op0=op0, op1=op1), r, w)

        def cp(out, in_, r, w, eng="dve"):
            P.op(eng, lambda e: e.tensor_copy(out, in_), r, w)

        def recip(out, in_, r, w):
            P.op("dve", lambda e: e.reciprocal(out, in_), r, w)

        def scan(out, d0, d1, init, op0, op1, r, w):
            P.op("dve", lambda e: e.tensor_tensor_scan(out=out, data0=d0, data1=d1, initial=init, op0=op0, op1=op1), r, w)

        def mset(ap, v, w, eng="dve"):
            P.op(eng, lambda e: e.memset(ap, v), (), w)

        def dma(out, in_, r, w, cast=False):
            eng = "pool" if cast else "sp"
            P.op(eng, lambda e: e.dma_start(out=out, in_=in_), r, w, dma=True)

        def rsqrt(out, in_, mul, eps, r, w):
            ts(out, in_, mul, eps, ALU.mult, ALU.add, r, w)
            act(out, out, AF.Sqrt, w, w)
            recip(out, out, w, w)

        CONST = af(NCONST); COLT = af(L * NCOL); CM = af(1); GB = af(L * 2); SC = af(8)
        IDB = ab(128); TSM = af(1024)
        ck = lambda n: CONST[:, CONST_OFF[n][0]:CONST_OFF[n][0] + CONST_OFF[n][1]]
        IDF, BONES, ONES = ck("ident"), ck("bones"), ck("ones")
        MS, MI, ML, I8, RMASK, RNEG, PICK = ck("ms"), ck("mi"), ck("ml"), ck("i8"), ck("rmask"), ck("rneg"), ck("pick")
        MI8, PK2, JF = ck("mi8"), ck("pk2"), ck("jrev")
        MSD, MLL, MLD = ck("msd"), ck("mll"), ML
        JB = ab(64)
        PICKv = PICK.rearrange("p (s j) -> p s j", j=8)
        ONEC = af(1); LN8C = af(1)
        dma(CONST, consts_d[:, :], [], ["CONST"]); dma(COLT, cols_d[:, :], [], ["COLT"]); dma(CM, cm_d[:, :], [], ["CM"])
        dma(GB, gbias_d[:, :], [], ["GB"]); dma(SC, cvcol[:, :], [], ["SC"]); dma(TSM, tsm_d[:, :], [], ["TSM"])
        cp(IDB, IDF, ["CONST"], ["IDB"])
        cp(JB[0:64, :], JF[0:64, 0:64], ["CONST"], ["JB"])
        mset(ONEC, 1.0, ["ONEC"]); mset(LN8C, -math.log(8.0), ["LN8C"])
        act(SC, SC, AF.Silu, ["SC"], ["SC"])
        col = lambda l, name, j=0: COLT[:, l * NCOL + COLS[name] + j: l * NCOL + COLS[name] + j + 1]
        persist_end = aoff["o"]

        def pool_phase(l):
            POOLM = ab(4 * 9 * 2 * 128); PINV = af(128)
            dma(POOLM, poolm_d[:, :], [], ["POOLM"], cast=True)
            dma(PINV, poolinv_d[:, :], [], ["PINV"])
            PWB = ab(2 * 128); PWBv = PWB.rearrange("p (j n) -> p j n", j=2)
            for j in range(2):
                dma(PWBv[:, j, :], poolw_d[(l * 2 + j) * 128:(l * 2 + j + 1) * 128, :], [], ["PWB"], cast=True)
            ZBT = ab(32 * 256); ZBTv = ZBT.rearrange("p (i c) -> p i c", i=32)
            dma(ZBTv, zB[:, :].rearrange("(i p) c -> p i c", p=128), [id(zB)], ["ZBT"])
            DIFF = ab(256); DT = ab(2 * 512); DTv = DT.rearrange("p (j t) -> p j t", j=2)
            YB = ab(2 * 512); YBv = YB.rearrange("p (j t) -> p j t", j=2)
            pm = lambda g, di, par: POOLM[:, ((g * 9 + di) * 2 + par) * 128:((g * 9 + di) * 2 + par + 1) * 128]
            for i in range(32):
                ps, pk = bank()
                for g in range(4):
                    dls = [(di, dl) for di, dl in enumerate(POOL_DELTAS[g]) if 0 <= i + dl < 32]
                    for n, (di, dl) in enumerate(dls):
                        mm(ps[:, g * 64:(g + 1) * 64], pm(g, di, i % 2), ZBTv[:, i + dl, g * 64:(g + 1) * 64],
                           n == 0, n == len(dls) - 1, ["POOLM", "ZBT"], [pk])
                for g in range(4):
                    stt(DIFF[:, g * 64:(g + 1) * 64], ps[:, g * 64:(g + 1) * 64], PINV[:, i * 4 + g:i * 4 + g + 1],
                        ZBTv[:, i, g * 64:(g + 1) * 64], ALU.mult, ALU.subtract, [pk, "PINV", "ZBT"], ["DIFF"])
                ps2, pk2 = bank(); psb = ps2.bitcast(BF16)
                for j in range(2):
                    tr(psb[:, j * 128:(j + 1) * 128], DIFF[:, j * 128:(j + 1) * 128], IDB, ["DIFF", "IDB"], [pk2])
                act(DTv[:, :, (i % 4) * 128:(i % 4 + 1) * 128], psb[:, 0:256].rearrange("p (j t) -> p j t", j=2), AF.Identity, [pk2], ["DT"])
                if i % 4 == 3:
                    b = i // 4
                    for j in range(2):
                        ps3, pk3 = bank()
                        mm(ps3, PWBv[:, j, :], DTv[:, j, :], True, True, ["PWB", "DT"], [pk3])
                        ts(YBv[:, j, :], ps3, col(l, "pscale", j), None, ALU.mult, None, [pk3, "COLT"], ["YB"])
                    dma(catT[256:512, b * 512:(b + 1) * 512].rearrange("(j p) t -> p j t", p=128), YBv, ["YB"], [id(catT)])

        def conv_phase(l):
            DIAG = ab(2 * 31 * 128); DIAGv = DIAG.rearrange("p (i k n) -> p i k n", i=2, k=31)
            for ti in range(2):
                for k in range(31):
                    ts(DIAGv[:, ti, k, :], IDF, col(l, "cdw", k * 2 + ti), None, ALU.mult, None, ["CONST", "COLT"], ["DIAG"])
            PWC = ab(2 * 256); PWCv = PWC.rearrange("p (k n) -> p k n", k=2)
            dma(PWCv, conv_pw[l * 256:(l + 1) * 256, :].rearrange("(k p) n -> p k n", p=128), [], ["PWC"], cast=True)
            RAWC = ab(4 * 286); RAWCv = RAWC.rearrange("p (j t) -> p j t", j=4)
            SG = af(2 * 286); SGv = SG.rearrange("p (j t) -> p j t", j=2)
            U = ab(2 * 286); Uv = U.rearrange("p (j t) -> p j t", j=2)
            v2 = lambda a: a.rearrange("p (j t) -> p j t", j=2)
            UC = af(512); CEN = af(512); SQ = af(512); RSTD = af(256); UN = af(512); UNB = ab(512); YC = ab(512)
            UCv, CENv, SQv, UNv, UNBv, YCv = v2(UC), v2(CEN), v2(SQ), v2(UN), v2(UNB), v2(YC)
            for sg in range(16):
                lo, hi = sg * 256 - 15, sg * 256 + 271
                clo, chi = max(lo, 0), min(hi, T)
                if clo != lo or chi != hi:
                    mset(RAWC, 0.0, ["RAWC"])
                dma(RAWCv[:, :, clo - lo:clo - lo + (chi - clo)], zCT[:, clo:chi].rearrange("(j p) t -> p j t", p=128), [id(zCT)], ["RAWC"])
                act(SGv, RAWCv[:, 2:4, :], AF.Sigmoid, ["RAWC"], ["SG"])
                tt(Uv, RAWCv[:, 0:2, :], SGv, ALU.mult, ["RAWC", "SG"], ["U"])
                ts(Uv[:, :, 0:15], Uv[:, :, 0:15], CM[:, 0:1], None, ALU.mult, None, ["U", "CM"], ["U"])
                ts(Uv[:, :, 271:286], Uv[:, :, 271:286], CM[:, 0:1], None, ALU.mult, None, ["U", "CM"], ["U"])
                for ti in range(2):
                    ps, pk = bank()
                    for k in range(31):
                        mm(ps[:, 0:256], DIAGv[:, ti, k, :], Uv[:, ti, k:k + 256], k == 0, k == 30, ["DIAG", "U"], [pk])
                    act(UCv[:, ti, :], ps[:, 0:256], AF.Identity, [pk, "COLT"], ["UC"], bias=col(l, "convb", ti))
                ps, pk = bank()
                for ti in range(2):
                    mm(ps[:, 0:256], ONES, UCv[:, ti, :], ti == 0, ti == 1, ["CONST", "UC"], [pk])
                for ti in range(2):
                    stt(CENv[:, ti, :], ps[:, 0:256], -1.0 / 256, UCv[:, ti, :], ALU.mult, ALU.add, [pk, "UC"], ["CEN"])
                act(SQ, CEN, AF.Square, ["CEN"], ["SQ"])
                ps, pk = bank()
                for ti in range(2):
                    mm(ps[:, 0:256], ONES, SQv[:, ti, :], ti == 0, ti == 1, ["CONST", "SQ"], [pk])
                rsqrt(RSTD, ps[:, 0:256], 1.0 / 256, CONV_LN_EPS, [pk], ["RSTD"])
                for ti in range(2):
                    tt(UNv[:, ti, :], CENv[:, ti, :], RSTD, ALU.mult, ["CEN", "RSTD"], ["UN"])
                    ts(UNv[:, ti, :], UNv[:, ti, :], col(l, "clng", ti), col(l, "clnb", ti), ALU.mult, ALU.add, ["UN", "COLT"], ["UN"])
                act(UNB, UN, AF.Silu, ["UN"], ["UNB"])
                for co in range(2):
                    ps, pk = bank()
                    for ci in range(2):
                        mm(ps[:, 0:256], PWCv[:, ci, co * 128:(co + 1) * 128], UNBv[:, ci, :], ci == 0, ci == 1, ["PWC", "UNB"], [pk])
                    act(YCv[:, co, :], ps[:, 0:256], AF.Identity, [pk], ["YC"])
                dma(catT[512:768, sg * 256:(sg + 1) * 256].rearrange("(j p) t -> p j t", p=128), YCv, ["YC"], [id(catT)])

        def rwkv_phase(l):
            WA = []
            for d in range(2):
                w_ = ab(256)
                dma(w_, waup_d[(l * 2 + d) * 128:(l * 2 + d + 1) * 128, :], [], ["WA%d" % d], cast=True)
                WA.append(w_)
            GUP = ab(256)
            dma(GUP, g_up[l * 128:(l + 1) * 128, :], [], ["GUP"], cast=True)
            MUH = af(8); OMU = af(8); OMKA = af(2)
            mu_c = COLT[:, l * NCOL + COLS["mu"]: l * NCOL + COLS["mu"] + 8]
            ka_c = COLT[:, l * NCOL + COLS["ka"]: l * NCOL + COLS["ka"] + 2]
            ts(MUH, mu_c, 0.5, None, ALU.mult, None, ["COLT"], ["MUH"])
            ts(OMU, mu_c, -1.0, 1.0, ALU.mult, ALU.add, ["COLT"], ["OMU"])
            ts(OMKA, ka_c, -1.0, 1.0, ALU.mult, ALU.add, ["COLT"], ["OMKA"])
            RAW = ab(7 * 514); RAWv = RAW.rearrange("p (k t) -> p k t", k=7)
            ZS = af(7 * 512); ZSv = ZS.rearrange("p (k t) -> p k t", k=7)
            T1 = af(512); T2 = af(512)
            post_off = aoff["o"]
            mk = lambda f: [[f() for _ in range(2)] for _ in range(2)]
            RS, KS, BS, KDS = mk(lambda: ab(512)), mk(lambda: ab(512)), mk(lambda: ab(512)), mk(lambda: ab(512))
            BT, KT = mk(lambda: ab(8 * 2 * 128)), mk(lambda: ab(8 * 2 * 128))
            VT = mk(lambda: ab(8 * 128)); GL = mk(lambda: af(8))
            WTB = ab(512); ADB = ab(512); VB = ab(512)
            LD, AA, KK, SQ, RN, KKN, KD, EP = [af(512) for _ in range(8)]
            BSE, KSE, KDSE = mk(lambda: ab(2 * 512)), mk(lambda: ab(2 * 512)), mk(lambda: ab(2 * 512))
            mset(WTB, 0.0, ["WTB"]); mset(ADB, 0.0, ["ADB"])
            XB = [ab(512), ab(512)]; XTB = [ab(512), ab(512)]; QB = [ab(512), ab(512)]
            AK, BB, BK, RR, UU = [ab(512) for _ in range(5)]
            WSB = af(512); NL = ab(512); R1 = ab(512)
            ST = ab(8 * 64); STv = ST.rearrange("p (u v) -> p u v", u=8)
            SI = [af(256), af(256)]; SOUT = af(256); SOUTv = SOUT.rearrange("p (q v) -> p q v", v=64)
            kname = lambda n, d, hp: "%s%d%d" % (n, d, hp)
            for d in range(2):
                for hp in range(2):
                    mset(BT[d][hp], 0.0, [kname("BT", d, hp)]); mset(KT[d][hp], 0.0, [kname("KT", d, hp)])
                    for (arr_, nm_) in ((BSE, "BSE"), (KSE, "KSE"), (KDSE, "KDSE")):
                        mset(arr_[d][hp], 0.0, [kname(nm_, d, hp)])
            mset(ST, 0.0, ["ST"])
            BTv = [[BT[d][hp].rearrange("p (c e n) -> p c e n", c=8, e=2) for hp in range(2)] for d in range(2)]
            KTv = [[KT[d][hp].rearrange("p (c e n) -> p c e n", c=8, e=2) for hp in range(2)] for d in range(2)]
            VTv = [[VT[d][hp].rearrange("p (c n) -> p c n", c=8) for hp in range(2)] for d in range(2)]

            def tshift(k, w, rev, out3, slot=None):
                ks_ = k
                k = k if slot is None else slot
                tt(T1, RAWv[:, k, 0:512], TSM[:, 0:512], ALU.mult, ["RAW", "TSM"], ["T1"])
                tt(T2, RAWv[:, k, 2:514], TSM[:, 512:1024], ALU.mult, ["RAW", "TSM"], ["T2"])
                tt(T1, T1, T2, ALU.add, ["T1", "T2"], ["T1"])
                ts(T1, T1, MUH[:, ks_:ks_ + 1], None, ALU.mult, None, ["T1", "MUH"], ["T1"])
                zo = out3[:, ::-1] if rev else out3
                stt(zo, RAWv[:, k, 1:513], OMU[:, ks_:ks_ + 1], T1, ALU.mult, ALU.add, ["RAW", "OMU", "T1"], ["ZS"])

            def load_raw(w, k0, k1, s0=None):
                s0 = k0 if s0 is None else s0
                t0 = w * 512
                lo, hi = t0 - 1, t0 + 513
                clo, chi = max(lo, 0), min(hi, T)
                if clo != lo:
                    mset(RAWv[:, :, 0:1], 0.0, ["RAW"])
                if chi != hi:
                    mset(RAWv[:, :, 513:514], 0.0, ["RAW"])
                dma(RAWv[:, s0:s0 + k1 - k0, clo - lo:clo - lo + chi - clo], zAT[k0 * 128:k1 * 128, clo:chi].rearrange("(k p) t -> p k t", p=128),
                    [id(zAT)], ["RAW"])

            def prep(d, w):
                t0 = w * 512
                load_raw(w, 0, 7)
                for k in range(7):
                    tshift(k, w, d == 1, ZSv[:, k, :])
                act(WTB[0:64, :], ZSv[0:64, 6, :], AF.Tanh, ["ZS"], ["WTB"])
                cp(ADB[64:128, :], ZSv[64:128, 6, :], ["ZS"], ["ADB"])
                for hp in range(2):
                    hs = slice(hp * 128, (hp + 1) * 128)
                    kn = lambda n: kname(n, d, hp)
                    ps, pk = bank()
                    mm(ps, WA[d][:, hs], WTB, True, True, ["WA%d" % d, "WTB"], [pk])
                    act(LD, ps, AF.Sigmoid, [pk, "COLT"], ["LD"], bias=col(l, "w0", d * 2 + hp))
                    ts(LD, LD, -math.exp(-0.5), None, ALU.mult, None, ["LD"], ["LD"])
                    ps, pk = bank()
                    mm(ps, WA[d][:, hs], ADB, True, True, ["WA%d" % d, "ADB"], [pk])
                    act(AA, ps, AF.Sigmoid, [pk, "COLT"], ["AA"], bias=col(l, "a0", d * 2 + hp))
                    ts(KK, ZSv[:, 2 + hp, :], col(l, "kk", hp), None, ALU.mult, None, ["ZS", "COLT"], ["KK"])
                    tt(SQ, KK, KK, ALU.mult, ["KK"], ["SQ"])
                    ps, pk = bank()
                    mm(ps, BONES, SQ, True, True, ["CONST", "SQ"], [pk])
                    rsqrt(RN, ps, 1.0, 1e-12, [pk], ["RN"])
                    tt(KKN, KK, RN, ALU.mult, ["KK", "RN"], ["KKN"])
                    ts(RN, AA, col(l, "ka", hp), OMKA[:, hp:hp + 1], ALU.mult, ALU.add, ["AA", "COLT", "OMKA"], ["RN"])
                    tt(KD, ZSv[:, 2 + hp, :], RN, ALU.mult, ["ZS", "RN"], ["KD"])
                    stt(SQ, ZSv[:, hp, :], col(l, "rk", hp), KD, ALU.mult, ALU.mult, ["ZS", "COLT", "KD"], ["SQ"])
                    ps, pk = bank()
                    mm(ps, BONES, SQ, True, True, ["CONST", "SQ"], [pk])
                    bo = KK[:, ::-1] if d == 1 else KK
                    tt(bo, ps, ZSv[:, 4 + hp, :], ALU.mult, [pk, "ZS"], ["KK"])
                    dma(bonT[d * 256 + hp * 128:d * 256 + (hp + 1) * 128, t0:t0 + 512], KK, ["KK"], ["bonT"])
                    scan(RN, RMASK, LD, 0.0, ALU.mult, ALU.add, ["CONST", "LD"], ["RN"])
                    tt(LD, RN, LD, ALU.subtract, ["RN", "LD"], ["LD"])
                    act(GL[d][hp], RN.rearrange("p (c t) -> p c t", t=64)[:, :, 63], AF.Exp, ["RN"], [kn("GL")])
                    act(EP, RN, AF.Exp, ["RN"], ["EP"])
                    tt(RS[d][hp], ZSv[:, hp, :], EP, ALU.mult, ["ZS", "EP"], [kn("RS")])
                    act(SQ, RN, AF.Exp, ["RN"], ["SQ"], scale=-1.0)
                    act(EP, LD, AF.Exp, ["LD"], ["EP"])
                    tt(KS[d][hp], KKN, EP, ALU.mult, ["KKN", "EP"], [kn("KS")])
                    tt(AA, AA, KKN, ALU.mult, ["AA", "KKN"], ["AA"])
                    tt(BS[d][hp], AA, SQ, ALU.mult, ["AA", "SQ"], [kn("BS")])
                    tt(KDS[d][hp], KD, SQ, ALU.mult, ["KD", "SQ"], [kn("KDS")])
                    for (src_, dst_, nm_) in ((KS, KSE, "KS"), (BS, BSE, "BS"), (KDS, KDSE, "KDS")):
                        dv_ = dst_[d][hp].rearrange("p (e t) -> p e t", e=2)
                        cp(dv_[0:64, 0, :], src_[d][hp][0:64, :], [kn(nm_)], [kn(nm_ + "E")], eng="pool")
                        cp(dv_[64:128, 1, :], src_[d][hp][64:128, :], [kn(nm_)], [kn(nm_ + "E")], eng="pool")
                    cp(VB, ZSv[:, 4 + hp, :], ["ZS"], ["VB"])
                    for (src, sk, dstv, dk, ex) in ((BS[d][hp], kn("BS"), BTv[d][hp], kn("BT"), True),
                                                    (KDS[d][hp], kn("KDS"), KTv[d][hp], kn("KT"), True),
                                                    (VB, "VB", VTv[d][hp], kn("VT"), False)):
                        ps, pk = bank(); psb = ps.bitcast(BF16)
                        for c in range(8):
                            tr(psb[0:64, c * 128:(c + 1) * 128], src[:, c * 64:(c + 1) * 64], IDB, [sk, "IDB"], [pk])
                        pv = psb[0:64, :].rearrange("p (c n) -> p c n", n=128)
                        if ex:
                            act(dstv[0:64, :, 0, 0:64], pv[:, :, 0:64], AF.Identity, [pk], [dk])
                            act(dstv[0:64, :, 1, 64:128], pv[:, :, 64:128], AF.Identity, [pk], [dk])
                        else:
                            act(dstv[0:64, :, :], pv, AF.Identity, [pk], [dk])

            UNITS = [(d * 4 + h, d, h, h // 2, h % 2) for d in range(2) for h in range(4)]
            psl = lambda hh: slice(hh * 64, hh * 64 + 64)
            ALLK = [kname(n, d, hp) for n in ("RS", "KS", "BS", "KDS", "KSE", "BSE", "KDSE") for d in range(2) for hp in range(2)]
            VTK = [kname("VT", d, hp) for d in range(2) for hp in range(2)]
            BTK = [kname(n, d, hp) for n in ("BT", "KT") for d in range(2) for hp in range(2)]
            GLK = [kname("GL", d, hp) for d in range(2) for hp in range(2)]

            def step(wf, wb, c8):
                tsl = slice(c8 * 64, (c8 + 1) * 64)

                def allmm(lf, rf, reads):
                    ps, pk = bank()
                    for (u, d, h, hp, hh) in UNITS:
                        cs = slice(u * 64, (u + 1) * 64)
                        mm(ps[0:64, cs], lf(u, d, hp, hh, cs), rf(u, d, hp, hh, cs), True, True, reads, [pk])
                    return ps, pk
                FK = lambda arr: (lambda u, d, hp, hh, cs: arr[d][hp][:, tsl])
                FE = lambda arr: (lambda u, d, hp, hh, cs: arr[d][hp].rearrange("p (e t) -> p e t", e=2)[:, hh, tsl])
                SB = lambda buf: (lambda u, d, hp, hh, cs: buf[0:64, cs])
                VU = lambda u, d, hp, hh, cs: VTv[d][hp][0:64, c8, hh * 64:(hh + 1) * 64]
                ps, pk = allmm(FE(BSE), FK(KS), ALLK)
                tt(XTB[0][0:64, :], ps[0:64, :], MSD[0:64, :], ALU.mult, [pk, "CONST"], ["XT0"])
                tt(QB[0][0:64, :], I8[0:64, :], XTB[0][0:64, :], ALU.subtract, ["CONST", "XT0"], ["Q0"])
                ps, pk = allmm(FE(KSE), FK(BS), ALLK)
                tt(XB[0][0:64, :], ps[0:64, :], MLD[0:64, :], ALU.mult, [pk, "CONST"], ["X0"])
                tt(NL[0:64, :], ps[0:64, :], MLL[0:64, :], ALU.mult, [pk, "CONST"], ["NL"])
                ps, pk = allmm(FE(KDSE), FK(KS), ALLK)
                tt(AK[0:64, :], ps[0:64, :], MS[0:64, :], ALU.mult, [pk, "CONST"], ["AK"])
                ps, pk = allmm(FE(BSE), FK(RS), ALLK)
                tt(BB[0:64, :], ps[0:64, :], MI[0:64, :], ALU.mult, [pk, "CONST"], ["BB"])
                ps, pk = allmm(FE(KDSE), FK(RS), ALLK)
                tt(BK[0:64, :], ps[0:64, :], MI[0:64, :], ALU.mult, [pk, "CONST"], ["BK"])
                if RW_CUT < 2:
                    return
                cur = 0
                for lev in range(1, 4):
                    nxt = 1 - cur
                    ps, pk = allmm(SB(XTB[cur]), SB(XB[cur]), ["XT%d" % cur, "X%d" % cur])
                    ps2 = None
                    if lev < 3:
                        ps2, pk2 = allmm(SB(XB[cur]), SB(XTB[cur]), ["XT%d" % cur, "X%d" % cur])
                    act(XB[nxt][0:64, :], ps[0:64, :], AF.Identity, [pk], ["X%d" % nxt])
                    if ps2 is not None:
                        act(XTB[nxt][0:64, :], ps2[0:64, :], AF.Identity, [pk2], ["XT%d" % nxt])
                    ps, pk = allmm(SB(XB[nxt]), SB(QB[cur]), ["X%d" % nxt, "Q%d" % cur])
                    tt(QB[nxt][0:64, :], ps[0:64, :], QB[cur][0:64, :], ALU.add, [pk, "Q%d" % cur], ["Q%d" % nxt])
                    cur = nxt
                qd = cur
                if RW_CUT < 3:
                    return
                YB, YTB, P1B, ZB = XB[0], XTB[0], XB[1], XTB[1]
                ps, pk = allmm(SB(QB[qd]), SB(NL), ["Q%d" % qd, "NL"])
                act(YB[0:64, :], ps[0:64, :], AF.Identity, [pk], ["X0"])
                ps, pk = allmm(SB(NL), SB(QB[qd]), ["Q%d" % qd, "NL"])
                cp(YTB[0:64, :], ps[0:64, :], [pk], ["XT0"])
                stt(WSB[0:64, :], ps[0:64, :], -1.0, I8[0:64, :], ALU.mult, ALU.add, ["CONST", pk], ["WSB"])
                ps, pk = allmm(SB(YB), SB(YTB), ["X0", "XT0"])
                cp(P1B[0:64, :], ps[0:64, :], [pk], ["X1"])
                tt(WSB[0:64, :], ps[0:64, :], WSB[0:64, :], ALU.add, ["WSB", pk], ["WSB"])
                ps, pk = allmm(SB(YB), SB(P1B), ["X0", "X1"])
                stt(ZB[0:64, :], ps[0:64, :], -1.0, WSB[0:64, :], ALU.mult, ALU.add, ["WSB", pk], ["XT1"])
                if RW_CUT < 4:
                    return
                ps, pk = allmm(SB(AK), VU, ["AK"] + VTK)
                act(WSB[0:64, :], ps[0:64, :], AF.Identity, [pk], ["WSB"])
                ps, pk = allmm(FK(KS), lambda u, d, hp, hh, cs: STv[:, u, :], ALLK + ["ST"])
                tt(RR[0:64, :], ps[0:64, :], WSB[0:64, :], ALU.add, [pk, "WSB"], ["RR"])
                ps, pk = allmm(SB(QB[qd]), SB(RR), ["Q%d" % qd, "RR"])
                act(R1[0:64, :], ps[0:64, :], AF.Identity, [pk], ["R1"])
                ps, pk = allmm(SB(ZB), SB(R1), ["XT1", "R1"])
                ts(UU[0:64, :], ps[0:64, :], -1.0, None, ALU.mult, None, [pk], ["UU"])
                if RW_CUT < 5:
                    return
                psy, pky = bank()
                pss, pks = bank()
                for (u, d, h, hp, hh) in UNITS:
                    cs = slice(u * 64, (u + 1) * 64)
                    vu = VU(u, d, hp, hh, cs)
                    mm(psy[0:64, cs], STv[:, u, :], RS[d][hp][:, tsl], True, False, ALLK + ["ST"], [pky])
                    mm(psy[0:64, cs], UU[0:64, cs], BB[0:64, cs], False, False, ["UU", "BB"], [pky])
                    mm(psy[0:64, cs], vu, BK[0:64, cs], False, True, ["BK"] + VTK, [pky])
                if RW_CUT < 6:
                    return
                for (u, d, h, hp, hh) in UNITS:
                    cs = slice(u * 64, (u + 1) * 64)
                    vu = VU(u, d, hp, hh, cs)
                    mm(pss[:, cs], BTv[d][hp][0:64, c8, hh, :], UU[0:64, cs], True, False, ["UU"] + BTK, [pks])
                    mm(pss[:, cs], KTv[d][hp][0:64, c8, hh, :], vu, False, False, BTK + VTK, [pks])
                    mm(pss[:, cs], IDB, STv[:, u, :], False, True, ["IDB", "ST"], [pks])
                for d in range(2):
                    for hp in range(2):
                        u0 = d * 4 + hp * 2
                        ts(STv[:, u0:u0 + 2, :], pss[:, u0 * 64:(u0 + 2) * 64].rearrange("p (u v) -> p u v", v=64),
                           GL[d][hp][:, c8:c8 + 1], None, ALU.mult, None, [pks] + GLK, ["ST"])
                if RW_CUT < 7:
                    return
                act(WSB[0:64, 0:256], psy[0:64, 0:256], AF.Identity, [pky], ["WSB"])
                act(WSB[0:64, 256:512].rearrange("p (h t) -> p h t", t=64)[:, :, ::-1],
                    psy[0:64, 256:512].rearrange("p (h t) -> p h t", t=64), AF.Identity, [pky], ["WSB"])
                for d in range(2):
                    tok0 = wf * 512 + c8 * 64 if d == 0 else wb * 512 + 512 - (c8 + 1) * 64
                    dma(yT[d * 256:(d + 1) * 256, tok0:tok0 + 64].rearrange("(h v) t -> v h t", v=64),
                        WSB[0:64, d * 256:(d + 1) * 256].rearrange("p (h t) -> p h t", t=64), ["WSB"], ["yT"])
                if RW_CUT < 8:
                    return
                if c8 % 4 == 3:
                    tt(SOUTv, STv.rearrange("p (q e) v -> p q e v", e=2)[:, :, 0, :], STv.rearrange("p (q e) v -> p q e v", e=2)[:, :, 1, :],
                       ALU.add, ["ST"], ["SOUT"])
                    for d in range(2):
                        seg = (2 * wf + c8 // 4) if d == 0 else (2 * wb + 1 - c8 // 4)
                        base = ((l * NSEG + seg) * 2 + d) * 256
                        dma(st_r[base:base + 256, :].rearrange("(q p) v -> p q v", p=128), SOUTv[:, d * 2:(d + 1) * 2, :], ["SOUT"], ["st_r"])
                        segn = seg + 1 if d == 0 else seg - 1
                        if 0 <= segn < NSEG:
                            rb = ((l * NSEG + segn) * 2 + d) * 128
                            dma(SI[d], s0r_d[rb:rb + 128, :], [], ["SI%d" % d])
                            stt(STv[:, d * 4:(d + 1) * 4, :], STv[:, d * 4:(d + 1) * 4, :], CM[:, 0:1],
                                SI[d].rearrange("p (h v) -> p h v", v=64), ALU.mult, ALU.add, ["ST", "CM", "SI%d" % d], ["ST"])

            for d in range(2):
                seg = 0 if d == 0 else NSEG - 1
                rb = ((l * NSEG + seg) * 2 + d) * 128
                dma(SI[d], s0r_d[rb:rb + 128, :], [], ["SI%d" % d])
                cp(STv[:, d * 4:(d + 1) * 4, :], SI[d].rearrange("p (h v) -> p h v", v=64), ["SI%d" % d], ["ST"])
            for j in range(8 if RW_STAGE >= 3 else 1):
                prep(0, j)
                prep(1, 7 - j)
                if RW_STAGE >= 2:
                    for c8 in range(8):
                        step(j, 7 - j, c8)
            P.barrier()
            aoff["o"] = post_off
            Y0, Y1, Y2, Y3, CEN2, SQ2, RS2, YO = [af(512) for _ in range(8)]
            SGB = ab(512); GZ = af(512); CATA = ab(512)
            for w in range(8):
                t0 = w * 512
                load_raw(w, 7, 8, 0)
                tshift(7, w, False, GZ, slot=0)
                act(SGB, GZ, AF.Sigmoid, ["ZS"], ["SGB"])
                for hp in range(2):
                    rows = lambda d: slice(d * 256 + hp * 128, d * 256 + (hp + 1) * 128)
                    dma(Y0, yT[rows(0), t0:t0 + 512], ["yT"], ["Y0"]); dma(Y1, yT[rows(1), t0:t0 + 512], ["yT"], ["Y1"])
                    dma(Y2, bonT[rows(0), t0:t0 + 512], ["bonT"], ["Y2"]); dma(Y3, bonT[rows(1), t0:t0 + 512], ["bonT"], ["Y3"])
                    tt(Y0, Y0, Y1, ALU.add, ["Y0", "Y1"], ["Y0"])
                    tt(Y2, Y2, Y3, ALU.add, ["Y2", "Y3"], ["Y2"])
                    tt(Y0, Y0, Y2, ALU.add, ["Y0", "Y2"], ["Y0"])
                    ps, pk = bank()
                    mm(ps, BONES, Y0, True, True, ["CONST", "Y0"], [pk])
                    stt(CEN2, ps, -1.0 / 64, Y0, ALU.mult, ALU.add, [pk, "Y0"], ["CEN2"])
                    act(SQ2, CEN2, AF.Square, ["CEN2"], ["SQ2"])
                    ps, pk = bank()
                    mm(ps, BONES, SQ2, True, True, ["CONST", "SQ2"], [pk])
                    rsqrt(RS2, ps, 1.0 / 64, RWKV_LN_EPS, [pk], ["RS2"])
                    tt(YO, CEN2, RS2, ALU.mult, ["CEN2", "RS2"], ["YO"])
                    ts(YO, YO, col(l, "lng", hp), col(l, "lnb", hp), ALU.mult, ALU.add, ["YO", "COLT"], ["YO"])
                    ps, pk = bank()
                    mm(ps, GUP[:, hp * 128:(hp + 1) * 128], SGB, True, True, ["GUP", "SGB"], [pk])
                    tt(CATA, YO, ps, ALU.mult, ["YO", pk], ["CATA"])
                    dma(catT[hp * 128:(hp + 1) * 128, t0:t0 + 512], CATA, ["CATA"], [id(catT)])

        def mlstm_phase(l):
            TG, GI, GF, LI, LF, BCUM, DD, GG, ED, CL, WE, TMPG = [af(256) for _ in range(12)]
            NFB = af(1); EBL = af(4)
            v4 = lambda a: a.rearrange("p (c t) -> p c t", t=64)
            for (gi, dst, dk) in ((0, GI, "GI"), (1, GF, "GF")):
                for d in range(2):
                    dma(TG[d * 64:(d + 1) * 64, :], zG[gi * 8 + d * 4:gi * 8 + (d + 1) * 4, :].rearrange("h (s t) -> (h s) t", t=256), ["zG"], ["TG"])
                cp(dst[0:64, :], TG[0:64, :], ["TG"], [dk])
                cp(dst[64:128, :], TG[64:128, ::-1], ["TG"], [dk])
            ts(LI, GI, GB[:, l * 2:l * 2 + 1], None, ALU.add, None, ["GI", "GB"], ["LI"])
            ts(NFB, GB[:, l * 2 + 1:l * 2 + 2], -1.0, None, ALU.mult, None, ["GB"], ["NFB"])
            act(LF, GF, AF.Exp, ["GF", "NFB"], ["LF"], bias=NFB[:, 0:1], scale=-1.0)
            act(LF, LF, AF.Ln, ["LF", "ONEC"], ["LF"], bias=ONEC[:, 0:1])
            ts(LF, LF, -1.0, None, ALU.mult, None, ["LF"], ["LF"])
            scan(BCUM, RMASK[:, 0:256], LF, 0.0, ALU.mult, ALU.add, ["CONST", "LF"], ["BCUM"])
            tt(DD, LI, BCUM, ALU.subtract, ["LI", "BCUM"], ["DD"])
            scan(GG, RNEG[:, 0:256], DD, -1e30, ALU.add, ALU.max, ["CONST", "DD"], ["GG"])
            CHG = af(64); CHB = af(64); S0M = af(16); MOUT = af(16); MC = af(1); EMI = af(16); EMO = af(16)
            XR = af(128); EMIR = af(128); EMOR = af(128); XA = af(512); ALT = af(512)
            for t_ in (CHG, CHB, S0M, MOUT, MC):
                mset(t_, 0.0, [id(t_)])
            GC4 = af(4); BL4 = af(4)
            cp(GC4, v4(GG)[:, :, 63], ["GG"], ["GC4"]); cp(BL4, v4(BCUM)[:, :, 63], ["BCUM"], ["BL4"])
            dma(gtab[0:128, 0:4], GC4, ["GC4"], ["gtab"])
            dma(gtab[128:256, 0:4], BL4, ["BL4"], ["gtab"])
            for d in range(2):
                rs = slice(d * 32, d * 32 + 4)
                dma(CHG[rs, :].rearrange("p (s c) -> p s c", c=4), gtab[d * 64:(d + 1) * 64, 0:4].rearrange("(h s) c -> h s c", s=16), ["gtab"], [id(CHG)])
                dma(CHB[rs, :].rearrange("p (s c) -> p s c", c=4), gtab[128 + d * 64:128 + (d + 1) * 64, 0:4].rearrange("(h s) c -> h s c", s=16), ["gtab"], [id(CHB)])
                dma(S0M[rs, :], s0m_d[l * 8 + d * 4:l * 8 + (d + 1) * 4, :], [], [id(S0M)])
            for d in range(2):
                rs = slice(d * 32, d * 32 + 4)
                mk_ = "MC%d" % d
                for n in range(64):
                    seg = n // 4 if d == 0 else 15 - n // 4
                    colt = seg * 4 + n % 4
                    if n % 4 == 0:
                        if n == 0:
                            cp(MC[rs, :], S0M[rs, seg:seg + 1], [id(S0M)], [mk_], eng="dve")
                        else:
                            ts(MC[rs, :], MC[rs, :], CM[rs, 0:1], None, ALU.mult, None, [mk_, "CM"], [mk_], eng="dve")
                            tt(MC[rs, :], MC[rs, :], S0M[rs, seg:seg + 1], ALU.add, [mk_, id(S0M)], [mk_], eng="dve")
                    tt(MC[rs, :], MC[rs, :], CHG[rs, colt:colt + 1], ALU.max, [mk_, id(CHG)], [mk_], eng="dve")
                    tt(MC[rs, :], MC[rs, :], CHB[rs, colt:colt + 1], ALU.add, [mk_, id(CHB)], [mk_], eng="dve")
                    if n % 4 == 3:
                        cp(MOUT[rs, seg:seg + 1], MC[rs, :], [mk_], ["MOUT%d" % d], eng="dve")
                dma(st_m[l * 8 + d * 4:l * 8 + (d + 1) * 4, :], MOUT[rs, :], ["MOUT%d" % d, id(MOUT)], ["st_m"])
            act(EMI[0:64, :], S0M[0:64, :], AF.Exp, [id(S0M)], ["EMI"])
            act(EMO[0:64, :], MOUT[0:64, :], AF.Exp, [id(MOUT), "MOUT0", "MOUT1"], ["EMO"], scale=-1.0)
            for (src, sk, dst, dk) in ((EMI, "EMI", EMIR, "EMIR"), (EMO, "EMO", EMOR, "EMOR")):
                tt(XR[0:64, :].rearrange("p (s j) -> p s j", j=8), src[0:64, :].unsqueeze(2).broadcast_to([64, 16, 8]),
                   PK2[0:64, 0:8].unsqueeze(1).broadcast_to([64, 16, 8]), ALU.mult, [sk, "CONST"], ["XR"])
                ps, pk = bank()
                mm(ps[:, 0:128], ONES[0:64, :], XR[0:64, :], True, True, ["CONST", "XR"], [pk])
                cp(dst, ps[:, 0:128], [pk], [dk])
            act(ED, DD, AF.Exp, ["DD"], ["ED"])
            act(CL, BCUM, AF.Exp, ["BCUM"], ["CL"], scale=-1.0)
            tt(v4(TMPG), v4(DD), v4(BCUM)[:, :, 63:64].broadcast_to([128, 4, 64]), ALU.add, ["DD", "BCUM"], ["TMPG"])
            act(WE, TMPG, AF.Exp, ["TMPG", "LN8C"], ["WE"], bias=LN8C[:, 0:1])
            act(EBL, v4(BCUM)[:, :, 63], AF.Exp, ["BCUM"], ["EBL"])
            tt(XA.rearrange("p (s c j) -> p s c j", s=16, c=4), PICKv.unsqueeze(2).broadcast_to([128, 16, 4, 8]),
               EBL.unsqueeze(1).unsqueeze(3).broadcast_to([128, 16, 4, 8]), ALU.mult, ["CONST", "EBL"], ["XA"])
            ps, pk = bank()
            mm(ps, ONES, XA, True, True, ["CONST", "XA"], [pk])
            cp(ALT, ps, [pk], ["ALT"])
            ALTv = ALT.rearrange("p (n j) -> p n j", j=8)
            EDT, WET, CLT = af(512), af(512), af(512)
            for (tab, tk, dst, dk) in ((ED, "ED", EDT, "EDT"), (WE, "WE", WET, "WET"), (CL, "CL", CLT, "CLT")):
                ps, pk = bank()
                for sg in range(16):
                    for c in range(4):
                        n = sg * 4 + c
                        mm(ps[0:64, n * 8:(n + 1) * 8], tab[:, c * 64:(c + 1) * 64], PICKv[:, sg, :], True, True, [tk, "CONST"], [pk])
                cp(dst[0:64, :], ps[0:64, :], [pk], [dk])
            EDTv, WETv, CLTv = [a.rearrange("p (n j) -> p n j", j=8) for a in (EDT, WET, CLT)]
            EMIRv = EMIR.rearrange("p (s j) -> p s j", j=8); EMORv = EMOR.rearrange("p (s j) -> p s j", j=8)
            RAWQ = ab(4 * 514); RAWQv = RAWQ.rearrange("p (k t) -> p k t", k=4)
            Q1, Q2, Q3 = af(512), af(512), af(512)
            QK = [[ab(512) for _ in range(4)] for _ in range(2)]
            KE = [[ab(2 * 512) for _ in range(2)] for _ in range(2)]
            KEv = [[KE[d][hp].rearrange("p (e t) -> p e t", e=2) for hp in range(2)] for d in range(2)]
            KH = [[ab(8 * 2 * 128) for _ in range(2)] for _ in range(2)]
            KHv = [[KH[d][hp].rearrange("p (c e n) -> p c e n", c=8, e=2) for hp in range(2)] for d in range(2)]
            VA = [ab(8 * 4 * 65), ab(8 * 4 * 65)]
            VAv = [VA[d].rearrange("p (c h n) -> p c h n", c=8, h=4) for d in range(2)]
            CT = af(8 * 65); CTv = CT.rearrange("p (u n) -> p u n", n=65)
            CB = ab(8 * 65); CBv = CB.rearrange("p (u n) -> p u n", n=65)
            VN = ab(8 * 256); HMR = af(256)
            SR = ab(512); TMPS = af(512); HM = af(512); DN = af(8); TMPC = af(260); SIC = af(260); CO = af(130)
            TMPCv = TMPC.rearrange("p (h n) -> p h n", n=65); SICv = SIC.rearrange("p (h n) -> p h n", n=65); COv = CO.rearrange("p (q n) -> p q n", n=65)
            for d in range(2):
                mset(VA[d], 1.0, ["VA%d" % d])
                for hp in range(2):
                    mset(KH[d][hp], 0.0, ["KH%d%d" % (d, hp)])
                    mset(KE[d][hp], 0.0, ["KE%d%d" % (d, hp)])
            mset(CT, 0.0, ["CT"])

            def seg_of(d, w, c8):
                return (2 * w + c8 // 4) if d == 0 else (2 * w + 1 - c8 // 4)

            def enter_seg(d, seg, first):
                rb = ((l * NSEG + seg) * 2 + d) * 128
                dma(SIC, s0c_d[rb:rb + 128, :], [], ["SIC"])
                tt(SICv, SICv, EMIRv[:, seg, d * 4:(d + 1) * 4].unsqueeze(2).broadcast_to([128, 4, 65]), ALU.mult, ["SIC", "EMIR"], ["SIC"])
                if first:
                    cp(CTv[:, d * 4:(d + 1) * 4, :], SICv, ["SIC"], ["CT"])
                else:
                    stt(CTv[:, d * 4:(d + 1) * 4, :], CTv[:, d * 4:(d + 1) * 4, :], CM[:, 0:1], SICv, ALU.mult, ALU.add, ["CT", "CM", "SIC"], ["CT"])
                cp(CBv[:, d * 4:(d + 1) * 4, :], CTv[:, d * 4:(d + 1) * 4, :], ["CT"], ["CB"])

            def prep(d, w):
                t0 = w * 512
                lo, hi = t0 - 1, t0 + 513
                clo, chi = max(lo, 0), min(hi, T)
                if clo != lo:
                    mset(RAWQv[:, :, 0:1], 0.0, ["RAWQ"])
                if chi != hi:
                    mset(RAWQv[:, :, 513:514], 0.0, ["RAWQ"])
                dma(RAWQv[:, :, clo - lo:clo - lo + chi - clo], zQKT[:, clo:chi].rearrange("(k p) t -> p k t", p=128), [id(zQKT)], ["RAWQ"])
                for k in range(4):
                    stt(Q1, RAWQv[:, k, 0:512], col(l, "qkc", 0 * 4 + k), TSM[:, 0:512], ALU.mult, ALU.mult, ["RAWQ", "COLT", "TSM"], ["Q1"])
                    stt(Q2, RAWQv[:, k, 2:514], col(l, "qkc", 2 * 4 + k), TSM[:, 512:1024], ALU.mult, ALU.mult, ["RAWQ", "COLT", "TSM"], ["Q2"])
                    stt(Q3, RAWQv[:, k, 1:513], col(l, "qkc", 1 * 4 + k), Q1, ALU.mult, ALU.add, ["RAWQ", "COLT", "Q1"], ["Q3"])
                    tt(Q3, Q3, Q2, ALU.add, ["Q3", "Q2"], ["Q3"])
                    qo = QK[d][k][:, ::-1] if d == 1 else QK[d][k]
                    act(qo, Q3, AF.Silu, ["Q3"], ["QK%d%d" % (d, k)])
                    if k >= 2:
                        cp(KEv[d][k - 2][0:64, 0, :], QK[d][k][0:64, :], ["QK%d%d" % (d, k)], ["KE%d%d" % (d, k - 2)], eng="pool")
                        cp(KEv[d][k - 2][64:128, 1, :], QK[d][k][64:128, :], ["QK%d%d" % (d, k)], ["KE%d%d" % (d, k - 2)], eng="pool")
                vsrc = zV[t0:t0 + 512, :]
                if d == 0:
                    for h in range(4):
                        dma(VAv[d][0:64, :, h, 0:64], vsrc[:, h * 64:(h + 1) * 64].rearrange("(c p) v -> p c v", p=64), [id(zV)], ["VA%d" % d])
                else:
                    dma(VN[0:64, :].rearrange("p (c n) -> p c n", c=8), vsrc.rearrange("(c p) n -> p c n", p=64), [id(zV)], ["VN"])
                    for i2 in range(4):
                        ps, pk = bank()
                        mm(ps[0:64, :], JB[0:64, :], VN[0:64, i2 * 512:(i2 + 1) * 512], True, True, ["JB", "VN"], [pk])
                        for cc in range(2):
                            act(VAv[d][0:64, i2 * 2 + cc, :, 0:64], ps[0:64, cc * 256:(cc + 1) * 256].rearrange("p (h v) -> p h v", v=64), AF.Identity, [pk], ["VA%d" % d])
                for hp in range(2):
                    ps, pk = bank(); psb = ps.bitcast(BF16)
                    for c in range(8):
                        tr(psb[0:64, c * 128:(c + 1) * 128], QK[d][2 + hp][:, c * 64:(c + 1) * 64], IDB, ["QK%d%d" % (d, 2 + hp), "IDB"], [pk])
                    pv = psb[0:64, :].rearrange("p (c n) -> p c n", n=128)
                    for half in range(2):
                        seg = seg_of(d, w, half * 4)
                        for hh in range(2):
                            u = d * 4 + hp * 2 + hh
                            tt(KHv[d][hp][0:64, half * 4:(half + 1) * 4, hh, hh * 64:(hh + 1) * 64], pv[:, half * 4:(half + 1) * 4, hh * 64:(hh + 1) * 64],
                               WETv[0:64, seg * 4:seg * 4 + 4, u:u + 1].broadcast_to([64, 4, 64]), ALU.mult, [pk, "WET"], ["KH%d%d" % (d, hp)])

            UNITS = [(d * 4 + h, d, h, h // 2, h % 2) for d in range(2) for h in range(4)]
            psl = lambda hh: slice(hh * 64, hh * 64 + 64)
            QKK = ["QK%d%d" % (d, k) for d in range(2) for k in range(4)] + ["KE%d%d" % (d, hp) for d in range(2) for hp in range(2)]

            def step(wf, wb, c8):
                tsl = slice(c8 * 64, (c8 + 1) * 64)
                pss, pks = bank()
                for (u, d, h, hp, hh) in UNITS:
                    cs = slice(u * 64, (u + 1) * 64)
                    mm(pss[0:64, cs], KEv[d][hp][:, hh, tsl], QK[d][hp][:, tsl], True, True, QKK, [pks])
                for d in range(2):
                    w = wf if d == 0 else wb
                    seg = seg_of(d, w, c8); n = seg * 4 + c8 % 4
                    tt(TMPS[0:64, d * 256:(d + 1) * 256].rearrange("p (h t) -> p h t", t=64), pss[0:64, d * 256:(d + 1) * 256].rearrange("p (h t) -> p h t", t=64),
                       EDTv[0:64, n, d * 4:(d + 1) * 4].unsqueeze(2).broadcast_to([64, 4, 64]), ALU.mult, [pks, "EDT"], ["TMPS"])
                tt(SR[0:64, :], TMPS[0:64, :], MI8[0:64, :], ALU.mult, ["TMPS", "CONST"], ["SR"])
                for d in range(2):
                    w = wf if d == 0 else wb
                    seg = seg_of(d, w, c8); n = seg * 4 + c8 % 4
                    psh, pkh = bank(); psc, pkc = bank()
                    pshv = psh[0:64, 0:260].rearrange("p (h n) -> p h n", n=65)
                    cv = c8 if d == 0 else 7 - c8
                    for h in range(4):
                        u, hp, hh = d * 4 + h, h // 2, h % 2
                        cs = slice(u * 64, (u + 1) * 64)
                        mm(psh[0:64, h * 65:(h + 1) * 65], SR[0:64, cs], VAv[d][0:64, cv, h, :], True, False, ["SR", "VA%d" % d], [pkh])
                        mm(psh[0:64, h * 65:(h + 1) * 65], QK[d][hp][:, tsl], CBv[:, u, :], False, True, QKK + ["CB"], [pkh])
                    for h in range(4):
                        u, hp, hh = d * 4 + h, h // 2, h % 2
                        mm(psc[:, h * 65:(h + 1) * 65], KHv[d][hp][0:64, c8, hh, :], VAv[d][0:64, cv, h, :], True, True, ["KH%d%d" % (d, hp), "VA%d" % d], [pkc])
                    dn = DN[0:64, d * 4:(d + 1) * 4]
                    act(dn, pshv[:, :, 64], AF.Abs, [pkh], ["DN%d" % d])
                    tt(dn, dn, CLTv[0:64, n, d * 4:(d + 1) * 4], ALU.max, ["DN%d" % d, "CLT"], ["DN%d" % d])
                    recip(dn, dn, ["DN%d" % d], ["DN%d" % d])
                    tt(HM[0:64, d * 256:(d + 1) * 256].rearrange("p (h v) -> p h v", v=64), pshv[:, :, 0:64],
                       dn.unsqueeze(2).broadcast_to([64, 4, 64]), ALU.mult, [pkh, "DN%d" % d], ["HM%d" % d])
                    tokb = (wf * 512 + c8 * 64) if d == 0 else (wb * 512 + 512 - (c8 + 1) * 64)
                    hdst = hm[d * T + tokb:d * T + tokb + 64, :]
                    if d == 0:
                        dma(hdst, HM[0:64, 0:256], ["HM0"], ["hm"])
                    else:
                        psr, pkr = bank()
                        mm(psr[0:64, 0:256], JF[0:64, 0:64], HM[0:64, 256:512], True, True, ["CONST", "HM1"], [pkr])
                        act(HMR[0:64, :], psr[0:64, 0:256], AF.Identity, [pkr], ["HMR"])
                        dma(hdst, HMR[0:64, :], ["HMR"], ["hm"])
                    tt(TMPCv, CTv[:, d * 4:(d + 1) * 4, :], ALTv[:, n, d * 4:(d + 1) * 4].unsqueeze(2).broadcast_to([128, 4, 65]), ALU.mult, ["CT", "ALT"], ["TMPC"])
                    tt(CTv[:, d * 4:(d + 1) * 4, :], TMPCv, psc[:, 0:260].rearrange("p (h n) -> p h n", n=65), ALU.add, ["TMPC", pkc], ["CT"])
                    if c8 % 4 == 3:
                        tt(TMPCv, CTv[:, d * 4:(d + 1) * 4, :], EMORv[:, seg, d * 4:(d + 1) * 4].unsqueeze(2).broadcast_to([128, 4, 65]), ALU.mult, ["CT", "EMOR"], ["TMPC"])
                        t4_ = TMPC.rearrange("p (q e n) -> p q e n", e=2, n=65)
                        tt(COv, t4_[:, :, 0, :], t4_[:, :, 1, :], ALU.add, ["TMPC"], ["CO"])
                        base = ((l * NSEG + seg) * 2 + d) * 256
                        dma(st_c[base:base + 256, :].rearrange("(q p) n -> p q n", p=128), COv, ["CO"], ["st_c"])
                        segn = seg + 1 if d == 0 else seg - 1
                        if 0 <= segn < NSEG:
                            enter_seg(d, segn, False)
                        else:
                            cp(CBv[:, d * 4:(d + 1) * 4, :], CTv[:, d * 4:(d + 1) * 4, :], ["CT"], ["CB"])
                    else:
                        cp(CBv[:, d * 4:(d + 1) * 4, :], CTv[:, d * 4:(d + 1) * 4, :], ["CT"], ["CB"])

            enter_seg(0, 0, True)
            enter_seg(1, NSEG - 1, True)
            for j in range(8):
                prep(0, j)
                prep(1, 7 - j)
                for c8 in range(8):
                    step(j, 7 - j, c8)
            P.barrier()
            H0, H1 = af(256), af(256); ZOB = ab(256); SGO = af(256); HSQ = af(256); SSM = af(4); HN = ab(256)
            CATD = ab(2 * 512); CATDv = CATD.rearrange("p (j t) -> p j t", j=2)
            for i in range(32):
                r0 = i * 128
                dma(H0, hm[r0:r0 + 128, :], ["hm"], ["H0"]); dma(H1, hm[T + r0:T + r0 + 128, :], ["hm"], ["H1"])
                dma(ZOB, zO[r0:r0 + 128, :], [id(zO)], ["ZOB"])
                tt(H0, H0, H1, ALU.add, ["H0", "H1"], ["H0"])
                act(SGO, ZOB, AF.Sigmoid, ["ZOB"], ["SGO"])
                tt(H0, H0, SGO, ALU.mult, ["H0", "SGO"], ["H0"])
                tt(HSQ, H0, H0, ALU.mult, ["H0"], ["HSQ"])
                P.op("dve", lambda e: e.tensor_reduce(SSM, HSQ.rearrange("p (h v) -> p h v", v=64), AX.X, ALU.add), ["HSQ"], ["SSM"])
                rsqrt(SSM, SSM, 1.0 / 64, EPS, ["SSM"], ["SSM"])
                tt(HN.rearrange("p (h v) -> p h v", v=64), H0.rearrange("p (h v) -> p h v", v=64),
                   SSM.unsqueeze(2).broadcast_to([128, 4, 64]), ALU.mult, ["H0", "SSM"], ["HN"])
                ps, pk = bank(); psb = ps.bitcast(BF16)
                for j in range(2):
                    tr(psb[:, j * 128:(j + 1) * 128], HN[:, j * 128:(j + 1) * 128], IDB, ["HN", "IDB"], [pk])
                for j in range(2):
                    ts(CATDv[:, j, (i % 4) * 128:(i % 4 + 1) * 128], psb[:, j * 128:(j + 1) * 128], col(l, "hng", j), None, ALU.mult, None, [pk, "COLT"], ["CATD"])
                if i % 4 == 3:
                    b = i // 4
                    dma(catT[768:1024, b * 512:(b + 1) * 512].rearrange("(j p) t -> p j t", p=128), CATDv, ["CATD"], [id(catT)])

        def make_norm(XN, XNB, SS, HTv):
            def norm_tile(xt, xk, A_, B_, tcol, hk):
                act(XN, xt, AF.Square, [xk], ["XN", "SS"], accum=SS[:, 0:1])
                rsqrt(SS[:, 1:2], SS[:, 0:1], 1.0 / D, EPS, ["SS"], ["SS1"])
                stt(XN, xt, SS[:, 1:2], A_, ALU.mult, ALU.mult, [xk, "SS1", id(A_)], ["XN"])
                tt(XNB, XN, B_, ALU.add, ["XN", id(B_)], ["XNB"])
                ps, pk = bank()
                psb = ps.bitcast(BF16)
                for k in range(8):
                    tr(psb[:, k * 128:(k + 1) * 128], XNB[:, k * 128:(k + 1) * 128], IDB, ["XNB", "IDB"], [pk])
                act(HTv[:, :, tcol * 128:(tcol + 1) * 128], psb.rearrange("p (k t) -> p k t", k=8), AF.Identity, [pk], [hk])
            return norm_tile

        bc = lambda src: src.partition_broadcast(128)
        for l in range(nlayers):
            aoff["o"] = persist_end
            WM = [af(8 * 512), af(8 * 512)]
            MR = af(512); BM = af(512)
            for cb in range(12):
                wt_ = WM[cb % 2]; wk = "WM%d" % (cb % 2)
                dma(wt_.rearrange("p (k n) -> p k n", k=8),
                    w_mod[l * D:(l + 1) * D, cb * 512:(cb + 1) * 512].rearrange("(k p) n -> p k n", p=128), [], [wk])
                dma(BM[0:1, :], b_mod[l:l + 1, cb * 512:(cb + 1) * 512], [], ["BM"])
                ps, pk = bank()
                for k in range(8):
                    mm(ps[0:1, :], SC[:, k:k + 1], wt_[:, k * 512:(k + 1) * 512], k == 0, k == 7, [wk, "SC"], [pk])
                tt(MR[0:1, :], ps[0:1, :], BM[0:1, :], ALU.add, [pk, "BM"], ["MR"])
                dma(modrow[l:l + 1, cb * 512:(cb + 1) * 512], MR[0:1, :], ["MR"], ["modrow"])
            P.barrier()
            aoff["o"] = persist_end
            A2, B2, G1, G2, GB2 = [af(D) for _ in range(5)]
            lay_end = aoff["o"]
            A1, B1, TMPB = af(D), af(D), af(D)
            for (A_, B_, G_, j0, rp) in ((A1, B1, G1, 0, 0), (A2, B2, G2, 3, 1)):
                dma(B_, bc(modrow[l:l + 1, (j0 + 0) * D:(j0 + 1) * D]), ["modrow"], [id(B_)])
                dma(A_, bc(modrow[l:l + 1, (j0 + 1) * D:(j0 + 2) * D]), ["modrow"], [id(A_)])
                dma(G_, bc(modrow[l:l + 1, (j0 + 2) * D:(j0 + 3) * D]), ["modrow"], [id(G_)])
                dma(TMPB, bc(rowp[l:l + 1, rp * D:(rp + 1) * D]), [], ["TMPB"])
                stt(A_, A_, 1.0, TMPB, ALU.add, ALU.mult, [id(A_), "TMPB"], [id(A_)])
            dma(TMPB, bc(rowp[l:l + 1, 2 * D:3 * D]), [], ["TMPB"])
            tt(GB2, G2, TMPB, ALU.mult, [id(G2), "TMPB"], ["GB2"])

            XT = [af(D), af(D)]; XN = af(D); XNB = ab(D); HT = ab(8 * 512); HTv = HT.rearrange("p (k t) -> p k t", k=8)
            SS = af(4)
            norm_tile = make_norm(XN, XNB, SS, HTv)
            WIN = ab(8 * DIN)
            WINv = WIN.rearrange("p (k n) -> p k n", k=8)
            for k in range(8):
                dma(WINv[:, k, :], w_in[l * D + k * 128: l * D + (k + 1) * 128, :], [], ["WIN"], cast=True)
            ZST = [ab(512) for _ in range(4)]; ZSF = af(512)
            xsrc = x_in if l == 0 else xs
            for b in range(NB):
                for t4 in range(4):
                    xt = XT[t4 % 2]; xk = "XT%d" % (t4 % 2)
                    r0 = b * 512 + t4 * 128
                    dma(xt, xsrc[r0:r0 + 128, :], ["xs"], [xk])
                    norm_tile(xt, xk, A1, B1, t4, "HT")
                zi = 0
                for (c0, ntile, dst) in ((C_A, 8, zAT), (C_C, 4, zCT), (C_QK, 4, zQKT)):
                    for ct in range(ntile):
                        ps, pk = bank()
                        for k in range(8):
                            mm(ps, WINv[:, k, c0 + ct * 128:c0 + (ct + 1) * 128], HTv[:, k, :], k == 0, k == 7, ["WIN", "HT"], [pk])
                        zs = ZST[zi % 4]; zk = "ZST%d" % (zi % 4); zi += 1
                        act(zs, ps, AF.Identity, [pk], [zk])
                        dma(dst[ct * 128:(ct + 1) * 128, b * 512:(b + 1) * 512], zs, [zk], [id(dst)])
                ps, pk = bank()
                for k in range(8):
                    mm(ps[0:16, :], WINv[:, k, C_I:C_I + 16], HTv[:, k, :], k == 0, k == 7, ["WIN", "HT"], [pk])
                cp(ZSF[0:16, :], ps[0:16, :], [pk], ["ZSF"])
                dma(zG[:, b * 512:(b + 1) * 512], ZSF[0:16, :], ["ZSF"], ["zG"])
                for t4 in range(4):
                    r0 = b * 512 + t4 * 128
                    for (c0, dst) in ((C_B, zB), (C_V, zV), (C_O, zO)):
                        ps, pk = bank()
                        for k in range(8):
                            mm(ps[:, 0:256], HTv[:, k, t4 * 128:(t4 + 1) * 128], WINv[:, k, c0:c0 + 256], k == 0, k == 7, ["WIN", "HT"], [pk])
                        zs = ZST[zi % 4]; zk = "ZST%d" % (zi % 4); zi += 1
                        act(zs[:, 0:256], ps[:, 0:256], AF.Identity, [pk], [zk])
                        dma(dst[r0:r0 + 128, :], zs[:, 0:256], [zk], [id(dst)])
            P.barrier()

            for nm_, fn_ in (("pool", pool_phase), ("conv", conv_phase), ("rwkv", rwkv_phase), ("mlstm", mlstm_phase)):
                if nm_ in MIX:
                    aoff["o"] = lay_end
                    fn_(l)
                    P.barrier()

            aoff["o"] = lay_end
            XT = [af(D), af(D)]; XN = af(D); XNB = ab(D); HT = ab(8 * 512); HTv = HT.rearrange("p (k t) -> p k t", k=8)
            SS = af(4)
            norm_tile = make_norm(XN, XNB, SS, HTv)
            WOUT = ab(8 * D); WOUTv = WOUT.rearrange("p (k n) -> p k n", k=8)
            for k in range(8):
                dma(WOUTv[:, k, :], w_out[l * D + k * 128: l * D + (k + 1) * 128, :], [], ["WOUT"], cast=True)
            W1P = [ab(8 * 512), ab(8 * 512)]; W2P = [ab(32 * 128), ab(32 * 128)]
            CATT = ab(8 * 512); CATTv = CATT.rearrange("p (k t) -> p k t", k=8)
            X1 = af(4 * D); X1v = X1.rearrange("p (t n) -> p t n", t=4)
            HID = ab(32 * 512); HIDv = HID.rearrange("p (f t) -> p f t", f=32)
            RL = [af(512), af(512)]; TM3 = [af(512), af(512)]
            FGT = None
            if l == nlayers - 1:
                FGT = af(D)
                dma(FGT, bc(rowp[l:l + 1, 3 * D:4 * D]), [], ["FGT"])
            for b in range(NB):
                dma(CATTv, catT[:, b * 512:(b + 1) * 512].rearrange("(k p) t -> p k t", p=128), [id(catT)], ["CATT"])
                for t4 in range(4):
                    xt = XT[t4 % 2]; xk = "XT%d" % (t4 % 2)
                    r0 = b * 512 + t4 * 128
                    dma(xt, xsrc[r0:r0 + 128, :], ["xs"], [xk])
                    for half in range(2):
                        ps, pk = bank()
                        for k in range(8):
                            mm(ps, CATTv[:, k, t4 * 128:(t4 + 1) * 128], WOUTv[:, k, half * 512:(half + 1) * 512], k == 0, k == 7, ["CATT", "WOUT"], [pk])
                        tm = TM3[half]; tk = "TM3%d" % half
                        tt(tm, ps, G1[:, half * 512:(half + 1) * 512], ALU.mult, [pk, id(G1)], [tk])
                        tt(X1v[:, t4, half * 512:(half + 1) * 512], tm, xt[:, half * 512:(half + 1) * 512], ALU.add, [tk, xk], ["X1_%d" % t4])
                    norm_tile(X1v[:, t4, :], "X1_%d" % t4, A2, B2, t4, "HT")
                    tt(X1v[:, t4, :], X1v[:, t4, :], GB2, ALU.add, ["X1_%d" % t4, "GB2"], ["X1_%d" % t4])
                for p1 in range(8):
                    wp = W1P[p1 % 2]; wk = "W1P%d" % (p1 % 2); wpv = wp.rearrange("p (k n) -> p k n", k=8)
                    dma(wpv, w1[l * D:(l + 1) * D, p1 * 512:(p1 + 1) * 512].rearrange("(k p) n -> p k n", p=128), [], [wk], cast=True)
                    for ft in range(4):
                        f = p1 * 4 + ft
                        ps, pk = bank()
                        for k in range(8):
                            mm(ps, wpv[:, k, ft * 128:(ft + 1) * 128], HTv[:, k, :], k == 0, k == 7, [wk, "HT"], [pk])
                        rl = RL[f % 2]; rk = "RL%d" % (f % 2)
                        act(rl, ps, AF.Relu, [pk, "COLT"], [rk], bias=col(l, "b1", f))
                        tt(HIDv[:, f, :], rl, rl, ALU.mult, [rk], ["HID"], eng="pool" if f % 2 else "dve")
                for p2 in range(8):
                    wp = W2P[p2 % 2]; wk = "W2P%d" % (p2 % 2); wpv = wp.rearrange("p (f n) -> p f n", f=32)
                    dma(wpv, w2[l * DFF:(l + 1) * DFF, p2 * 128:(p2 + 1) * 128].rearrange("(f p) n -> p f n", p=128), [], [wk], cast=True)
                    for t4 in range(4):
                        ps, pk = bank()
                        for f in range(32):
                            mm(ps[:, 0:128], HIDv[:, f, t4 * 128:(t4 + 1) * 128], wpv[:, f, :], f == 0, f == 31, ["HID", wk], [pk])
                        tm = TM3[t4 % 2]; tk = "TM3%d" % (t4 % 2)
                        tt(tm[:, 0:128], ps[:, 0:128], G2[:, p2 * 128:(p2 + 1) * 128], ALU.mult, [pk, id(G2)], [tk])
                        tt(X1v[:, t4, p2 * 128:(p2 + 1) * 128], tm[:, 0:128], X1v[:, t4, p2 * 128:(p2 + 1) * 128], ALU.add, [tk, "X1_%d" % t4], ["X1_%d" % t4])
                for t4 in range(4):
                    r0 = b * 512 + t4 * 128
                    xk = "X1_%d" % t4
                    if l < nlayers - 1:
                        dma(xs[r0:r0 + 128, :], X1v[:, t4, :], [xk], ["xs"])
                    else:
                        act(XN, X1v[:, t4, :], AF.Square, [xk], ["XN", "SS"], accum=SS[:, 0:1])
                        rsqrt(SS[:, 1:2], SS[:, 0:1], 1.0 / D, EPS, ["SS"], ["SS1"])
                        stt(XN, X1v[:, t4, :], SS[:, 1:2], FGT, ALU.mult, ALU.mult, [xk, "SS1", "FGT"], ["XN"])
                        dma(y_out[r0:r0 + 128, :], XN, ["XN"], ["yout"])
            P.barrier()

        print('arena peak', aoff.get('max'), 'of', ARENA)
        P.barrier(["sp"])

        @block.tensor
        def _(e):
            P.emit("pe", e)

        @block.scalar
        def _(e):
            P.emit("act", e)

        @block.vector
        def _(e):
            P.emit("dve", e)

        @block.gpsimd
        def _(e):
            P.emit("pool", e)

        @block.sync
        def _(e):
            P.emit("sp", e)
    return nc


MIX = ("pool", "conv", "rwkv", "mlstm")
_NC_CACHE = {}


def kernel(**inp):
    inp = {k: np.asarray(v) for k, v in inp.items()}
    shared = _shared_consts(inp)
    xp = inp["x_prompt"].astype(np.float32); xsm = inp["x_sample"].astype(np.float32)
    cores = []
    for i in range(2):
        cores.append(_prep_core("p", xp[i * 16:(i + 1) * 16], inp["c_ctx"], inp))
    for b in range(2):
        cores.append(_prep_core("s", xsm[b], inp["c"][b], inp, inp["state_rwkv"][b], inp["state_mlstm_C"][b],
                                inp["state_mlstm_n"][b], inp["state_mlstm_m"][b]))
    in_maps = []
    for i in range(8):
        m = dict(shared); m.update(cores[i % 4]); in_maps.append(m)
    if "nc" not in _NC_CACHE:
        _NC_CACHE["nc"] = build()
    res = run_bass_kernel_spmd(_NC_CACHE["nc"], in_maps, core_ids=list(range(8)))
    R = res.results
    y_prompt = np.concatenate([R[0]["y"], R[1]["y"]], 0).reshape(32, 256, D).astype(np.float32)
    y_sample = np.stack([R[2]["y"], R[3]["y"]], 0).reshape(2, T, D).astype(np.float32)
    nr = np.zeros((32, L, 2, 4, 64, 64), np.float32); ncc = np.zeros((32, L, 2, 4, 64, 64), np.float32)
    nn = np.zeros((32, L, 2, 4, 64), np.float32); nm = np.zeros((32, L, 2, 4), np.float32)
    for i in range(2):
        sr = R[i]["st_r"].reshape(L, NSEG, 2, 4, 64, 64)
        scc = R[i]["st_c"].reshape(L, NSEG, 2, 4, 64, 65)
        smm = R[i]["st_m"].reshape(L, 2, 4, NSEG)
        nr[i * 16:(i + 1) * 16] = sr.transpose(1, 0, 2, 3, 5, 4)
        ncc[i * 16:(i + 1) * 16] = scc[..., :64].transpose(1, 0, 2, 3, 5, 4)
        nn[i * 16:(i + 1) * 16] = scc[..., 64].transpose(1, 0, 2, 3, 4)
        nm[i * 16:(i + 1) * 16] = smm.transpose(3, 0, 1, 2)
    return (y_prompt, y_sample, nr, ncc, nn, nm)
```

```python
import contextlib
import math
import numpy as np
import ml_dtypes
import concourse.bass as bass
import concourse.mybir as mybir
from concourse.bass_utils import run_bass_kernel_spmd

F32, BF16 = mybir.dt.float32, mybir.dt.bfloat16
AF = mybir.ActivationFunctionType
ALU = mybir.AluOpType
AX = mybir.AxisListType

D = 1024
L = 4
T = 4096
NSEG = 16
DIN = 2832
DFF = 4096
NB = 8
ARENA = 52000
DEBUG = False
RW_STAGE = 3
RW_CUT = 9
EPS = 1e-6
RWKV_LN_EPS = 64e-5
CONV_LN_EPS = 1e-5
POOL_WINS = (2, 4, 8, 16)
POOL_DELTAS = {0: (-1, 0, 1), 1: (-1, 0, 1), 2: (-2, -1, 0, 1, 2), 3: (-4, -3, -2, -1, 0, 1, 2, 3, 4)}

C_A, C_B, C_C, C_QK, C_V, C_I, C_F, C_O = 0, 1024, 1280, 1792, 2304, 2560, 2568, 2576

COLS = {}
_off = 0
for _n, _w in [("mu", 8), ("w0", 4), ("a0", 4), ("kk", 2), ("ka", 2), ("rk", 2), ("lng", 2), ("lnb", 2),
               ("pscale", 2), ("convb", 2), ("clng", 2), ("clnb", 2), ("cdw", 62), ("qkc", 12), ("hng", 2),
               ("b1", 32)]:
    COLS[_n] = _off
    _off += _w
NCOL = _off


class Prog:
    def __init__(self, esem, dsems):
        self.q = {e: [] for e in ("pe", "act", "dve", "pool", "sp")}
        self.cnt = dict.fromkeys(self.q, 0)
        self.esem, self.dsems = esem, dsems
        self.duse = {e: [0] * len(dsems[e]) for e in dsems}
        self.dnext = {e: 0 for e in dsems}
        self.lastw, self.readers = {}, {}
        self.seen = {e: {} for e in self.q}
        self.alltok = {}

    def sem(self, sid):
        return self.esem[sid[1]] if sid[0] == "e" else self.dsems[sid[1]][sid[2]]

    def op(self, eng, fn, reads=(), writes=(), dma=False):
        need = {}

        def add(tok):
            if tok is not None and need.get(tok[0], 0) < tok[1]:
                need[tok[0]] = tok[1]

        for k in reads:
            add(self.lastw.get(k))
        for k in writes:
            add(self.lastw.get(k))
            for sid, val in self.readers.get(k, {}).items():
                add((sid, val))
        if dma:
            i = self.dnext[eng]
            self.dnext[eng] = (i + 1) % len(self.dsems[eng])
            sid = ("d", eng, i)
            if self.duse[eng][i] > 0:
                need[sid] = max(need.get(sid, 0), 16 * self.duse[eng][i])
            self.duse[eng][i] += 1
            tok = (sid, 16 * self.duse[eng][i])
        else:
            self.cnt[eng] += 1
            tok = (("e", eng), self.cnt[eng])
        waits = []
        for sid, val in need.items():
            if sid == ("e", "pe") and eng == "pe":
                continue
            if self.seen[eng].get(sid, 0) >= val:
                continue
            self.seen[eng][sid] = val
            waits.append((sid, val))
        self.q[eng].append((waits, fn, tok))
        for k in reads:
            r = self.readers.setdefault(k, {})
            if r.get(tok[0], 0) < tok[1]:
                r[tok[0]] = tok[1]
        for k in writes:
            self.lastw[k] = tok
            self.readers[k] = {}
        self.alltok[tok[0]] = tok[1]

    def barrier(self, engines=None):
        for eng in (engines or self.q):
            waits = []
            for sid, val in self.alltok.items():
                if self.seen[eng].get(sid, 0) >= val:
                    continue
                self.seen[eng][sid] = val
                waits.append((sid, val))
            self.q[eng].append((waits, None, None))

    def emit(self, eng, e):
        for waits, fn, tok in self.q[eng]:
            for sid, val in waits:
                e.wait_ge(self.sem(sid), val)
            if fn is not None:
                ins = fn(e)
                ins.then_inc(self.sem(tok[0]), 16 if tok[0][0] == "d" else 1)


def _colify(v):
    v = np.asarray(v, np.float32).reshape(-1, 128)
    return np.ascontiguousarray(v.T)


def _window_bounds(n, win):
    t = np.arange(n)
    return np.clip(t - win // 2, 0, n), np.clip(t + win // 2, 0, n)


def _pool_consts(grid):
    mats = np.zeros((4, 9, 2, 128, 128), np.float32)
    inv = np.zeros((128, 32, 4), np.float32)
    for g, win in enumerate(POOL_WINS):
        if grid:
            rlo, rhi = _window_bounds(64, win)
            clo, chi = _window_bounds(64, win)
            Mfull = None
            R = np.zeros((64, 64), np.float32)
            for r in range(64):
                R[r, rlo[r]:rhi[r]] = 1
            Cm = np.zeros((64, 64), np.float32)
            for c in range(64):
                Cm[c, clo[c]:chi[c]] = 1
            Mfull = np.kron(R, Cm)
            cnt = Mfull.sum(1)
        else:
            lo, hi = _window_bounds(256, win)
            M1 = np.zeros((256, 256), np.float32)
            for t in range(256):
                M1[t, lo[t]:hi[t]] = 1
            Mfull = np.kron(np.eye(16, dtype=np.float32), M1)
            cnt = Mfull.sum(1)
        inv[:, :, g] = (1.0 / cnt).reshape(32, 128).T
        for di, dl in enumerate(POOL_DELTAS[g]):
            for par in range(2):
                acc = None
                for i in range(par, 32, 2):
                    j = i + dl
                    if j < 0 or j >= 32:
                        continue
                    blk = Mfull[i * 128:(i + 1) * 128, j * 128:(j + 1) * 128]
                    if acc is None:
                        acc = blk
                    else:
                        assert np.array_equal(acc, blk), (g, dl, par, i)
                if acc is not None:
                    mats[g, di, par] = acc.T
    return mats, inv


def _prep_core(kind, xs, cvec, inp, sr=None, sc=None, sn=None, sm=None):
    m = {}
    m["x"] = np.ascontiguousarray(xs.reshape(T, D), np.float32)
    m["cvcol"] = _colify(cvec)
    cmv = 0.0 if kind == "p" else 1.0
    m["cm"] = np.full((128, 1), cmv, np.float32)
    mp = np.ones((128, 512), np.float32)
    mn = np.ones((128, 512), np.float32)
    if kind == "p":
        mp[:, 0] = 0; mp[:, 256] = 0
        mn[:, 255] = 0; mn[:, 511] = 0
    m["tsm"] = np.stack([mp, mn], 1).reshape(128, 1024)
    mats, inv = _pool_consts(kind == "s")
    m["poolm"] = np.ascontiguousarray(mats.transpose(3, 0, 1, 2, 4).reshape(128, 4 * 9 * 2 * 128))
    m["poolinv"] = np.ascontiguousarray(inv.reshape(128, 128))
    s0r = np.zeros((L, NSEG, 2, 128, 4, 64), np.float32)
    s0c = np.zeros((L, NSEG, 2, 128, 4, 65), np.float32)
    s0m = np.zeros((L, 2, 4, NSEG), np.float32)
    if kind == "s":
        for l in range(L):
            for d in range(2):
                seg = 0 if d == 0 else NSEG - 1
                for h in range(4):
                    hh = h % 2
                    s0r[l, seg, d, hh * 64:(hh + 1) * 64, h, :] = sr[l, d, h].T
                    s0c[l, seg, d, hh * 64:(hh + 1) * 64, h, :64] = sc[l, d, h].T
                    s0c[l, seg, d, hh * 64:(hh + 1) * 64, h, 64] = sn[l, d, h]
                    s0m[l, d, h, seg] = sm[l, d, h]
    m["s0r"] = s0r.reshape(L * NSEG * 2 * 128, 256)
    m["s0c"] = s0c.reshape(L * NSEG * 2 * 128, 260)
    m["s0m"] = s0m.reshape(L * 8, NSEG)
    return m


def _shared_consts(inp):
    m = {}
    f = lambda a: np.ascontiguousarray(a, np.float32)
    for k in ("w_mod", "w_in", "w_out", "mlp_w1", "mlp_w2", "conv_pw", "rwkv_g_up"):
        m[k] = f(inp[k]).reshape(-1, inp[k].shape[-1])
    m["b_mod"] = f(inp["b_mod"])
    m["rowp"] = f(np.concatenate([inp["norm1_g"], inp["norm2_g"], inp["mlp_b2"],
                                  np.broadcast_to(inp["final_g"], (L, D))], 1))
    cols = np.zeros((128, L, NCOL), np.float32)
    for l in range(L):
        def put(name, v):
            c = _colify(v)
            cols[:, l, COLS[name]:COLS[name] + c.shape[1]] = c
        put("mu", inp["rwkv_mu"][l])
        put("w0", inp["rwkv_w0"][l].reshape(-1))
        put("a0", inp["rwkv_a0"][l].reshape(-1))
        put("kk", inp["rwkv_k_k"][l]); put("ka", inp["rwkv_k_a"][l]); put("rk", inp["rwkv_r_k"][l].reshape(-1))
        put("lng", inp["rwkv_ln_g"][l]); put("lnb", inp["rwkv_ln_b"][l])
        put("pscale", inp["pool_scale"][l]); put("convb", inp["conv_b"][l])
        put("clng", inp["conv_ln_g"][l]); put("clnb", inp["conv_ln_b"][l])
        put("cdw", inp["conv_dw"][l].reshape(-1))
        put("qkc", inp["mlstm_qk_conv"][l].reshape(-1))
        put("hng", inp["mlstm_hn_g"][l]); put("b1", inp["mlp_b1"][l])
    m["cols"] = cols.reshape(128, L * NCOL)
    gb = np.zeros((128, L, 2), np.float32)
    for l in range(L):
        for d in range(2):
            for h in range(4):
                gb[d * 64 + h * 16:(d * 64 + h * 16 + 16), l, 0] = inp["mlstm_i_bias"][l, d * 4 + h]
                gb[d * 64 + h * 16:(d * 64 + h * 16 + 16), l, 1] = inp["mlstm_f_bias"][l, d * 4 + h]
    m["gbias"] = gb.reshape(128, L * 2)
    wa = np.concatenate([inp["rwkv_w_up"], inp["rwkv_a_up"]], 2)
    m["waup"] = f(wa).reshape(L * 2 * 128, 256)
    pw = np.zeros((L, 2, 128, 128), np.float32)
    for l in range(L):
        for g in range(4):
            p, o = g // 2, (g % 2) * 64
            pw[l, p, o:o + 64, o:o + 64] = inp["pool_w"][l, g]
    m["poolw"] = pw.reshape(L * 2 * 128, 128)
    c = {}
    c["ident"] = np.eye(128, dtype=np.float32)
    bo = np.zeros((128, 128), np.float32); bo[:64, :64] = 1; bo[64:, 64:] = 1
    c["bones"] = bo
    c["ones"] = np.ones((128, 128), np.float32)
    s = np.arange(64)
    ms = (s[:, None] < s[None, :]).astype(np.float32)
    mi = (s[:, None] <= s[None, :]).astype(np.float32)
    ml = (s[:, None] > s[None, :]).astype(np.float32)
    pad = lambda a: np.concatenate([np.tile(a, (1, 8)), np.zeros((64, 512), np.float32)], 0)
    blk = (s[:, None] // 16 == s[None, :] // 16).astype(np.float32)
    c["ms"] = pad(ms); c["mi"] = pad(mi); c["ml"] = pad(ml * blk)
    c["msd"] = pad(ms * blk); c["mll"] = pad(ml * (1 - blk))
    c["i8"] = pad(np.eye(64, dtype=np.float32))
    rm = np.ones((128, 512), np.float32); rm[:, ::64] = 0
    c["rmask"] = rm
    rn = np.zeros((128, 512), np.float32); rn[:, ::64] = -1e30
    c["rneg"] = rn
    pk = np.zeros((128, 16, 8), np.float32)
    for d in range(2):
        for h in range(4):
            for sg in range(16):
                pk[d * 64 + h * 16 + sg, sg, d * 4 + h] = 1
    c["pick"] = pk.reshape(128, 128)
    c["mi8"] = c["mi"] * 0.125
    p2 = np.zeros((128, 64), np.float32)
    for d in range(2):
        for h in range(4):
            p2[d * 32 + h, d * 4 + h] = 1
    c["pk2"] = p2
    jr = np.zeros((128, 64), np.float32)
    jr[np.arange(64), 63 - np.arange(64)] = 1
    c["jrev"] = jr
    order = ["ident", "bones", "ones", "ms", "mi", "ml", "i8", "rmask", "rneg", "pick", "mi8", "pk2", "jrev", "msd", "mll"]
    m["consts"] = np.concatenate([c[k] for k in order], 1)
    return m


CONST_OFF = {}
_o = 0
for _k, _w in [("ident", 128), ("bones", 128), ("ones", 128), ("ms", 512), ("mi", 512), ("ml", 512), ("i8", 512), ("rmask", 512),
               ("rneg", 512), ("pick", 128), ("mi8", 512), ("pk2", 64), ("jrev", 64), ("msd", 512), ("mll", 512)]:
    CONST_OFF[_k] = (_o, _w)
    _o += _w
NCONST = _o


def build(nlayers=L):
    nc = bass.Bass("TRN2", target_bir_lowering=False)
    din = lambda n, s, dt=F32: nc.dram_tensor(n, list(s), dt, kind="ExternalInput").ap()
    dout = lambda n, s, dt=F32: nc.dram_tensor(n, list(s), dt, kind="ExternalOutput").ap()
    dscr = lambda n, s, dt=F32: nc.dram_tensor(n, list(s), dt, kind="ExternalOutput" if DEBUG else "Internal").ap()
    x_in = din("x", [T, D]); cvcol = din("cvcol", [128, 8]); cm_d = din("cm", [128, 1])
    tsm_d = din("tsm", [128, 1024]); poolm_d = din("poolm", [128, 4 * 9 * 2 * 128]); poolinv_d = din("poolinv", [128, 128])
    s0r_d = din("s0r", [L * NSEG * 2 * 128, 256]); s0c_d = din("s0c", [L * NSEG * 2 * 128, 260]); s0m_d = din("s0m", [L * 8, NSEG])
    w_mod = din("w_mod", [L * D, 6 * D]); b_mod = din("b_mod", [L, 6 * D])
    w_in = din("w_in", [L * D, DIN]); w_out = din("w_out", [L * D, D])
    w1 = din("mlp_w1", [L * D, DFF]); w2 = din("mlp_w2", [L * DFF, D])
    conv_pw = din("conv_pw", [L * 256, 256]); g_up = din("rwkv_g_up", [L * 128, 256])
    rowp = din("rowp", [L, 4 * D]); cols_d = din("cols", [128, L * NCOL]); gbias_d = din("gbias", [128, L * 2])
    waup_d = din("waup", [L * 2 * 128, 256]); poolw_d = din("poolw", [L * 2 * 128, 128]); consts_d = din("consts", [128, NCONST])
    y_out = dout("y", [T, D]); st_r = dout("st_r", [L * NSEG * 2 * 2 * 128, 64])
    st_c = dout("st_c", [L * NSEG * 2 * 2 * 128, 65]); st_m = dout("st_m", [L * 8, NSEG])
    modrow = dscr("modrow", [L, 6 * D]); xs = dscr("xs", [T, D])
    zAT = dscr("zAT", [1024, T], BF16); zB = dscr("zB", [T, 256], BF16); zCT = dscr("zCT", [512, T], BF16)
    zQKT = dscr("zQKT", [512, T], BF16); zV = dscr("zV", [T, 256], BF16); zO = dscr("zO", [T, 256], BF16)
    zG = dscr("zG", [16, T]); yT = dscr("yT", [2 * 256, T]); bonT = dscr("bonT", [2 * 256, T])
    W1B = dscr("W1B", [8 * 128, 4096], BF16); W2B = dscr("W2B", [4 * 128, 8192], BF16)
    catT = dscr("catT", [1024, T], BF16); hm = dscr("hm", [2 * T, 256]); gtab = dscr("gtab", [4 * 128, 256])

    es = contextlib.ExitStack()
    with es:
        arena = es.enter_context(nc.sbuf_tensor("arena", [128, ARENA], F32))
        psall = es.enter_context(nc.psum_tensor("psall", [128, 4096], F32))
        esem = {e: es.enter_context(nc.semaphore("e_" + e)) for e in ("pe", "act", "dve", "pool", "sp")}
        dsems = {e: [es.enter_context(nc.semaphore("d_%s%d" % (e, i))) for i in range(12)] for e in ("sp", "pool")}
        block = es.enter_context(nc.Block())
        P = Prog(esem, dsems)
        PS = [psall[:, i * 512:(i + 1) * 512] for i in range(8)]
        PSK = ["ps%d" % i for i in range(8)]
        pstate = {"i": 0}

        def bank():
            i = pstate["i"]; pstate["i"] = (i + 1) % 8
            return PS[i], PSK[i]

        aoff = {"o": 0}

        def af(n):
            o = aoff["o"]; aoff["o"] = o + n
            aoff["max"] = max(aoff.get("max", 0), aoff["o"])
            assert aoff["o"] <= ARENA, aoff["o"]
            return arena[:, o:o + n]

        def ab(n):
            n32 = (n + 1) // 2
            return af(n32).bitcast(BF16)[:, 0:n]

        def mm(out, lhsT, rhs, st, sp_, r, w):
            P.op("pe", lambda e: e.matmul(out, lhsT, rhs, start=st, stop=sp_), r, w)

        def tr(out, in_, ident, r, w):
            P.op("pe", lambda e: e.transpose(out, in_, ident), r, w)

        def act(out, in_, func, r, w, bias=None, scale=None, accum=None):
            kw = {}
            if bias is not None: kw["bias"] = bias
            if scale is not None: kw["scale"] = scale
            if accum is not None: kw["accum_out"] = accum
            P.op("act", lambda e: e.activation(out=out, in_=in_, func=func, **kw), r, w)

        def tt(out, a, b, op, r, w, eng="dve"):
            P.op(eng, lambda e: e.tensor_tensor(out=out, in0=a, in1=b, op=op), r, w)

        def ts(out, a, s1, s2, op0, op1, r, w, eng="dve"):
            if op1 is None:
                P.op(eng, lambda e: e.tensor_scalar(out, a, s1, None, op0), r, w)
            else:
                P.op(eng, lambda e: e.tensor_scalar(out, a, s1, s2, op0, op1), r, w)

        def stt(out, a, s, b, op0, op1, r, w):
            P.op("dve", lambda e: e.scalar_tensor_tensor(out=out, in0=a, scalar=s, in1=b, op0=op0, op1=op1), r, w)

        def cp(out, in_, r, w, eng="dve"):
            P.op(eng, lambda e: e.tensor_copy(out, in_), r, w)

        def recip(out, in_, r, w):
            P.op("dve", lambda e: e.reciprocal(out, in_), r, w)

        def scan(out, d0, d1, init, op0, op1, r, w):
            P.op("dve", lambda e: e.tensor_tensor_scan(out=out, data0=d0, data1=d1, initial=init, op0=op0, op1=op1), r, w)

        def mset(ap, v, w, eng="dve"):
            P.op(eng, lambda e: e.memset(ap, v), (), w)

        def dma(out, in_, r, w, cast=False, q=None):
            eng = q or ("pool" if cast else "sp")
            P.op(eng, lambda e: e.dma_start(out=out, in_=in_), r, w, dma=True)

        def rsqrt(out, in_, mul, eps, r, w):
            ts(out, in_, mul, eps, ALU.mult, ALU.add, r, w)
            act(out, out, AF.Sqrt, w, w)
            recip(out, out, w, w)

        CONST = af(NCONST); COLT = af(L * NCOL); CM = af(1); GB = af(L * 2); SC = af(8)
        IDB = ab(128); TSM = af(1024)
        ck = lambda n: CONST[:, CONST_OFF[n][0]:CONST_OFF[n][0] + CONST_OFF[n][1]]
        IDF, BONES, ONES = ck("ident"), ck("bones"), ck("ones")
        MS, MI, ML, I8, RMASK, RNEG, PICK = ck("ms"), ck("mi"), ck("ml"), ck("i8"), ck("rmask"), ck("rneg"), ck("pick")
        MI8, PK2, JF = ck("mi8"), ck("pk2"), ck("jrev")
        MSD, MLL, MLD = ck("msd"), ck("mll"), ML
        JB = ab(64)
        PICKv = PICK.rearrange("p (s j) -> p s j", j=8)
        ONEC = af(1); LN8C = af(1)
        dma(CONST, consts_d[:, :], [], ["CONST"]); dma(COLT, cols_d[:, :], [], ["COLT"]); dma(CM, cm_d[:, :], [], ["CM"])
        dma(GB, gbias_d[:, :], [], ["GB"]); dma(SC, cvcol[:, :], [], ["SC"]); dma(TSM, tsm_d[:, :], [], ["TSM"])
        cp(IDB, IDF, ["CONST"], ["IDB"])
        cp(JB[0:64, :], JF[0:64, 0:64], ["CONST"], ["JB"])
        mset(ONEC, 1.0, ["ONEC"]); mset(LN8C, -math.log(8.0), ["LN8C"])
        act(SC, SC, AF.Silu, ["SC"], ["SC"])
        col = lambda l, name, j=0: COLT[:, l * NCOL + COLS[name] + j: l * NCOL + COLS[name] + j + 1]
        persist_end = aoff["o"]

        def pool_phase(l):
            POOLM = ab(4 * 9 * 2 * 128); PINV = af(128)
            dma(POOLM, poolm_d[:, :], [], ["POOLM"], cast=True)
            dma(PINV, poolinv_d[:, :], [], ["PINV"])
            PWB = ab(2 * 128); PWBv = PWB.rearrange("p (j n) -> p j n", j=2)
            for j in range(2):
                dma(PWBv[:, j, :], poolw_d[(l * 2 + j) * 128:(l * 2 + j + 1) * 128, :], [], ["PWB"], cast=True)
            ZBT = ab(32 * 256); ZBTv = ZBT.rearrange("p (i c) -> p i c", i=32)
            dma(ZBTv, zB[:, :].rearrange("(i p) c -> p i c", p=128), [id(zB)], ["ZBT"])
            DIFF = ab(256); DT = ab(2 * 512); DTv = DT.rearrange("p (j t) -> p j t", j=2)
            YB = ab(2 * 512); YBv = YB.rearrange("p (j t) -> p j t", j=2)
            pm = lambda g, di, par: POOLM[:, ((g * 9 + di) * 2 + par) * 128:((g * 9 + di) * 2 + par + 1) * 128]
            for i in range(32):
                ps, pk = bank()
                for g in range(4):
                    dls = [(di, dl) for di, dl in enumerate(POOL_DELTAS[g]) if 0 <= i + dl < 32]
                    for n, (di, dl) in enumerate(dls):
                        mm(ps[:, g * 64:(g + 1) * 64], pm(g, di, i % 2), ZBTv[:, i + dl, g * 64:(g + 1) * 64],
                           n == 0, n == len(dls) - 1, ["POOLM", "ZBT"], [pk])
                for g in range(4):
                    stt(DIFF[:, g * 64:(g + 1) * 64], ps[:, g * 64:(g + 1) * 64], PINV[:, i * 4 + g:i * 4 + g + 1],
                        ZBTv[:, i, g * 64:(g + 1) * 64], ALU.mult, ALU.subtract, [pk, "PINV", "ZBT"], ["DIFF"])
                ps2, pk2 = bank(); psb = ps2.bitcast(BF16)
                for j in range(2):
                    tr(psb[:, j * 128:(j + 1) * 128], DIFF[:, j * 128:(j + 1) * 128], IDB, ["DIFF", "IDB"], [pk2])
                act(DTv[:, :, (i % 4) * 128:(i % 4 + 1) * 128], psb[:, 0:256].rearrange("p (j t) -> p j t", j=2), AF.Identity, [pk2], ["DT"])
                if i % 4 == 3:
                    b = i // 4
                    for j in range(2):
                        ps3, pk3 = bank()
                        mm(ps3, PWBv[:, j, :], DTv[:, j, :], True, True, ["PWB", "DT"], [pk3])
                        ts(YBv[:, j, :], ps3, col(l, "pscale", j), None, ALU.mult, None, [pk3, "COLT"], ["YB"])
                    dma(catT[256:512, b * 512:(b + 1) * 512].rearrange("(j p) t -> p j t", p=128), YBv, ["YB"], [id(catT)])

        def conv_phase(l):
            DIAG = ab(2 * 31 * 128); DIAGv = DIAG.rearrange("p (i k n) -> p i k n", i=2, k=31)
            for ti in range(2):
                for k in range(31):
                    ts(DIAGv[:, ti, k, :], IDF, col(l, "cdw", k * 2 + ti), None, ALU.mult, None, ["CONST", "COLT"], ["DIAG"])
            PWC = ab(2 * 256); PWCv = PWC.rearrange("p (k n) -> p k n", k=2)
            dma(PWCv, conv_pw[l * 256:(l + 1) * 256, :].rearrange("(k p) n -> p k n", p=128), [], ["PWC"], cast=True)
            RAWC = ab(4 * 286); RAWCv = RAWC.rearrange("p (j t) -> p j t", j=4)
            SG = af(2 * 286); SGv = SG.rearrange("p (j t) -> p j t", j=2)
            U = ab(2 * 286); Uv = U.rearrange("p (j t) -> p j t", j=2)
            v2 = lambda a: a.rearrange("p (j t) -> p j t", j=2)
            UC = af(512); CEN = af(512); SQ = af(512); RSTD = af(256); UN = af(512); UNB = ab(512); YC = ab(512)
            UCv, CENv, SQv, UNv, UNBv, YCv = v2(UC), v2(CEN), v2(SQ), v2(UN), v2(UNB), v2(YC)
            for sg in range(16):
                lo, hi = sg * 256 - 15, sg * 256 + 271
                clo, chi = max(lo, 0), min(hi, T)
                if clo != lo or chi != hi:
                    mset(RAWC, 0.0, ["RAWC"])
                dma(RAWCv[:, :, clo - lo:clo - lo + (chi - clo)], zCT[:, clo:chi].rearrange("(j p) t -> p j t", p=128), [id(zCT)], ["RAWC"])
                act(SGv, RAWCv[:, 2:4, :], AF.Sigmoid, ["RAWC"], ["SG"])
                tt(Uv, RAWCv[:, 0:2, :], SGv, ALU.mult, ["RAWC", "SG"], ["U"])
                ts(Uv[:, :, 0:15], Uv[:, :, 0:15], CM[:, 0:1], None, ALU.mult, None, ["U", "CM"], ["U"])
                ts(Uv[:, :, 271:286], Uv[:, :, 271:286], CM[:, 0:1], None, ALU.mult, None, ["U", "CM"], ["U"])
                for ti in range(2):
                    ps, pk = bank()
                    for k in range(31):
                        mm(ps[:, 0:256], DIAGv[:, ti, k, :], Uv[:, ti, k:k + 256], k == 0, k == 30, ["DIAG", "U"], [pk])
                    act(UCv[:, ti, :], ps[:, 0:256], AF.Identity, [pk, "COLT"], ["UC"], bias=col(l, "convb", ti))
                ps, pk = bank()
                for ti in range(2):
                    mm(ps[:, 0:256], ONES, UCv[:, ti, :], ti == 0, ti == 1, ["CONST", "UC"], [pk])
                for ti in range(2):
                    stt(CENv[:, ti, :], ps[:, 0:256], -1.0 / 256, UCv[:, ti, :], ALU.mult, ALU.add, [pk, "UC"], ["CEN"])
                act(SQ, CEN, AF.Square, ["CEN"], ["SQ"])
                ps, pk = bank()
                for ti in range(2):
                    mm(ps[:, 0:256], ONES, SQv[:, ti, :], ti == 0, ti == 1, ["CONST", "SQ"], [pk])
                rsqrt(RSTD, ps[:, 0:256], 1.0 / 256, CONV_LN_EPS, [pk], ["RSTD"])
                for ti in range(2):
                    tt(UNv[:, ti, :], CENv[:, ti, :], RSTD, ALU.mult, ["CEN", "RSTD"], ["UN"])
                    ts(UNv[:, ti, :], UNv[:, ti, :], col(l, "clng", ti), col(l, "clnb", ti), ALU.mult, ALU.add, ["UN", "COLT"], ["UN"])
                act(UNB, UN, AF.Silu, ["UN"], ["UNB"])
                for co in range(2):
                    ps, pk = bank()
                    for ci in range(2):
                        mm(ps[:, 0:256], PWCv[:, ci, co * 128:(co + 1) * 128], UNBv[:, ci, :], ci == 0, ci == 1, ["PWC", "UNB"], [pk])
                    act(YCv[:, co, :], ps[:, 0:256], AF.Identity, [pk], ["YC"])
                dma(catT[512:768, sg * 256:(sg + 1) * 256].rearrange("(j p) t -> p j t", p=128), YCv, ["YC"], [id(catT)])

        def rwkv_phase(l):
            WA = []
            for d in range(2):
                w_ = ab(256)
                dma(w_, waup_d[(l * 2 + d) * 128:(l * 2 + d + 1) * 128, :], [], ["WA%d" % d], cast=True)
                WA.append(w_)
            GUP = ab(256)
            dma(GUP, g_up[l * 128:(l + 1) * 128, :], [], ["GUP"], cast=True)
            MUH = af(8); OMU = af(8); OMKA = af(2)
            mu_c = COLT[:, l * NCOL + COLS["mu"]: l * NCOL + COLS["mu"] + 8]
            ka_c = COLT[:, l * NCOL + COLS["ka"]: l * NCOL + COLS["ka"] + 2]
            ts(MUH, mu_c, 0.5, None, ALU.mult, None, ["COLT"], ["MUH"])
            ts(OMU, mu_c, -1.0, 1.0, ALU.mult, ALU.add, ["COLT"], ["OMU"])
            ts(OMKA, ka_c, -1.0, 1.0, ALU.mult, ALU.add, ["COLT"], ["OMKA"])
            RAW = ab(7 * 514); RAWv = RAW.rearrange("p (k t) -> p k t", k=7)
            ZS = af(7 * 512); ZSv = ZS.rearrange("p (k t) -> p k t", k=7)
            T1 = af(512); T2 = af(512)
            post_off = aoff["o"]
            mk = lambda f: [[f() for _ in range(2)] for _ in range(2)]
            RS, KS, BS, KDS = mk(lambda: ab(512)), mk(lambda: ab(512)), mk(lambda: ab(512)), mk(lambda: ab(512))
            BT, KT = mk(lambda: ab(8 * 2 * 128)), mk(lambda: ab(8 * 2 * 128))
            VT = mk(lambda: ab(8 * 128)); GL = mk(lambda: af(8))
            WTB = ab(512); ADB = ab(512); VB = ab(512)
            LD, AA, KK, SQ, RN, KKN, KD, EP = [af(512) for _ in range(8)]
            BSE, KSE, KDSE = mk(lambda: ab(2 * 512)), mk(lambda: ab(2 * 512)), mk(lambda: ab(2 * 512))
            mset(WTB, 0.0, ["WTB"]); mset(ADB, 0.0, ["ADB"])
            XB = [ab(512), ab(512)]; XTB = [ab(512), ab(512)]; QB = [ab(512), ab(512)]
            AK, BB, BK, RR, UU = [ab(512) for _ in range(5)]
            WSB = af(512); NL = ab(512); R1 = ab(512)
            ST = ab(8 * 64); STv = ST.rearrange("p (u v) -> p u v", u=8)
            SI = [af(256), af(256)]; SOUT = af(256); SOUTv = SOUT.rearrange("p (q v) -> p q v", v=64)
            kname = lambda n, d, hp: "%s%d%d" % (n, d, hp)
            for d in range(2):
                for hp in range(2):
                    mset(BT[d][hp], 0.0, [kname("BT", d, hp)]); mset(KT[d][hp], 0.0, [kname("KT", d, hp)])
                    for (arr_, nm_) in ((BSE, "BSE"), (KSE, "KSE"), (KDSE, "KDSE")):
                        mset(arr_[d][hp], 0.0, [kname(nm_, d, hp)])
            mset(ST, 0.0, ["ST"])
            BTv = [[BT[d][hp].rearrange("p (c e n) -> p c e n", c=8, e=2) for hp in range(2)] for d in range(2)]
            KTv = [[KT[d][hp].rearrange("p (c e n) -> p c e n", c=8, e=2) for hp in range(2)] for d in range(2)]
            VTv = [[VT[d][hp].rearrange("p (c n) -> p c n", c=8) for hp in range(2)] for d in range(2)]

            def tshift(k, w, rev, out3, slot=None):
                ks_ = k
                k = k if slot is None else slot
                tt(T1, RAWv[:, k, 0:512], TSM[:, 0:512], ALU.mult, ["RAW", "TSM"], ["T1"])
                tt(T2, RAWv[:, k, 2:514], TSM[:, 512:1024], ALU.mult, ["RAW", "TSM"], ["T2"])
                tt(T1, T1, T2, ALU.add, ["T1", "T2"], ["T1"])
                ts(T1, T1, MUH[:, ks_:ks_ + 1], None, ALU.mult, None, ["T1", "MUH"], ["T1"])
                zo = out3[:, ::-1] if rev else out3
                stt(zo, RAWv[:, k, 1:513], OMU[:, ks_:ks_ + 1], T1, ALU.mult, ALU.add, ["RAW", "OMU", "T1"], ["ZS"])

            def load_raw(w, k0, k1, s0=None):
                s0 = k0 if s0 is None else s0
                t0 = w * 512
                lo, hi = t0 - 1, t0 + 513
                clo, chi = max(lo, 0), min(hi, T)
                if clo != lo:
                    mset(RAWv[:, :, 0:1], 0.0, ["RAW"])
                if chi != hi:
                    mset(RAWv[:, :, 513:514], 0.0, ["RAW"])
                dma(RAWv[:, s0:s0 + k1 - k0, clo - lo:clo - lo + chi - clo], zAT[k0 * 128:k1 * 128, clo:chi].rearrange("(k p) t -> p k t", p=128),
                    [id(zAT)], ["RAW"])

            def prep(d, w):
                t0 = w * 512
                load_raw(w, 0, 7)
                for k in range(7):
                    tshift(k, w, d == 1, ZSv[:, k, :])
                act(WTB[0:64, :], ZSv[0:64, 6, :], AF.Tanh, ["ZS"], ["WTB"])
                cp(ADB[64:128, :], ZSv[64:128, 6, :], ["ZS"], ["ADB"])
                for hp in range(2):
                    hs = slice(hp * 128, (hp + 1) * 128)
                    kn = lambda n: kname(n, d, hp)
                    ps, pk = bank()
                    mm(ps, WA[d][:, hs], WTB, True, True, ["WA%d" % d, "WTB"], [pk])
                    act(LD, ps, AF.Sigmoid, [pk, "COLT"], ["LD"], bias=col(l, "w0", d * 2 + hp))
                    ts(LD, LD, -math.exp(-0.5), None, ALU.mult, None, ["LD"], ["LD"])
                    ps, pk = bank()
                    mm(ps, WA[d][:, hs], ADB, True, True, ["WA%d" % d, "ADB"], [pk])
                    act(AA, ps, AF.Sigmoid, [pk, "COLT"], ["AA"], bias=col(l, "a0", d * 2 + hp))
                    ts(KK, ZSv[:, 2 + hp, :], col(l, "kk", hp), None, ALU.mult, None, ["ZS", "COLT"], ["KK"])
                    tt(SQ, KK, KK, ALU.mult, ["KK"], ["SQ"])
                    ps, pk = bank()
                    mm(ps, BONES, SQ, True, True, ["CONST", "SQ"], [pk])
                    rsqrt(RN, ps, 1.0, 1e-12, [pk], ["RN"])
                    tt(KKN, KK, RN, ALU.mult, ["KK", "RN"], ["KKN"])
                    ts(RN, AA, col(l, "ka", hp), OMKA[:, hp:hp + 1], ALU.mult, ALU.add, ["AA", "COLT", "OMKA"], ["RN"])
                    tt(KD, ZSv[:, 2 + hp, :], RN, ALU.mult, ["ZS", "RN"], ["KD"])
                    stt(SQ, ZSv[:, hp, :], col(l, "rk", hp), KD, ALU.mult, ALU.mult, ["ZS", "COLT", "KD"], ["SQ"])
                    ps, pk = bank()
                    mm(ps, BONES, SQ, True, True, ["CONST", "SQ"], [pk])
                    bo = KK[:, ::-1] if d == 1 else KK
                    tt(bo, ps, ZSv[:, 4 + hp, :], ALU.mult, [pk, "ZS"], ["KK"])
                    dma(bonT[d * 256 + hp * 128:d * 256 + (hp + 1) * 128, t0:t0 + 512], KK, ["KK"], ["bonT"])
                    scan(RN, RMASK, LD, 0.0, ALU.mult, ALU.add, ["CONST", "LD"], ["RN"])
                    tt(LD, RN, LD, ALU.subtract, ["RN", "LD"], ["LD"])
                    act(GL[d][hp], RN.rearrange("p (c t) -> p c t", t=64)[:, :, 63], AF.Exp, ["RN"], [kn("GL")])
                    act(EP, RN, AF.Exp, ["RN"], ["EP"])
                    tt(RS[d][hp], ZSv[:, hp, :], EP, ALU.mult, ["ZS", "EP"], [kn("RS")])
                    act(SQ, RN, AF.Exp, ["RN"], ["SQ"], scale=-1.0)
                    act(EP, LD, AF.Exp, ["LD"], ["EP"])
                    tt(KS[d][hp], KKN, EP, ALU.mult, ["KKN", "EP"], [kn("KS")])
                    tt(AA, AA, KKN, ALU.mult, ["AA", "KKN"], ["AA"])
                    tt(BS[d][hp], AA, SQ, ALU.mult, ["AA", "SQ"], [kn("BS")])
                    tt(KDS[d][hp], KD, SQ, ALU.mult, ["KD", "SQ"], [kn("KDS")])
                    for (src_, dst_, nm_) in ((KS, KSE, "KS"), (BS, BSE, "BS"), (KDS, KDSE, "KDS")):
                        dv_ = dst_[d][hp].rearrange("p (e t) -> p e t", e=2)
                        cp(dv_[0:64, 0, :], src_[d][hp][0:64, :], [kn(nm_)], [kn(nm_ + "E")], eng="pool")
                        cp(dv_[64:128, 1, :], src_[d][hp][64:128, :], [kn(nm_)], [kn(nm_ + "E")], eng="pool")
                    cp(VB, ZSv[:, 4 + hp, :], ["ZS"], ["VB"])
                    for (src, sk, dstv, dk, ex) in ((BS[d][hp], kn("BS"), BTv[d][hp], kn("BT"), True),
                                                    (KDS[d][hp], kn("KDS"), KTv[d][hp], kn("KT"), True),
                                                    (VB, "VB", VTv[d][hp], kn("VT"), False)):
                        ps, pk = bank(); psb = ps.bitcast(BF16)
                        for c in range(8):
                            tr(psb[0:64, c * 128:(c + 1) * 128], src[:, c * 64:(c + 1) * 64], IDB, [sk, "IDB"], [pk])
                        pv = psb[0:64, :].rearrange("p (c n) -> p c n", n=128)
                        if ex:
                            act(dstv[0:64, :, 0, 0:64], pv[:, :, 0:64], AF.Identity, [pk], [dk])
                            act(dstv[0:64, :, 1, 64:128], pv[:, :, 64:128], AF.Identity, [pk], [dk])
                        else:
                            act(dstv[0:64, :, :], pv, AF.Identity, [pk], [dk])

            UNITS = [(d * 4 + h, d, h, h // 2, h % 2) for d in range(2) for h in range(4)]
            psl = lambda hh: slice(hh * 64, hh * 64 + 64)
            ALLK = [kname(n, d, hp) for n in ("RS", "KS", "BS", "KDS", "KSE", "BSE", "KDSE") for d in range(2) for hp in range(2)]
            VTK = [kname("VT", d, hp) for d in range(2) for hp in range(2)]
            BTK = [kname(n, d, hp) for n in ("BT", "KT") for d in range(2) for hp in range(2)]
            GLK = [kname("GL", d, hp) for d in range(2) for hp in range(2)]

            def step(wf, wb, c8):
                tsl = slice(c8 * 64, (c8 + 1) * 64)

                def allmm(lf, rf, reads):
                    ps, pk = bank()
                    for (u, d, h, hp, hh) in UNITS:
                        cs = slice(u * 64, (u + 1) * 64)
                        mm(ps[0:64, cs], lf(u, d, hp, hh, cs), rf(u, d, hp, hh, cs), True, True, reads, [pk])
                    return ps, pk
                FK = lambda arr: (lambda u, d, hp, hh, cs: arr[d][hp][:, tsl])
                FE = lambda arr: (lambda u, d, hp, hh, cs: arr[d][hp].rearrange("p (e t) -> p e t", e=2)[:, hh, tsl])
                SB = lambda buf: (lambda u, d, hp, hh, cs: buf[0:64, cs])
                VU = lambda u, d, hp, hh, cs: VTv[d][hp][0:64, c8, hh * 64:(hh + 1) * 64]
                ps, pk = allmm(FE(BSE), FK(KS), ALLK)
                tt(XTB[0][0:64, :], ps[0:64, :], MSD[0:64, :], ALU.mult, [pk, "CONST"], ["XT0"])
                tt(QB[0][0:64, :], I8[0:64, :], XTB[0][0:64, :], ALU.subtract, ["CONST", "XT0"], ["Q0"])
                ps, pk = allmm(FE(KSE), FK(BS), ALLK)
                tt(XB[0][0:64, :], ps[0:64, :], MLD[0:64, :], ALU.mult, [pk, "CONST"], ["X0"])
                tt(NL[0:64, :], ps[0:64, :], MLL[0:64, :], ALU.mult, [pk, "CONST"], ["NL"])
                ps, pk = allmm(FE(KDSE), FK(KS), ALLK)
                tt(AK[0:64, :], ps[0:64, :], MS[0:64, :], ALU.mult, [pk, "CONST"], ["AK"])
                ps, pk = allmm(FE(BSE), FK(RS), ALLK)
                tt(BB[0:64, :], ps[0:64, :], MI[0:64, :], ALU.mult, [pk, "CONST"], ["BB"])
                ps, pk = allmm(FE(KDSE), FK(RS), ALLK)
                tt(BK[0:64, :], ps[0:64, :], MI[0:64, :], ALU.mult, [pk, "CONST"], ["BK"])
                if RW_CUT < 2:
                    return
                cur = 0
                for lev in range(1, 4):
                    nxt = 1 - cur
                    ps, pk = allmm(SB(XTB[cur]), SB(XB[cur]), ["XT%d" % cur, "X%d" % cur])
                    ps2 = None
                    if lev < 3:
                        ps2, pk2 = allmm(SB(XB[cur]), SB(XTB[cur]), ["XT%d" % cur, "X%d" % cur])
                    act(XB[nxt][0:64, :], ps[0:64, :], AF.Identity, [pk], ["X%d" % nxt])
                    if ps2 is not None:
                        act(XTB[nxt][0:64, :], ps2[0:64, :], AF.Identity, [pk2], ["XT%d" % nxt])
                    ps, pk = allmm(SB(XB[nxt]), SB(QB[cur]), ["X%d" % nxt, "Q%d" % cur])
                    tt(QB[nxt][0:64, :], ps[0:64, :], QB[cur][0:64, :], ALU.add, [pk, "Q%d" % cur], ["Q%d" % nxt])
                    cur = nxt
                qd = cur
                if RW_CUT < 3:
                    return
                YB, YTB, P1B, ZB = XB[0], XTB[0], XB[1], XTB[1]
                ps, pk = allmm(SB(QB[qd]), SB(NL), ["Q%d" % qd, "NL"])
                act(YB[0:64, :], ps[0:64, :], AF.Identity, [pk], ["X0"])
                ps, pk = allmm(SB(NL), SB(QB[qd]), ["Q%d" % qd, "NL"])
                cp(YTB[0:64, :], ps[0:64, :], [pk], ["XT0"])
                stt(WSB[0:64, :], ps[0:64, :], -1.0, I8[0:64, :], ALU.mult, ALU.add, ["CONST", pk], ["WSB"])
                ps, pk = allmm(SB(YB), SB(YTB), ["X0", "XT0"])
                cp(P1B[0:64, :], ps[0:64, :], [pk], ["X1"])
                tt(WSB[0:64, :], ps[0:64, :], WSB[0:64, :], ALU.add, ["WSB", pk], ["WSB"])
                ps, pk = allmm(SB(YB), SB(P1B), ["X0", "X1"])
                stt(ZB[0:64, :], ps[0:64, :], -1.0, WSB[0:64, :], ALU.mult, ALU.add, ["WSB", pk], ["XT1"])
                if RW_CUT < 4:
                    return
                ps, pk = allmm(SB(AK), VU, ["AK"] + VTK)
                act(WSB[0:64, :], ps[0:64, :], AF.Identity, [pk], ["WSB"])
                ps, pk = allmm(FK(KS), lambda u, d, hp, hh, cs: STv[:, u, :], ALLK + ["ST"])
                tt(RR[0:64, :], ps[0:64, :], WSB[0:64, :], ALU.add, [pk, "WSB"], ["RR"])
                ps, pk = allmm(SB(QB[qd]), SB(RR), ["Q%d" % qd, "RR"])
                act(R1[0:64, :], ps[0:64, :], AF.Identity, [pk], ["R1"])
                ps, pk = allmm(SB(ZB), SB(R1), ["XT1", "R1"])
                ts(UU[0:64, :], ps[0:64, :], -1.0, None, ALU.mult, None, [pk], ["UU"])
                if RW_CUT < 5:
                    return
                psy, pky = bank()
                pss, pks = bank()
                for (u, d, h, hp, hh) in UNITS:
                    cs = slice(u * 64, (u + 1) * 64)
                    vu = VU(u, d, hp, hh, cs)
                    mm(psy[0:64, cs], STv[:, u, :], RS[d][hp][:, tsl], True, False, ALLK + ["ST"], [pky])
                    mm(psy[0:64, cs], UU[0:64, cs], BB[0:64, cs], False, False, ["UU", "BB"], [pky])
                    mm(psy[0:64, cs], vu, BK[0:64, cs], False, True, ["BK"] + VTK, [pky])
                if RW_CUT < 6:
                    return
                for (u, d, h, hp, hh) in UNITS:
                    cs = slice(u * 64, (u + 1) * 64)
                    vu = VU(u, d, hp, hh, cs)
                    mm(pss[:, cs], BTv[d][hp][0:64, c8, hh, :], UU[0:64, cs], True, False, ["UU"] + BTK, [pks])
                    mm(pss[:, cs], KTv[d][hp][0:64, c8, hh, :], vu, False, False, BTK + VTK, [pks])
                    mm(pss[:, cs], IDB, STv[:, u, :], False, True, ["IDB", "ST"], [pks])
                for d in range(2):
                    for hp in range(2):
                        u0 = d * 4 + hp * 2
                        ts(STv[:, u0:u0 + 2, :], pss[:, u0 * 64:(u0 + 2) * 64].rearrange("p (u v) -> p u v", v=64),
                           GL[d][hp][:, c8:c8 + 1], None, ALU.mult, None, [pks] + GLK, ["ST"])
                if RW_CUT < 7:
                    return
                act(WSB[0:64, 0:256], psy[0:64, 0:256], AF.Identity, [pky], ["WSB"])
                act(WSB[0:64, 256:512].rearrange("p (h t) -> p h t", t=64)[:, :, ::-1],
                    psy[0:64, 256:512].rearrange("p (h t) -> p h t", t=64), AF.Identity, [pky], ["WSB"])
                for d in range(2):
                    tok0 = wf * 512 + c8 * 64 if d == 0 else wb * 512 + 512 - (c8 + 1) * 64
                    dma(yT[d * 256:(d + 1) * 256, tok0:tok0 + 64].rearrange("(h v) t -> v h t", v=64),
                        WSB[0:64, d * 256:(d + 1) * 256].rearrange("p (h t) -> p h t", t=64), ["WSB"], ["yT"])
                if RW_CUT < 8:
                    return
                if c8 % 4 == 3:
                    tt(SOUTv, STv.rearrange("p (q e) v -> p q e v", e=2)[:, :, 0, :], STv.rearrange("p (q e) v -> p q e v", e=2)[:, :, 1, :],
                       ALU.add, ["ST"], ["SOUT"])
                    for d in range(2):
                        seg = (2 * wf + c8 // 4) if d == 0 else (2 * wb + 1 - c8 // 4)
                        base = ((l * NSEG + seg) * 2 + d) * 256
                        dma(st_r[base:base + 256, :].rearrange("(q p) v -> p q v", p=128), SOUTv[:, d * 2:(d + 1) * 2, :], ["SOUT"], ["st_r"])
                        segn = seg + 1 if d == 0 else seg - 1
                        if 0 <= segn < NSEG:
                            rb = ((l * NSEG + segn) * 2 + d) * 128
                            dma(SI[d], s0r_d[rb:rb + 128, :], [], ["SI%d" % d])
                            stt(STv[:, d * 4:(d + 1) * 4, :], STv[:, d * 4:(d + 1) * 4, :], CM[:, 0:1],
                                SI[d].rearrange("p (h v) -> p h v", v=64), ALU.mult, ALU.add, ["ST", "CM", "SI%d" % d], ["ST"])

            for d in range(2):
                seg = 0 if d == 0 else NSEG - 1
                rb = ((l * NSEG + seg) * 2 + d) * 128
                dma(SI[d], s0r_d[rb:rb + 128, :], [], ["SI%d" % d])
                cp(STv[:, d * 4:(d + 1) * 4, :], SI[d].rearrange("p (h v) -> p h v", v=64), ["SI%d" % d], ["ST"])
            for j in range(8 if RW_STAGE >= 3 else 1):
                prep(0, j)
                prep(1, 7 - j)
                if RW_STAGE >= 2:
                    for c8 in range(8):
                        step(j, 7 - j, c8)
            P.barrier()
            aoff["o"] = post_off
            Y0, Y1, Y2, Y3, CEN2, SQ2, RS2, YO = [af(512) for _ in range(8)]
            SGB = ab(512); GZ = af(512); CATA = ab(512)
            for w in range(8):
                t0 = w * 512
                load_raw(w, 7, 8, 0)
                tshift(7, w, False, GZ, slot=0)
                act(SGB, GZ, AF.Sigmoid, ["ZS"], ["SGB"])
                for hp in range(2):
                    rows = lambda d: slice(d * 256 + hp * 128, d * 256 + (hp + 1) * 128)
                    dma(Y0, yT[rows(0), t0:t0 + 512], ["yT"], ["Y0"]); dma(Y1, yT[rows(1), t0:t0 + 512], ["yT"], ["Y1"])
                    dma(Y2, bonT[rows(0), t0:t0 + 512], ["bonT"], ["Y2"]); dma(Y3, bonT[rows(1), t0:t0 + 512], ["bonT"], ["Y3"])
                    tt(Y0, Y0, Y1, ALU.add, ["Y0", "Y1"], ["Y0"])
                    tt(Y2, Y2, Y3, ALU.add, ["Y2", "Y3"], ["Y2"])
                    tt(Y0, Y0, Y2, ALU.add, ["Y0", "Y2"], ["Y0"])
                    ps, pk = bank()
                    mm(ps, BONES, Y0, True, True, ["CONST", "Y0"], [pk])
                    stt(CEN2, ps, -1.0 / 64, Y0, ALU.mult, ALU.add, [pk, "Y0"], ["CEN2"])
                    act(SQ2, CEN2, AF.Square, ["CEN2"], ["SQ2"])
                    ps, pk = bank()
                    mm(ps, BONES, SQ2, True, True, ["CONST", "SQ2"], [pk])
                    rsqrt(RS2, ps, 1.0 / 64, RWKV_LN_EPS, [pk], ["RS2"])
                    tt(YO, CEN2, RS2, ALU.mult, ["CEN2", "RS2"], ["YO"])
                    ts(YO, YO, col(l, "lng", hp), col(l, "lnb", hp), ALU.mult, ALU.add, ["YO", "COLT"], ["YO"])
                    ps, pk = bank()
                    mm(ps, GUP[:, hp * 128:(hp + 1) * 128], SGB, True, True, ["GUP", "SGB"], [pk])
                    tt(CATA, YO, ps, ALU.mult, ["YO", pk], ["CATA"])
                    dma(catT[hp * 128:(hp + 1) * 128, t0:t0 + 512], CATA, ["CATA"], [id(catT)])

        def mlstm_phase(l):
            TG, GI, GF, LI, LF, BCUM, DD, GG, ED, CL, WE, TMPG = [af(256) for _ in range(12)]
            NFB = af(1); EBL = af(4)
            v4 = lambda a: a.rearrange("p (c t) -> p c t", t=64)
            for (gi, dst, dk) in ((0, GI, "GI"), (1, GF, "GF")):
                for d in range(2):
                    dma(TG[d * 64:(d + 1) * 64, :], zG[gi * 8 + d * 4:gi * 8 + (d + 1) * 4, :].rearrange("h (s t) -> (h s) t", t=256), ["zG"], ["TG"])
                cp(dst[0:64, :], TG[0:64, :], ["TG"], [dk])
                cp(dst[64:128, :], TG[64:128, ::-1], ["TG"], [dk])
            ts(LI, GI, GB[:, l * 2:l * 2 + 1], None, ALU.add, None, ["GI", "GB"], ["LI"])
            ts(NFB, GB[:, l * 2 + 1:l * 2 + 2], -1.0, None, ALU.mult, None, ["GB"], ["NFB"])
            act(LF, GF, AF.Exp, ["GF", "NFB"], ["LF"], bias=NFB[:, 0:1], scale=-1.0)
            act(LF, LF, AF.Ln, ["LF", "ONEC"], ["LF"], bias=ONEC[:, 0:1])
            ts(LF, LF, -1.0, None, ALU.mult, None, ["LF"], ["LF"])
            scan(BCUM, RMASK[:, 0:256], LF, 0.0, ALU.mult, ALU.add, ["CONST", "LF"], ["BCUM"])
            tt(DD, LI, BCUM, ALU.subtract, ["LI", "BCUM"], ["DD"])
            scan(GG, RNEG[:, 0:256], DD, -1e30, ALU.add, ALU.max, ["CONST", "DD"], ["GG"])
            CHG = af(64); CHB = af(64); S0M = af(16); MOUT = af(16); MC = af(1); EMI = af(16); EMO = af(16)
            XR = af(128); EMIR = af(128); EMOR = af(128); XA = af(512); ALT = af(512)
            for t_ in (CHG, CHB, S0M, MOUT, MC):
                mset(t_, 0.0, [id(t_)])
            GC4 = af(4); BL4 = af(4)
            cp(GC4, v4(GG)[:, :, 63], ["GG"], ["GC4"]); cp(BL4, v4(BCUM)[:, :, 63], ["BCUM"], ["BL4"])
            dma(gtab[0:128, 0:4], GC4, ["GC4"], ["gtab"])
            dma(gtab[128:256, 0:4], BL4, ["BL4"], ["gtab"])
            for d in range(2):
                rs = slice(d * 32, d * 32 + 4)
                dma(CHG[rs, :].rearrange("p (s c) -> p s c", c=4), gtab[d * 64:(d + 1) * 64, 0:4].rearrange("(h s) c -> h s c", s=16), ["gtab"], [id(CHG)])
                dma(CHB[rs, :].rearrange("p (s c) -> p s c", c=4), gtab[128 + d * 64:128 + (d + 1) * 64, 0:4].rearrange("(h s) c -> h s c", s=16), ["gtab"], [id(CHB)])
                dma(S0M[rs, :], s0m_d[l * 8 + d * 4:l * 8 + (d + 1) * 4, :], [], [id(S0M)])
            for d in range(2):
                rs = slice(d * 32, d * 32 + 4)
                mk_ = "MC%d" % d
                for n in range(64):
                    seg = n // 4 if d == 0 else 15 - n // 4
                    colt = seg * 4 + n % 4
                    if n % 4 == 0:
                        if n == 0:
                            cp(MC[rs, :], S0M[rs, seg:seg + 1], [id(S0M)], [mk_], eng="dve")
                        else:
                            ts(MC[rs, :], MC[rs, :], CM[rs, 0:1], None, ALU.mult, None, [mk_, "CM"], [mk_], eng="dve")
                            tt(MC[rs, :], MC[rs, :], S0M[rs, seg:seg + 1], ALU.add, [mk_, id(S0M)], [mk_], eng="dve")
                    tt(MC[rs, :], MC[rs, :], CHG[rs, colt:colt + 1], ALU.max, [mk_, id(CHG)], [mk_], eng="dve")
                    tt(MC[rs, :], MC[rs, :], CHB[rs, colt:colt + 1], ALU.add, [mk_, id(CHB)], [mk_], eng="dve")
                    if n % 4 == 3:
                        cp(MOUT[rs, seg:seg + 1], MC[rs, :], [mk_], ["MOUT%d" % d], eng="dve")
                dma(st_m[l * 8 + d * 4:l * 8 + (d + 1) * 4, :], MOUT[rs, :], ["MOUT%d" % d, id(MOUT)], ["st_m"])
            act(EMI[0:64, :], S0M[0:64, :], AF.Exp, [id(S0M)], ["EMI"])
            act(EMO[0:64, :], MOUT[0:64, :], AF.Exp, [id(MOUT), "MOUT0", "MOUT1"], ["EMO"], scale=-1.0)
            for (src, sk, dst, dk) in ((EMI, "EMI", EMIR, "EMIR"), (EMO, "EMO", EMOR, "EMOR")):
                tt(XR[0:64, :].rearrange("p (s j) -> p s j", j=8), src[0:64, :].unsqueeze(2).broadcast_to([64, 16, 8]),
                   PK2[0:64, 0:8].unsqueeze(1).broadcast_to([64, 16, 8]), ALU.mult, [sk, "CONST"], ["XR"])
                ps, pk = bank()
                mm(ps[:, 0:128], ONES[0:64, :], XR[0:64, :], True, True, ["CONST", "XR"], [pk])
                cp(dst, ps[:, 0:128], [pk], [dk])
            act(ED, DD, AF.Exp, ["DD"], ["ED"])
            act(CL, BCUM, AF.Exp, ["BCUM"], ["CL"], scale=-1.0)
            tt(v4(TMPG), v4(DD), v4(BCUM)[:, :, 63:64].broadcast_to([128, 4, 64]), ALU.add, ["DD", "BCUM"], ["TMPG"])
            act(WE, TMPG, AF.Exp, ["TMPG", "LN8C"], ["WE"], bias=LN8C[:, 0:1])
            act(EBL, v4(BCUM)[:, :, 63], AF.Exp, ["BCUM"], ["EBL"])
            tt(XA.rearrange("p (s c j) -> p s c j", s=16, c=4), PICKv.unsqueeze(2).broadcast_to([128, 16, 4, 8]),
               EBL.unsqueeze(1).unsqueeze(3).broadcast_to([128, 16, 4, 8]), ALU.mult, ["CONST", "EBL"], ["XA"])
            ps, pk = bank()
            mm(ps, ONES, XA, True, True, ["CONST", "XA"], [pk])
            cp(ALT, ps, [pk], ["ALT"])
            ALTv = ALT.rearrange("p (n j) -> p n j", j=8)
            EDT, WET, CLT = af(512), af(512), af(512)
            for (tab, tk, dst, dk) in ((ED, "ED", EDT, "EDT"), (WE, "WE", WET, "WET"), (CL, "CL", CLT, "CLT")):
                ps, pk = bank()
                for sg in range(16):
                    for c in range(4):
                        n = sg * 4 + c
                        mm(ps[0:64, n * 8:(n + 1) * 8], tab[:, c * 64:(c + 1) * 64], PICKv[:, sg, :], True, True, [tk, "CONST"], [pk])
                cp(dst[0:64, :], ps[0:64, :], [pk], [dk])
            EDTv, WETv, CLTv = [a.rearrange("p (n j) -> p n j", j=8) for a in (EDT, WET, CLT)]
            EMIRv = EMIR.rearrange("p (s j) -> p s j", j=8); EMORv = EMOR.rearrange("p (s j) -> p s j", j=8)
            RAWQ = ab(4 * 514); RAWQv = RAWQ.rearrange("p (k t) -> p k t", k=4)
            Q1, Q2, Q3 = af(512), af(512), af(512)
            QK = [[ab(512) for _ in range(4)] for _ in range(2)]
            KE = [[ab(2 * 512) for _ in range(2)] for _ in range(2)]
            KEv = [[KE[d][hp].rearrange("p (e t) -> p e t", e=2) for hp in range(2)] for d in range(2)]
            KH = [[ab(8 * 2 * 128) for _ in range(2)] for _ in range(2)]
            KHv = [[KH[d][hp].rearrange("p (c e n) -> p c e n", c=8, e=2) for hp in range(2)] for d in range(2)]
            VA = [ab(8 * 4 * 65), ab(8 * 4 * 65)]
            VAv = [VA[d].rearrange("p (c h n) -> p c h n", c=8, h=4) for d in range(2)]
            CT = af(8 * 65); CTv = CT.rearrange("p (u n) -> p u n", n=65)
            CB = ab(8 * 65); CBv = CB.rearrange("p (u n) -> p u n", n=65)
            VN = ab(8 * 256); HMR = af(256)
            SR = ab(512); TMPS = af(512); HM = af(512); DN = af(8); TMPC = af(260); SIC = af(260); CO = af(130)
            TMPCv = TMPC.rearrange("p (h n) -> p h n", n=65); SICv = SIC.rearrange("p (h n) -> p h n", n=65); COv = CO.rearrange("p (q n) -> p q n", n=65)
            for d in range(2):
                mset(VA[d], 1.0, ["VA%d" % d])
                for hp in range(2):
                    mset(KH[d][hp], 0.0, ["KH%d%d" % (d, hp)])
                    mset(KE[d][hp], 0.0, ["KE%d%d" % (d, hp)])
            mset(CT, 0.0, ["CT"])

            def seg_of(d, w, c8):
                return (2 * w + c8 // 4) if d == 0 else (2 * w + 1 - c8 // 4)

            def enter_seg(d, seg, first):
                rb = ((l * NSEG + seg) * 2 + d) * 128
                dma(SIC, s0c_d[rb:rb + 128, :], [], ["SIC"])
                tt(SICv, SICv, EMIRv[:, seg, d * 4:(d + 1) * 4].unsqueeze(2).broadcast_to([128, 4, 65]), ALU.mult, ["SIC", "EMIR"], ["SIC"])
                if first:
                    cp(CTv[:, d * 4:(d + 1) * 4, :], SICv, ["SIC"], ["CT"])
                else:
                    stt(CTv[:, d * 4:(d + 1) * 4, :], CTv[:, d * 4:(d + 1) * 4, :], CM[:, 0:1], SICv, ALU.mult, ALU.add, ["CT", "CM", "SIC"], ["CT"])
                cp(CBv[:, d * 4:(d + 1) * 4, :], CTv[:, d * 4:(d + 1) * 4, :], ["CT"], ["CB"])

            def prep(d, w):
                t0 = w * 512
                lo, hi = t0 - 1, t0 + 513
                clo, chi = max(lo, 0), min(hi, T)
                if clo != lo:
                    mset(RAWQv[:, :, 0:1], 0.0, ["RAWQ"])
                if chi != hi:
                    mset(RAWQv[:, :, 513:514], 0.0, ["RAWQ"])
                dma(RAWQv[:, :, clo - lo:clo - lo + chi - clo], zQKT[:, clo:chi].rearrange("(k p) t -> p k t", p=128), [id(zQKT)], ["RAWQ"])
                for k in range(4):
                    stt(Q1, RAWQv[:, k, 0:512], col(l, "qkc", 0 * 4 + k), TSM[:, 0:512], ALU.mult, ALU.mult, ["RAWQ", "COLT", "TSM"], ["Q1"])
                    stt(Q2, RAWQv[:, k, 2:514], col(l, "qkc", 2 * 4 + k), TSM[:, 512:1024], ALU.mult, ALU.mult, ["RAWQ", "COLT", "TSM"], ["Q2"])
                    stt(Q3, RAWQv[:, k, 1:513], col(l, "qkc", 1 * 4 + k), Q1, ALU.mult, ALU.add, ["RAWQ", "COLT", "Q1"], ["Q3"])
                    tt(Q3, Q3, Q2, ALU.add, ["Q3", "Q2"], ["Q3"])
                    qo = QK[d][k][:, ::-1] if d == 1 else QK[d][k]
                    act(qo, Q3, AF.Silu, ["Q3"], ["QK%d%d" % (d, k)])
                    if k >= 2:
                        cp(KEv[d][k - 2][0:64, 0, :], QK[d][k][0:64, :], ["QK%d%d" % (d, k)], ["KE%d%d" % (d, k - 2)], eng="pool")
                        cp(KEv[d][k - 2][64:128, 1, :], QK[d][k][64:128, :], ["QK%d%d" % (d, k)], ["KE%d%d" % (d, k - 2)], eng="pool")
                vsrc = zV[t0:t0 + 512, :]
                if d == 0:
                    for h in range(4):
                        dma(VAv[d][0:64, :, h, 0:64], vsrc[:, h * 64:(h + 1) * 64].rearrange("(c p) v -> p c v", p=64), [id(zV)], ["VA%d" % d])
                else:
                    dma(VN[0:64, :].rearrange("p (c n) -> p c n", c=8), vsrc.rearrange("(c p) n -> p c n", p=64), [id(zV)], ["VN"])
                    for i2 in range(4):
                        ps, pk = bank()
                        mm(ps[0:64, :], JB[0:64, :], VN[0:64, i2 * 512:(i2 + 1) * 512], True, True, ["JB", "VN"], [pk])
                        for cc in range(2):
                            act(VAv[d][0:64, i2 * 2 + cc, :, 0:64], ps[0:64, cc * 256:(cc + 1) * 256].rearrange("p (h v) -> p h v", v=64), AF.Identity, [pk], ["VA%d" % d])
                for hp in range(2):
                    ps, pk = bank(); psb = ps.bitcast(BF16)
                    for c in range(8):
                        tr(psb[0:64, c * 128:(c + 1) * 128], QK[d][2 + hp][:, c * 64:(c + 1) * 64], IDB, ["QK%d%d" % (d, 2 + hp), "IDB"], [pk])
                    pv = psb[0:64, :].rearrange("p (c n) -> p c n", n=128)
                    for half in range(2):
                        seg = seg_of(d, w, half * 4)
                        for hh in range(2):
                            u = d * 4 + hp * 2 + hh
                            tt(KHv[d][hp][0:64, half * 4:(half + 1) * 4, hh, hh * 64:(hh + 1) * 64], pv[:, half * 4:(half + 1) * 4, hh * 64:(hh + 1) * 64],
                               WETv[0:64, seg * 4:seg * 4 + 4, u:u + 1].broadcast_to([64, 4, 64]), ALU.mult, [pk, "WET"], ["KH%d%d" % (d, hp)])

            UNITS = [(d * 4 + h, d, h, h // 2, h % 2) for d in range(2) for h in range(4)]
            psl = lambda hh: slice(hh * 64, hh * 64 + 64)
            QKK = ["QK%d%d" % (d, k) for d in range(2) for k in range(4)] + ["KE%d%d" % (d, hp) for d in range(2) for hp in range(2)]

            def step(wf, wb, c8):
                tsl = slice(c8 * 64, (c8 + 1) * 64)
                pss, pks = bank()
                for (u, d, h, hp, hh) in UNITS:
                    cs = slice(u * 64, (u + 1) * 64)
                    mm(pss[0:64, cs], KEv[d][hp][:, hh, tsl], QK[d][hp][:, tsl], True, True, QKK, [pks])
                for d in range(2):
                    w = wf if d == 0 else wb
                    seg = seg_of(d, w, c8); n = seg * 4 + c8 % 4
                    tt(TMPS[0:64, d * 256:(d + 1) * 256].rearrange("p (h t) -> p h t", t=64), pss[0:64, d * 256:(d + 1) * 256].rearrange("p (h t) -> p h t", t=64),
                       EDTv[0:64, n, d * 4:(d + 1) * 4].unsqueeze(2).broadcast_to([64, 4, 64]), ALU.mult, [pks, "EDT"], ["TMPS"])
                tt(SR[0:64, :], TMPS[0:64, :], MI8[0:64, :], ALU.mult, ["TMPS", "CONST"], ["SR"])
                for d in range(2):
                    w = wf if d == 0 else wb
                    seg = seg_of(d, w, c8); n = seg * 4 + c8 % 4
                    psh, pkh = bank(); psc, pkc = bank()
                    pshv = psh[0:64, 0:260].rearrange("p (h n) -> p h n", n=65)
                    cv = c8 if d == 0 else 7 - c8
                    for h in range(4):
                        u, hp, hh = d * 4 + h, h // 2, h % 2
                        cs = slice(u * 64, (u + 1) * 64)
                        mm(psh[0:64, h * 65:(h + 1) * 65], SR[0:64, cs], VAv[d][0:64, cv, h, :], True, False, ["SR", "VA%d" % d], [pkh])
                        mm(psh[0:64, h * 65:(h + 1) * 65], QK[d][hp][:, tsl], CBv[:, u, :], False, True, QKK + ["CB"], [pkh])
                    for h in range(4):
                        u, hp, hh = d * 4 + h, h // 2, h % 2
                        mm(psc[:, h * 65:(h + 1) * 65], KHv[d][hp][0:64, c8, hh, :], VAv[d][0:64, cv, h, :], True, True, ["KH%d%d" % (d, hp), "VA%d" % d], [pkc])
                    dn = DN[0:64, d * 4:(d + 1) * 4]
                    act(dn, pshv[:, :, 64], AF.Abs, [pkh], ["DN%d" % d])
                    tt(dn, dn, CLTv[0:64, n, d * 4:(d + 1) * 4], ALU.max, ["DN%d" % d, "CLT"], ["DN%d" % d])
                    recip(dn, dn, ["DN%d" % d], ["DN%d" % d])
                    tt(HM[0:64, d * 256:(d + 1) * 256].rearrange("p (h v) -> p h v", v=64), pshv[:, :, 0:64],
                       dn.unsqueeze(2).broadcast_to([64, 4, 64]), ALU.mult, [pkh, "DN%d" % d], ["HM%d" % d])
                    tokb = (wf * 512 + c8 * 64) if d == 0 else (wb * 512 + 512 - (c8 + 1) * 64)
                    hdst = hm[d * T + tokb:d * T + tokb + 64, :]
                    if d == 0:
                        dma(hdst, HM[0:64, 0:256], ["HM0"], ["hm"])
                    else:
                        psr, pkr = bank()
                        mm(psr[0:64, 0:256], JF[0:64, 0:64], HM[0:64, 256:512], True, True, ["CONST", "HM1"], [pkr])
                        act(HMR[0:64, :], psr[0:64, 0:256], AF.Identity, [pkr], ["HMR"])
                        dma(hdst, HMR[0:64, :], ["HMR"], ["hm"])
                    tt(TMPCv, CTv[:, d * 4:(d + 1) * 4, :], ALTv[:, n, d * 4:(d + 1) * 4].unsqueeze(2).broadcast_to([128, 4, 65]), ALU.mult, ["CT", "ALT"], ["TMPC"])
                    tt(CTv[:, d * 4:(d + 1) * 4, :], TMPCv, psc[:, 0:260].rearrange("p (h n) -> p h n", n=65), ALU.add, ["TMPC", pkc], ["CT"])
                    if c8 % 4 == 3:
                        tt(TMPCv, CTv[:, d * 4:(d + 1) * 4, :], EMORv[:, seg, d * 4:(d + 1) * 4].unsqueeze(2).broadcast_to([128, 4, 65]), ALU.mult, ["CT", "EMOR"], ["TMPC"])
                        t4_ = TMPC.rearrange("p (q e n) -> p q e n", e=2, n=65)
                        tt(COv, t4_[:, :, 0, :], t4_[:, :, 1, :], ALU.add, ["TMPC"], ["CO"])
                        base = ((l * NSEG + seg) * 2 + d) * 256
                        dma(st_c[base:base + 256, :].rearrange("(q p) n -> p q n", p=128), COv, ["CO"], ["st_c"])
                        segn = seg + 1 if d == 0 else seg - 1
                        if 0 <= segn < NSEG:
                            enter_seg(d, segn, False)
                        else:
                            cp(CBv[:, d * 4:(d + 1) * 4, :], CTv[:, d * 4:(d + 1) * 4, :], ["CT"], ["CB"])
                    else:
                        cp(CBv[:, d * 4:(d + 1) * 4, :], CTv[:, d * 4:(d + 1) * 4, :], ["CT"], ["CB"])

            enter_seg(0, 0, True)
            enter_seg(1, NSEG - 1, True)
            for j in range(8):
                prep(0, j)
                prep(1, 7 - j)
                for c8 in range(8):
                    step(j, 7 - j, c8)
            P.barrier()
            H0, H1 = af(256), af(256); ZOB = ab(256); SGO = af(256); HSQ = af(256); SSM = af(4); HN = ab(256)
            CATD = ab(2 * 512); CATDv = CATD.rearrange("p (j t) -> p j t", j=2)
            for i in range(32):
                r0 = i * 128
                dma(H0, hm[r0:r0 + 128, :], ["hm"], ["H0"]); dma(H1, hm[T + r0:T + r0 + 128, :], ["hm"], ["H1"])
                dma(ZOB, zO[r0:r0 + 128, :], [id(zO)], ["ZOB"])
                tt(H0, H0, H1, ALU.add, ["H0", "H1"], ["H0"])
                act(SGO, ZOB, AF.Sigmoid, ["ZOB"], ["SGO"])
                tt(H0, H0, SGO, ALU.mult, ["H0", "SGO"], ["H0"])
                tt(HSQ, H0, H0, ALU.mult, ["H0"], ["HSQ"])
                P.op("dve", lambda e: e.tensor_reduce(SSM, HSQ.rearrange("p (h v) -> p h v", v=64), AX.X, ALU.add), ["HSQ"], ["SSM"])
                rsqrt(SSM, SSM, 1.0 / 64, EPS, ["SSM"], ["SSM"])
                tt(HN.rearrange("p (h v) -> p h v", v=64), H0.rearrange("p (h v) -> p h v", v=64),
                   SSM.unsqueeze(2).broadcast_to([128, 4, 64]), ALU.mult, ["H0", "SSM"], ["HN"])
                ps, pk = bank(); psb = ps.bitcast(BF16)
                for j in range(2):
                    tr(psb[:, j * 128:(j + 1) * 128], HN[:, j * 128:(j + 1) * 128], IDB, ["HN", "IDB"], [pk])
                for j in range(2):
                    ts(CATDv[:, j, (i % 4) * 128:(i % 4 + 1) * 128], psb[:, j * 128:(j + 1) * 128], col(l, "hng", j), None, ALU.mult, None, [pk, "COLT"], ["CATD"])
                if i % 4 == 3:
                    b = i // 4
                    dma(catT[768:1024, b * 512:(b + 1) * 512].rearrange("(j p) t -> p j t", p=128), CATDv, ["CATD"], [id(catT)])

        def make_norm(XN, XNB, SS, HTv):
            def norm_tile(xt, xk, A_, B_, tcol, hk):
                act(XN, xt, AF.Square, [xk], ["XN", "SS"], accum=SS[:, 0:1])
                rsqrt(SS[:, 1:2], SS[:, 0:1], 1.0 / D, EPS, ["SS"], ["SS1"])
                stt(XN, xt, SS[:, 1:2], A_, ALU.mult, ALU.mult, [xk, "SS1", id(A_)], ["XN"])
                tt(XNB, XN, B_, ALU.add, ["XN", id(B_)], ["XNB"])
                ps, pk = bank()
                psb = ps.bitcast(BF16)
                for k in range(8):
                    tr(psb[:, k * 128:(k + 1) * 128], XNB[:, k * 128:(k + 1) * 128], IDB, ["XNB", "IDB"], [pk])
                act(HTv[:, :, tcol * 128:(tcol + 1) * 128], psb.rearrange("p (k t) -> p k t", k=8), AF.Identity, [pk], [hk])
            return norm_tile

        bc = lambda src: src.partition_broadcast(128)
        for l in range(nlayers):
            aoff["o"] = persist_end
            WM = [af(8 * 512), af(8 * 512)]
            MR = af(512); BM = af(512)
            for cb in range(12):
                wt_ = WM[cb % 2]; wk = "WM%d" % (cb % 2)
                dma(wt_.rearrange("p (k n) -> p k n", k=8),
                    w_mod[l * D:(l + 1) * D, cb * 512:(cb + 1) * 512].rearrange("(k p) n -> p k n", p=128), [], [wk])
                dma(BM[0:1, :], b_mod[l:l + 1, cb * 512:(cb + 1) * 512], [], ["BM"])
                ps, pk = bank()
                for k in range(8):
                    mm(ps[0:1, :], SC[:, k:k + 1], wt_[:, k * 512:(k + 1) * 512], k == 0, k == 7, [wk, "SC"], [pk])
                tt(MR[0:1, :], ps[0:1, :], BM[0:1, :], ALU.add, [pk, "BM"], ["MR"])
                dma(modrow[l:l + 1, cb * 512:(cb + 1) * 512], MR[0:1, :], ["MR"], ["modrow"])
            P.barrier()
            aoff["o"] = persist_end
            A2, B2, G1, G2, GB2 = [af(D) for _ in range(5)]
            lay_end = aoff["o"]
            A1, B1, TMPB = af(D), af(D), af(D)
            for (A_, B_, G_, j0, rp) in ((A1, B1, G1, 0, 0), (A2, B2, G2, 3, 1)):
                dma(B_, bc(modrow[l:l + 1, (j0 + 0) * D:(j0 + 1) * D]), ["modrow"], [id(B_)])
                dma(A_, bc(modrow[l:l + 1, (j0 + 1) * D:(j0 + 2) * D]), ["modrow"], [id(A_)])
                dma(G_, bc(modrow[l:l + 1, (j0 + 2) * D:(j0 + 3) * D]), ["modrow"], [id(G_)])
                dma(TMPB, bc(rowp[l:l + 1, rp * D:(rp + 1) * D]), [], ["TMPB"])
                stt(A_, A_, 1.0, TMPB, ALU.add, ALU.mult, [id(A_), "TMPB"], [id(A_)])
            dma(TMPB, bc(rowp[l:l + 1, 2 * D:3 * D]), [], ["TMPB"])
            tt(GB2, G2, TMPB, ALU.mult, [id(G2), "TMPB"], ["GB2"])

            XT = [af(D), af(D)]; XN = af(D); XNB = ab(D); HT = ab(8 * 512); HTv = HT.rearrange("p (k t) -> p k t", k=8)
            SS = af(4)
            norm_tile = make_norm(XN, XNB, SS, HTv)
            WIN = ab(8 * DIN)
            WINv = WIN.rearrange("p (k n) -> p k n", k=8)
            for k in range(8):
                dma(WINv[:, k, :], w_in[l * D + k * 128: l * D + (k + 1) * 128, :], [], ["WIN"], cast=True)
            ZST = [ab(512) for _ in range(4)]; ZSF = af(512)
            CVB = [ab(8192), ab(8192)]
            xsrc = x_in if l == 0 else xs
            ncv = 0
            for b in range(NB):
                cv = CVB[ncv % 2]; ck_ = "CVB%d" % (ncv % 2); ncv += 1
                dma(cv[:, 0:4096].rearrange("p (k n) -> p k n", k=8), w1[l * D:(l + 1) * D, b * 512:(b + 1) * 512].rearrange("(k p) n -> p k n", p=128),
                    [], [ck_], cast=True)
                dma(W1B[b * 128:(b + 1) * 128, :], cv[:, 0:4096], [ck_], ["W1B"], q="pool")
                if b % 2 == 0:
                    p2 = b // 2
                    cv = CVB[ncv % 2]; ck_ = "CVB%d" % (ncv % 2); ncv += 1
                    dma(cv.rearrange("p (f n) -> p f n", f=32), w2[l * DFF:(l + 1) * DFF, p2 * 256:(p2 + 1) * 256].rearrange("(f p) n -> p f n", p=128),
                        [], [ck_], cast=True)
                    dma(W2B[p2 * 128:(p2 + 1) * 128, :], cv, [ck_], ["W2B"], q="pool")
                for t4 in range(4):
                    xt = XT[t4 % 2]; xk = "XT%d" % (t4 % 2)
                    r0 = b * 512 + t4 * 128
                    dma(xt, xsrc[r0:r0 + 128, :], ["xs"], [xk])
                    norm_tile(xt, xk, A1, B1, t4, "HT")
                zi = 0
                for (c0, ntile, dst) in ((C_A, 8, zAT), (C_C, 4, zCT), (C_QK, 4, zQKT)):
                    for ct in range(ntile):
                        ps, pk = bank()
                        for k in range(8):
                            mm(ps, WINv[:, k, c0 + ct * 128:c0 + (ct + 1) * 128], HTv[:, k, :], k == 0, k == 7, ["WIN", "HT"], [pk])
                        zs = ZST[zi % 4]; zk = "ZST%d" % (zi % 4); zi += 1
                        act(zs, ps, AF.Identity, [pk], [zk])
                        dma(dst[ct * 128:(ct + 1) * 128, b * 512:(b + 1) * 512], zs, [zk], [id(dst)])
                ps, pk = bank()
                for k in range(8):
                    mm(ps[0:16, :], WINv[:, k, C_I:C_I + 16], HTv[:, k, :], k == 0, k == 7, ["WIN", "HT"], [pk])
                cp(ZSF[0:16, :], ps[0:16, :], [pk], ["ZSF"])
                dma(zG[:, b * 512:(b + 1) * 512], ZSF[0:16, :], ["ZSF"], ["zG"])
                for t4 in range(4):
                    r0 = b * 512 + t4 * 128
                    for (c0, dst) in ((C_B, zB), (C_V, zV), (C_O, zO)):
                        ps, pk = bank()
                        for k in range(8):
                            mm(ps[:, 0:256], HTv[:, k, t4 * 128:(t4 + 1) * 128], WINv[:, k, c0:c0 + 256], k == 0, k == 7, ["WIN", "HT"], [pk])
                        zs = ZST[zi % 4]; zk = "ZST%d" % (zi % 4); zi += 1
                        act(zs[:, 0:256], ps[:, 0:256], AF.Identity, [pk], [zk])
                        dma(dst[r0:r0 + 128, :], zs[:, 0:256], [zk], [id(dst)])
            P.barrier()

            for nm_, fn_ in (("pool", pool_phase), ("conv", conv_phase), ("rwkv", rwkv_phase), ("mlstm", mlstm_phase)):
                if nm_ in MIX:
                    aoff["o"] = lay_end
                    fn_(l)
                    P.barrier()

            aoff["o"] = lay_end
            XT = [af(D), af(D)]; XN = af(D); XNB = ab(D); HT = ab(8 * 512); HTv = HT.rearrange("p (k t) -> p k t", k=8)
            SS = af(4)
            norm_tile = make_norm(XN, XNB, SS, HTv)
            WOUT = ab(8 * D); WOUTv = WOUT.rearrange("p (k n) -> p k n", k=8)
            for k in range(8):
                dma(WOUTv[:, k, :], w_out[l * D + k * 128: l * D + (k + 1) * 128, :], [], ["WOUT"], cast=True)
            W1P = [ab(8 * 512), ab(8 * 512)]; W2P = [ab(32 * 256), ab(32 * 256)]
            CATT = ab(8 * 512); CATTv = CATT.rearrange("p (k t) -> p k t", k=8)
            X1 = af(4 * D); X1v = X1.rearrange("p (t n) -> p t n", t=4)
            HID = ab(32 * 512); HIDv = HID.rearrange("p (f t) -> p f t", f=32)
            RL = [af(512), af(512)]; TM3 = [af(512), af(512)]
            FGT = None
            if l == nlayers - 1:
                FGT = af(D)
                dma(FGT, bc(rowp[l:l + 1, 3 * D:4 * D]), [], ["FGT"])
            for b in range(NB):
                dma(CATTv, catT[:, b * 512:(b + 1) * 512].rearrange("(k p) t -> p k t", p=128), [id(catT)], ["CATT"])
                for t4 in range(4):
                    xt = XT[t4 % 2]; xk = "XT%d" % (t4 % 2)
                    r0 = b * 512 + t4 * 128
                    dma(xt, xsrc[r0:r0 + 128, :], ["xs"], [xk])
                    for half in range(2):
                        ps, pk = bank()
                        for k in range(8):
                            mm(ps, CATTv[:, k, t4 * 128:(t4 + 1) * 128], WOUTv[:, k, half * 512:(half + 1) * 512], k == 0, k == 7, ["CATT", "WOUT"], [pk])
                        tm = TM3[half]; tk = "TM3%d" % half
                        tt(tm, ps, G1[:, half * 512:(half + 1) * 512], ALU.mult, [pk, id(G1)], [tk])
                        tt(X1v[:, t4, half * 512:(half + 1) * 512], tm, xt[:, half * 512:(half + 1) * 512], ALU.add, [tk, xk], ["X1_%d" % t4])
                    norm_tile(X1v[:, t4, :], "X1_%d" % t4, A2, B2, t4, "HT")
                    tt(X1v[:, t4, :], X1v[:, t4, :], GB2, ALU.add, ["X1_%d" % t4, "GB2"], ["X1_%d" % t4])
                for p1 in range(8):
                    wp = W1P[p1 % 2]; wk = "W1P%d" % (p1 % 2); wpv = wp.rearrange("p (k n) -> p k n", k=8)
                    dma(wp, W1B[p1 * 128:(p1 + 1) * 128, :], ["W1B"], [wk])
                    for ft in range(4):
                        f = p1 * 4 + ft
                        ps, pk = bank()
                        for k in range(8):
                            mm(ps, wpv[:, k, ft * 128:(ft + 1) * 128], HTv[:, k, :], k == 0, k == 7, [wk, "HT"], [pk])
                        rl = RL[f % 2]; rk = "RL%d" % (f % 2)
                        act(rl, ps, AF.Relu, [pk, "COLT"], [rk], bias=col(l, "b1", f))
                        tt(HIDv[:, f, :], rl, rl, ALU.mult, [rk], ["HID"], eng="pool" if f % 2 else "dve")
                for p2 in range(4):
                    wp = W2P[p2 % 2]; wk = "W2P%d" % (p2 % 2); wpv = wp.rearrange("p (f n) -> p f n", f=32)
                    dma(wp, W2B[p2 * 128:(p2 + 1) * 128, :], ["W2B"], [wk])
                    for t4 in range(4):
                        ps, pk = bank()
                        for f in range(32):
                            mm(ps[:, 0:256], HIDv[:, f, t4 * 128:(t4 + 1) * 128], wpv[:, f, :], f == 0, f == 31, ["HID", wk], [pk])
                        tm = TM3[t4 % 2]; tk = "TM3%d" % (t4 % 2)
                        tt(tm[:, 0:256], ps[:, 0:256], G2[:, p2 * 256:(p2 + 1) * 256], ALU.mult, [pk, id(G2)], [tk])
                        tt(X1v[:, t4, p2 * 256:(p2 + 1) * 256], tm[:, 0:256], X1v[:, t4, p2 * 256:(p2 + 1) * 256], ALU.add, [tk, "X1_%d" % t4], ["X1_%d" % t4])
                for t4 in range(4):
                    r0 = b * 512 + t4 * 128
                    xk = "X1_%d" % t4
                    if l < nlayers - 1:
                        dma(xs[r0:r0 + 128, :], X1v[:, t4, :], [xk], ["xs"])
                    else:
                        act(XN, X1v[:, t4, :], AF.Square, [xk], ["XN", "SS"], accum=SS[:, 0:1])
                        rsqrt(SS[:, 1:2], SS[:, 0:1], 1.0 / D, EPS, ["SS"], ["SS1"])
                        stt(XN, X1v[:, t4, :], SS[:, 1:2], FGT, ALU.mult, ALU.mult, [xk, "SS1", "FGT"], ["XN"])
                        dma(y_out[r0:r0 + 128, :], XN, ["XN"], ["yout"])
            P.barrier()

        print('arena peak', aoff.get('max'), 'of', ARENA)
        P.barrier(["sp"])

        @block.tensor
        def _(e):
            P.emit("pe", e)

        @block.scalar
        def _(e):
            P.emit("act", e)

        @block.vector
        def _(e):
            P.emit("dve", e)

        @block.gpsimd
        def _(e):
            P.emit("pool", e)

        @block.sync
        def _(e):
            P.emit("sp", e)
    return nc


MIX = ("pool", "conv", "rwkv", "mlstm")
_NC_CACHE = {}


def kernel(**inp):
    inp = {k: np.asarray(v) for k, v in inp.items()}
    shared = _shared_consts(inp)
    xp = inp["x_prompt"].astype(np.float32); xsm = inp["x_sample"].astype(np.float32)
    cores = []
    for i in range(2):
        cores.append(_prep_core("p", xp[i * 16:(i + 1) * 16], inp["c_ctx"], inp))
    for b in range(2):
        cores.append(_prep_core("s", xsm[b], inp["c"][b], inp, inp["state_rwkv"][b], inp["state_mlstm_C"][b],
                                inp["state_mlstm_n"][b], inp["state_mlstm_m"][b]))
    in_maps = []
    for i in range(8):
        m = dict(shared); m.update(cores[i % 4]); in_maps.append(m)
    if "nc" not in _NC_CACHE:
        _NC_CACHE["nc"] = build()
    res = run_bass_kernel_spmd(_NC_CACHE["nc"], in_maps, core_ids=list(range(8)))
    R = res.results
    y_prompt = np.concatenate([R[0]["y"], R[1]["y"]], 0).reshape(32, 256, D).astype(np.float32)
    y_sample = np.stack([R[2]["y"], R[3]["y"]], 0).reshape(2, T, D).astype(np.float32)
    nr = np.zeros((32, L, 2, 4, 64, 64), np.float32); ncc = np.zeros((32, L, 2, 4, 64, 64), np.float32)
    nn = np.zeros((32, L, 2, 4, 64), np.float32); nm = np.zeros((32, L, 2, 4), np.float32)
    for i in range(2):
        sr = R[i]["st_r"].reshape(L, NSEG, 2, 4, 64, 64)
        scc = R[i]["st_c"].reshape(L, NSEG, 2, 4, 64, 65)
        smm = R[i]["st_m"].reshape(L, 2, 4, NSEG)
        nr[i * 16:(i + 1) * 16] = sr.transpose(1, 0, 2, 3, 5, 4)
        ncc[i * 16:(i + 1) * 16] = scc[..., :64].transpose(1, 0, 2, 3, 5, 4)
        nn[i * 16:(i + 1) * 16] = scc[..., 64].transpose(1, 0, 2, 3, 4)
        nm[i * 16:(i + 1) * 16] = smm.transpose(3, 0, 1, 2)
    return (y_prompt, y_sample, nr, ncc, nn, nm)
```

```python
import contextlib
import math
import numpy as np
import ml_dtypes
import concourse.bass as bass
import concourse.mybir as mybir
from concourse.bass_utils import run_bass_kernel_spmd

F32, BF16 = mybir.dt.float32, mybir.dt.bfloat16
AF = mybir.ActivationFunctionType
ALU = mybir.AluOpType
AX = mybir.AxisListType

D = 1024
L = 4
T = 4096
NSEG = 16
DIN = 2832
DFF = 4096
NB = 8
ARENA = 52000
DEBUG = False
RW_STAGE = 3
RW_CUT = 9
EPS = 1e-6
RWKV_LN_EPS = 64e-5
CONV_LN_EPS = 1e-5
POOL_WINS = (2, 4, 8, 16)
POOL_DELTAS = {0: (-1, 0, 1), 1: (-1, 0, 1), 2: (-2, -1, 0, 1, 2), 3: (-4, -3, -2, -1, 0, 1, 2, 3, 4)}

C_A, C_B, C_C, C_QK, C_V, C_I, C_F, C_O = 0, 1024, 1280, 1792, 2304, 2560, 2568, 2576

COLS = {}
_off = 0
for _n, _w in [("mu", 8), ("w0", 4), ("a0", 4), ("kk", 2), ("ka", 2), ("rk", 2), ("lng", 2), ("lnb", 2),
               ("pscale", 2), ("convb", 2), ("clng", 2), ("clnb", 2), ("cdw", 62), ("qkc", 12), ("hng", 2),
               ("b1", 32)]:
    COLS[_n] = _off
    _off += _w
NCOL = _off


class Prog:
    def __init__(self, esem, dsems):
        self.q = {e: [] for e in ("pe", "act", "dve", "pool", "sp")}
        self.cnt = dict.fromkeys(self.q, 0)
        self.esem, self.dsems = esem, dsems
        self.duse = {e: [0] * len(dsems[e]) for e in dsems}
        self.dnext = {e: 0 for e in dsems}
        self.lastw, self.readers = {}, {}
        self.seen = {e: {} for e in self.q}
        self.alltok = {}

    def sem(self, sid):
        return self.esem[sid[1]] if sid[0] == "e" else self.dsems[sid[1]][sid[2]]

    def op(self, eng, fn, reads=(), writes=(), dma=False):
        need = {}

        def add(tok):
            if tok is not None and need.get(tok[0], 0) < tok[1]:
                need[tok[0]] = tok[1]

        for k in reads:
            add(self.lastw.get(k))
        for k in writes:
            add(self.lastw.get(k))
            for sid, val in self.readers.get(k, {}).items():
                add((sid, val))
        if dma:
            i = self.dnext[eng]
            self.dnext[eng] = (i + 1) % len(self.dsems[eng])
            sid = ("d", eng, i)
            if self.duse[eng][i] > 0:
                need[sid] = max(need.get(sid, 0), 16 * self.duse[eng][i])
            self.duse[eng][i] += 1
            tok = (sid, 16 * self.duse[eng][i])
        else:
            self.cnt[eng] += 1
            tok = (("e", eng), self.cnt[eng])
        waits = []
        for sid, val in need.items():
            if sid == ("e", "pe") and eng == "pe":
                continue
            if self.seen[eng].get(sid, 0) >= val:
                continue
            self.seen[eng][sid] = val
            waits.append((sid, val))
        self.q[eng].append((waits, fn, tok))
        for k in reads:
            r = self.readers.setdefault(k, {})
            if r.get(tok[0], 0) < tok[1]:
                r[tok[0]] = tok[1]
        for k in writes:
            self.lastw[k] = tok
            self.readers[k] = {}
        self.alltok[tok[0]] = tok[1]

    def barrier(self, engines=None):
        for eng in (engines or self.q):
            waits = []
            for sid, val in self.alltok.items():
                if self.seen[eng].get(sid, 0) >= val:
                    continue
                self.seen[eng][sid] = val
                waits.append((sid, val))
            self.q[eng].append((waits, None, None))

    def emit(self, eng, e):
        for waits, fn, tok in self.q[eng]:
            for sid, val in waits:
                e.wait_ge(self.sem(sid), val)
            if fn is not None:
                ins = fn(e)
                ins.then_inc(self.sem(tok[0]), 16 if tok[0][0] == "d" else 1)


def _colify(v):
    v = np.asarray(v, np.float32).reshape(-1, 128)
    return np.ascontiguousarray(v.T)


def _window_bounds(n, win):
    t = np.arange(n)
    return np.clip(t - win // 2, 0, n), np.clip(t + win // 2, 0, n)


def _pool_consts(grid):
    mats = np.zeros((4, 9, 2, 128, 128), np.float32)
    inv = np.zeros((128, 32, 4), np.float32)
    for g, win in enumerate(POOL_WINS):
        if grid:
            rlo, rhi = _window_bounds(64, win)
            clo, chi = _window_bounds(64, win)
            Mfull = None
            R = np.zeros((64, 64), np.float32)
            for r in range(64):
                R[r, rlo[r]:rhi[r]] = 1
            Cm = np.zeros((64, 64), np.float32)
            for c in range(64):
                Cm[c, clo[c]:chi[c]] = 1
            Mfull = np.kron(R, Cm)
            cnt = Mfull.sum(1)
        else:
            lo, hi = _window_bounds(256, win)
            M1 = np.zeros((256, 256), np.float32)
            for t in range(256):
                M1[t, lo[t]:hi[t]] = 1
            Mfull = np.kron(np.eye(16, dtype=np.float32), M1)
            cnt = Mfull.sum(1)
        inv[:, :, g] = (1.0 / cnt).reshape(32, 128).T
        for di, dl in enumerate(POOL_DELTAS[g]):
            for par in range(2):
                acc = None
                for i in range(par, 32, 2):
                    j = i + dl
                    if j < 0 or j >= 32:
                        continue
                    blk = Mfull[i * 128:(i + 1) * 128, j * 128:(j + 1) * 128]
                    if acc is None:
                        acc = blk
                    else:
                        assert np.array_equal(acc, blk), (g, dl, par, i)
                if acc is not None:
                    mats[g, di, par] = acc.T
    return mats, inv


def _prep_core(kind, xs, cvec, inp, sr=None, sc=None, sn=None, sm=None):
    m = {}
    m["x"] = np.ascontiguousarray(xs.reshape(T, D), np.float32)
    m["cvcol"] = _colify(cvec)
    cmv = 0.0 if kind == "p" else 1.0
    m["cm"] = np.full((128, 1), cmv, np.float32)
    mp = np.ones((128, 512), np.float32)
    mn = np.ones((128, 512), np.float32)
    if kind == "p":
        mp[:, 0] = 0; mp[:, 256] = 0
        mn[:, 255] = 0; mn[:, 511] = 0
    m["tsm"] = np.stack([mp, mn], 1).reshape(128, 1024)
    mats, inv = _pool_consts(kind == "s")
    m["poolm"] = np.ascontiguousarray(mats.transpose(3, 0, 1, 2, 4).reshape(128, 4 * 9 * 2 * 128))
    m["poolinv"] = np.ascontiguousarray(inv.reshape(128, 128))
    s0r = np.zeros((L, NSEG, 2, 128, 4, 64), np.float32)
    s0c = np.zeros((L, NSEG, 2, 128, 4, 65), np.float32)
    s0m = np.zeros((L, 2, 4, NSEG), np.float32)
    if kind == "s":
        for l in range(L):
            for d in range(2):
                seg = 0 if d == 0 else NSEG - 1
                for h in range(4):
                    hh = h % 2
                    s0r[l, seg, d, hh * 64:(hh + 1) * 64, h, :] = sr[l, d, h].T
                    s0c[l, seg, d, hh * 64:(hh + 1) * 64, h, :64] = sc[l, d, h].T
                    s0c[l, seg, d, hh * 64:(hh + 1) * 64, h, 64] = sn[l, d, h]
                    s0m[l, d, h, seg] = sm[l, d, h]
    m["s0r"] = s0r.reshape(L * NSEG * 2 * 128, 256)
    m["s0c"] = s0c.reshape(L * NSEG * 2 * 128, 260)
    m["s0m"] = s0m.reshape(L * 8, NSEG)
    return m


def _shared_consts(inp):
    m = {}
    f = lambda a: np.ascontiguousarray(a, np.float32)
    for k in ("w_mod", "w_in", "w_out", "mlp_w1", "mlp_w2", "conv_pw", "rwkv_g_up"):
        m[k] = f(inp[k]).reshape(-1, inp[k].shape[-1])
    m["b_mod"] = f(inp["b_mod"])
    m["rowp"] = f(np.concatenate([inp["norm1_g"], inp["norm2_g"], inp["mlp_b2"],
                                  np.broadcast_to(inp["final_g"], (L, D))], 1))
    cols = np.zeros((128, L, NCOL), np.float32)
    for l in range(L):
        def put(name, v):
            c = _colify(v)
            cols[:, l, COLS[name]:COLS[name] + c.shape[1]] = c
        put("mu", inp["rwkv_mu"][l])
        put("w0", inp["rwkv_w0"][l].reshape(-1))
        put("a0", inp["rwkv_a0"][l].reshape(-1))
        put("kk", inp["rwkv_k_k"][l]); put("ka", inp["rwkv_k_a"][l]); put("rk", inp["rwkv_r_k"][l].reshape(-1))
        put("lng", inp["rwkv_ln_g"][l]); put("lnb", inp["rwkv_ln_b"][l])
        put("pscale", inp["pool_scale"][l]); put("convb", inp["conv_b"][l])
        put("clng", inp["conv_ln_g"][l]); put("clnb", inp["conv_ln_b"][l])
        put("cdw", inp["conv_dw"][l].reshape(-1))
        put("qkc", inp["mlstm_qk_conv"][l].reshape(-1))
        put("hng", inp["mlstm_hn_g"][l]); put("b1", inp["mlp_b1"][l])
    m["cols"] = cols.reshape(128, L * NCOL)
    gb = np.zeros((128, L, 2), np.float32)
    for l in range(L):
        for d in range(2):
            for h in range(4):
                gb[d * 64 + h * 16:(d * 64 + h * 16 + 16), l, 0] = inp["mlstm_i_bias"][l, d * 4 + h]
                gb[d * 64 + h * 16:(d * 64 + h * 16 + 16), l, 1] = inp["mlstm_f_bias"][l, d * 4 + h]
    m["gbias"] = gb.reshape(128, L * 2)
    wa = np.concatenate([inp["rwkv_w_up"], inp["rwkv_a_up"]], 2)
    m["waup"] = f(wa).reshape(L * 2 * 128, 256)
    pw = np.zeros((L, 2, 128, 128), np.float32)
    for l in range(L):
        for g in range(4):
            p, o = g // 2, (g % 2) * 64
            pw[l, p, o:o + 64, o:o + 64] = inp["pool_w"][l, g]
    m["poolw"] = pw.reshape(L * 2 * 128, 128)
    c = {}
    c["ident"] = np.eye(128, dtype=np.float32)
    bo = np.zeros((128, 128), np.float32); bo[:64, :64] = 1; bo[64:, 64:] = 1
    c["bones"] = bo
    c["ones"] = np.ones((128, 128), np.float32)
    s = np.arange(64)
    ms = (s[:, None] < s[None, :]).astype(np.float32)
    mi = (s[:, None] <= s[None, :]).astype(np.float32)
    ml = (s[:, None] > s[None, :]).astype(np.float32)
    pad = lambda a: np.concatenate([np.tile(a, (1, 8)), np.zeros((64, 512), np.float32)], 0)
    blk = (s[:, None] // 16 == s[None, :] // 16).astype(np.float32)
    c["ms"] = pad(ms); c["mi"] = pad(mi); c["ml"] = pad(ml * blk)
    c["msd"] = pad(ms * blk); c["mll"] = pad(ml * (1 - blk))
    c["i8"] = pad(np.eye(64, dtype=np.float32))
    rm = np.ones((128, 512), np.float32); rm[:, ::64] = 0
    c["rmask"] = rm
    rn = np.zeros((128, 512), np.float32); rn[:, ::64] = -1e30
    c["rneg"] = rn
    pk = np.zeros((128, 16, 8), np.float32)
    for d in range(2):
        for h in range(4):
            for sg in range(16):
                pk[d * 64 + h * 16 + sg, sg, d * 4 + h] = 1
    c["pick"] = pk.reshape(128, 128)
    c["mi8"] = c["mi"] * 0.125
    p2 = np.zeros((128, 64), np.float32)
    for d in range(2):
        for h in range(4):
            p2[d * 32 + h, d * 4 + h] = 1
    c["pk2"] = p2
    jr = np.zeros((128, 64), np.float32)
    jr[np.arange(64), 63 - np.arange(64)] = 1
    c["jrev"] = jr
    order = ["ident", "bones", "ones", "ms", "mi", "ml", "i8", "rmask", "rneg", "pick", "mi8", "pk2", "jrev", "msd", "mll"]
    m["consts"] = np.concatenate([c[k] for k in order], 1)
    return m


CONST_OFF = {}
_o = 0
for _k, _w in [("ident", 128), ("bones", 128), ("ones", 128), ("ms", 512), ("mi", 512), ("ml", 512), ("i8", 512), ("rmask", 512),
               ("rneg", 512), ("pick", 128), ("mi8", 512), ("pk2", 64), ("jrev", 64), ("msd", 512), ("mll", 512)]:
    CONST_OFF[_k] = (_o, _w)
    _o += _w
NCONST = _o


def build(nlayers=L):
    nc = bass.Bass("TRN2", target_bir_lowering=False)
    din = lambda n, s, dt=F32: nc.dram_tensor(n, list(s), dt, kind="ExternalInput").ap()
    dout = lambda n, s, dt=F32: nc.dram_tensor(n, list(s), dt, kind="ExternalOutput").ap()
    dscr = lambda n, s, dt=F32: nc.dram_tensor(n, list(s), dt, kind="ExternalOutput" if DEBUG else "Internal").ap()
    x_in = din("x", [T, D]); cvcol = din("cvcol", [128, 8]); cm_d = din("cm", [128, 1])
    tsm_d = din("tsm", [128, 1024]); poolm_d = din("poolm", [128, 4 * 9 * 2 * 128]); poolinv_d = din("poolinv", [128, 128])
    s0r_d = din("s0r", [L * NSEG * 2 * 128, 256]); s0c_d = din("s0c", [L * NSEG * 2 * 128, 260]); s0m_d = din("s0m", [L * 8, NSEG])
    w_mod = din("w_mod", [L * D, 6 * D]); b_mod = din("b_mod", [L, 6 * D])
    w_in = din("w_in", [L * D, DIN]); w_out = din("w_out", [L * D, D])
    w1 = din("mlp_w1", [L * D, DFF]); w2 = din("mlp_w2", [L * DFF, D])
    conv_pw = din("conv_pw", [L * 256, 256]); g_up = din("rwkv_g_up", [L * 128, 256])
    rowp = din("rowp", [L, 4 * D]); cols_d = din("cols", [128, L * NCOL]); gbias_d = din("gbias", [128, L * 2])
    waup_d = din("waup", [L * 2 * 128, 256]); poolw_d = din("poolw", [L * 2 * 128, 128]); consts_d = din("consts", [128, NCONST])
    y_out = dout("y", [T, D]); st_r = dout("st_r", [L * NSEG * 2 * 2 * 128, 64])
    st_c = dout("st_c", [L * NSEG * 2 * 2 * 128, 65]); st_m = dout("st_m", [L * 8, NSEG])
    modrow = dscr("modrow", [L, 6 * D]); xs = dscr("xs", [T, D])
    zAT = dscr("zAT", [1024, T], BF16); zB = dscr("zB", [T, 256], BF16); zCT = dscr("zCT", [512, T], BF16)
    zQKT = dscr("zQKT", [512, T], BF16); zV = dscr("zV", [T, 256], BF16); zO = dscr("zO", [T, 256], BF16)
    zG = dscr("zG", [16, T]); yT = dscr("yT", [2 * 256, T]); bonT = dscr("bonT", [2 * 256, T])
    W1B = dscr("W1B", [8 * 128, 4096], BF16); W2B = dscr("W2B", [4 * 128, 8192], BF16)
    catT = dscr("catT", [1024, T], BF16); hm = dscr("hm", [2 * T, 256]); gtab = dscr("gtab", [4 * 128, 256])

    es = contextlib.ExitStack()
    with es:
        arena = es.enter_context(nc.sbuf_tensor("arena", [128, ARENA], F32))
        psall = es.enter_context(nc.psum_tensor("psall", [128, 4096], F32))
        esem = {e: es.enter_context(nc.semaphore("e_" + e)) for e in ("pe", "act", "dve", "pool", "sp")}
        dsems = {e: [es.enter_context(nc.semaphore("d_%s%d" % (e, i))) for i in range(12)] for e in ("sp", "pool")}
        block = es.enter_context(nc.Block())
        P = Prog(esem, dsems)
        PS = [psall[:, i * 512:(i + 1) * 512] for i in range(8)]
        PSK = ["ps%d" % i for i in range(8)]
        pstate = {"i": 0}

        def bank():
            i = pstate["i"]; pstate["i"] = (i + 1) % 8
            return PS[i], PSK[i]

        aoff = {"o": 0}

        def af(n):
            o = aoff["o"]; aoff["o"] = o + n
            aoff["max"] = max(aoff.get("max", 0), aoff["o"])
            assert aoff["o"] <= ARENA, aoff["o"]
            return arena[:, o:o + n]

        def ab(n):
            n32 = (n + 1) // 2
            return af(n32).bitcast(BF16)[:, 0:n]

        def mm(out, lhsT, rhs, st, sp_, r, w):
            P.op("pe", lambda e: e.matmul(out, lhsT, rhs, start=st, stop=sp_), r, w)

        def tr(out, in_, ident, r, w):
            P.op("pe", lambda e: e.transpose(out, in_, ident), r, w)

        def act(out, in_, func, r, w, bias=None, scale=None, accum=None):
            kw = {}
            if bias is not None: kw["bias"] = bias
            if scale is not None: kw["scale"] = scale
            if accum is not None: kw["accum_out"] = accum
            P.op("act", lambda e: e.activation(out=out, in_=in_, func=func, **kw), r, w)

        def tt(out, a, b, op, r, w, eng="dve"):
            P.op(eng, lambda e: e.tensor_tensor(out=out, in0=a, in1=b, op=op), r, w)

        def ts(out, a, s1, s2, op0, op1, r, w, eng="dve"):
            if op1 is None:
                P.op(eng, lambda e: e.tensor_scalar(out, a, s1, None, op0), r, w)
            else:
                P.op(eng, lambda e: e.tensor_scalar(out, a, s1, s2, op0, op1), r, w)

        def stt(out, a, s, b, op0, op1, r, w):
            P.op("dve", lambda e: e.scalar_tensor_tensor(out=out, in0=a, scalar=s, in1=b, op0=op0, op1=op1), r, w)

        def cp(out, in_, r, w, eng="dve"):
            P.op(eng, lambda e: e.tensor_copy(out, in_), r, w)

        def recip(out, in_, r, w):
            P.op("dve", lambda e: e.reciprocal(out, in_), r, w)

        def scan(out, d0, d1, init, op0, op1, r, w):
            P.op("dve", lambda e: e.tensor_tensor_scan(out=out, data0=d0, data1=d1, initial=init, op0=op0, op1=op1), r, w)

        def mset(ap, v, w, eng="dve"):
            P.op(eng, lambda e: e.memset(ap, v), (), w)

        def dma(out, in_, r, w, cast=False, q=None):
            eng = q or ("pool" if cast else "sp")
            P.op(eng, lambda e: e.dma_start(out=out, in_=in_), r, w, dma=True)

        def rsqrt(out, in_, mul, eps, r, w):
            ts(out, in_, mul, eps, ALU.mult, ALU.add, r, w)
            act(out, out, AF.Sqrt, w, w)
            recip(out, out, w, w)

        CONST = af(NCONST); COLT = af(L * NCOL); CM = af(1); GB = af(L * 2); SC = af(8)
        IDB = ab(128); TSM = af(1024)
        ck = lambda n: CONST[:, CONST_OFF[n][0]:CONST_OFF[n][0] + CONST_OFF[n][1]]
        IDF, BONES, ONES = ck("ident"), ck("bones"), ck("ones")
        MS, MI, ML, I8, RMASK, RNEG, PICK = ck("ms"), ck("mi"), ck("ml"), ck("i8"), ck("rmask"), ck("rneg"), ck("pick")
        MI8, PK2, JF = ck("mi8"), ck("pk2"), ck("jrev")
        MSD, MLL, MLD = ck("msd"), ck("mll"), ML
        JB = ab(64)
        PICKv = PICK.rearrange("p (s j) -> p s j", j=8)
        ONEC = af(1); LN8C = af(1)
        dma(CONST, consts_d[:, :], [], ["CONST"]); dma(COLT, cols_d[:, :], [], ["COLT"]); dma(CM, cm_d[:, :], [], ["CM"])
        dma(GB, gbias_d[:, :], [], ["GB"]); dma(SC, cvcol[:, :], [], ["SC"]); dma(TSM, tsm_d[:, :], [], ["TSM"])
        cp(IDB, IDF, ["CONST"], ["IDB"])
        cp(JB[0:64, :], JF[0:64, 0:64], ["CONST"], ["JB"])
        mset(ONEC, 1.0, ["ONEC"]); mset(LN8C, -math.log(8.0), ["LN8C"])
        act(SC, SC, AF.Silu, ["SC"], ["SC"])
        col = lambda l, name, j=0: COLT[:, l * NCOL + COLS[name] + j: l * NCOL + COLS[name] + j + 1]
        persist_end = aoff["o"]

        def pool_phase(l):
            POOLM = ab(4 * 9 * 2 * 128); PINV = af(128)
            dma(POOLM, poolm_d[:, :], [], ["POOLM"], cast=True)
            dma(PINV, poolinv_d[:, :], [], ["PINV"])
            PWB = ab(2 * 128); PWBv = PWB.rearrange("p (j n) -> p j n", j=2)
            for j in range(2):
                dma(PWBv[:, j, :], poolw_d[(l * 2 + j) * 128:(l * 2 + j + 1) * 128, :], [], ["PWB"], cast=True)
            ZBT = ab(32 * 256); ZBTv = ZBT.rearrange("p (i c) -> p i c", i=32)
            dma(ZBTv, zB[:, :].rearrange("(i p) c -> p i c", p=128), [id(zB)], ["ZBT"])
            DIFF = ab(256); DT = ab(2 * 512); DTv = DT.rearrange("p (j t) -> p j t", j=2)
            YB = ab(2 * 512); YBv = YB.rearrange("p (j t) -> p j t", j=2)
            pm = lambda g, di, par: POOLM[:, ((g * 9 + di) * 2 + par) * 128:((g * 9 + di) * 2 + par + 1) * 128]
            for i in range(32):
                ps, pk = bank()
                for g in range(4):
                    dls = [(di, dl) for di, dl in enumerate(POOL_DELTAS[g]) if 0 <= i + dl < 32]
                    for n, (di, dl) in enumerate(dls):
                        mm(ps[:, g * 64:(g + 1) * 64], pm(g, di, i % 2), ZBTv[:, i + dl, g * 64:(g + 1) * 64],
                           n == 0, n == len(dls) - 1, ["POOLM", "ZBT"], [pk])
                for g in range(4):
                    stt(DIFF[:, g * 64:(g + 1) * 64], ps[:, g * 64:(g + 1) * 64], PINV[:, i * 4 + g:i * 4 + g + 1],
                        ZBTv[:, i, g * 64:(g + 1) * 64], ALU.mult, ALU.subtract, [pk, "PINV", "ZBT"], ["DIFF"])
                ps2, pk2 = bank(); psb = ps2.bitcast(BF16)
                for j in range(2):
                    tr(psb[:, j * 128:(j + 1) * 128], DIFF[:, j * 128:(j + 1) * 128], IDB, ["DIFF", "IDB"], [pk2])
                act(DTv[:, :, (i % 4) * 128:(i % 4 + 1) * 128], psb[:, 0:256].rearrange("p (j t) -> p j t", j=2), AF.Identity, [pk2], ["DT"])
                if i % 4 == 3:
                    b = i // 4
                    for j in range(2):
                        ps3, pk3 = bank()
                        mm(ps3, PWBv[:, j, :], DTv[:, j, :], True, True, ["PWB", "DT"], [pk3])
                        ts(YBv[:, j, :], ps3, col(l, "pscale", j), None, ALU.mult, None, [pk3, "COLT"], ["YB"])
                    dma(catT[256:512, b * 512:(b + 1) * 512].rearrange("(j p) t -> p j t", p=128), YBv, ["YB"], [id(catT)])

        def conv_phase(l):
            DIAG = ab(2 * 31 * 128); DIAGv = DIAG.rearrange("p (i k n) -> p i k n", i=2, k=31)
            for ti in range(2):
                for k in range(31):
                    ts(DIAGv[:, ti, k, :], IDF, col(l, "cdw", k * 2 + ti), None, ALU.mult, None, ["CONST", "COLT"], ["DIAG"])
            PWC = ab(2 * 256); PWCv = PWC.rearrange("p (k n) -> p k n", k=2)
            dma(PWCv, conv_pw[l * 256:(l + 1) * 256, :].rearrange("(k p) n -> p k n", p=128), [], ["PWC"], cast=True)
            RAWC = ab(4 * 286); RAWCv = RAWC.rearrange("p (j t) -> p j t", j=4)
            SG = af(2 * 286); SGv = SG.rearrange("p (j t) -> p j t", j=2)
            U = ab(2 * 286); Uv = U.rearrange("p (j t) -> p j t", j=2)
            v2 = lambda a: a.rearrange("p (j t) -> p j t", j=2)
            UC = af(512); CEN = af(512); SQ = af(512); RSTD = af(256); UN = af(512); UNB = ab(512); YC = ab(512)
            UCv, CENv, SQv, UNv, UNBv, YCv = v2(UC), v2(CEN), v2(SQ), v2(UN), v2(UNB), v2(YC)
            for sg in range(16):
                lo, hi = sg * 256 - 15, sg * 256 + 271
                clo, chi = max(lo, 0), min(hi, T)
                if clo != lo or chi != hi:
                    mset(RAWC, 0.0, ["RAWC"])
                dma(RAWCv[:, :, clo - lo:clo - lo + (chi - clo)], zCT[:, clo:chi].rearrange("(j p) t -> p j t", p=128), [id(zCT)], ["RAWC"])
                act(SGv, RAWCv[:, 2:4, :], AF.Sigmoid, ["RAWC"], ["SG"])
                tt(Uv, RAWCv[:, 0:2, :], SGv, ALU.mult, ["RAWC", "SG"], ["U"])
                ts(Uv[:, :, 0:15], Uv[:, :, 0:15], CM[:, 0:1], None, ALU.mult, None, ["U", "CM"], ["U"])
                ts(Uv[:, :, 271:286], Uv[:, :, 271:286], CM[:, 0:1], None, ALU.mult, None, ["U", "CM"], ["U"])
                for ti in range(2):
                    ps, pk = bank()
                    for k in range(31):
                        mm(ps[:, 0:256], DIAGv[:, ti, k, :], Uv[:, ti, k:k + 256], k == 0, k == 30, ["DIAG", "U"], [pk])
                    act(UCv[:, ti, :], ps[:, 0:256], AF.Identity, [pk, "COLT"], ["UC"], bias=col(l, "convb", ti))
                ps, pk = bank()
                for ti in range(2):
                    mm(ps[:, 0:256], ONES, UCv[:, ti, :], ti == 0, ti == 1, ["CONST", "UC"], [pk])
                for ti in range(2):
                    stt(CENv[:, ti, :], ps[:, 0:256], -1.0 / 256, UCv[:, ti, :], ALU.mult, ALU.add, [pk, "UC"], ["CEN"])
                act(SQ, CEN, AF.Square, ["CEN"], ["SQ"])
                ps, pk = bank()
                for ti in range(2):
                    mm(ps[:, 0:256], ONES, SQv[:, ti, :], ti == 0, ti == 1, ["CONST", "SQ"], [pk])
                rsqrt(RSTD, ps[:, 0:256], 1.0 / 256, CONV_LN_EPS, [pk], ["RSTD"])
                for ti in range(2):
                    tt(UNv[:, ti, :], CENv[:, ti, :], RSTD, ALU.mult, ["CEN", "RSTD"], ["UN"])
                    ts(UNv[:, ti, :], UNv[:, ti, :], col(l, "clng", ti), col(l, "clnb", ti), ALU.mult, ALU.add, ["UN", "COLT"], ["UN"])
                act(UNB, UN, AF.Silu, ["UN"], ["UNB"])
                for co in range(2):
                    ps, pk = bank()
                    for ci in range(2):
                        mm(ps[:, 0:256], PWCv[:, ci, co * 128:(co + 1) * 128], UNBv[:, ci, :], ci == 0, ci == 1, ["PWC", "UNB"], [pk])
                    act(YCv[:, co, :], ps[:, 0:256], AF.Identity, [pk], ["YC"])
                dma(catT[512:768, sg * 256:(sg + 1) * 256].rearrange("(j p) t -> p j t", p=128), YCv, ["YC"], [id(catT)])

        def rwkv_phase(l):
            WA = []
            for d in range(2):
                w_ = ab(256)
                dma(w_, waup_d[(l * 2 + d) * 128:(l * 2 + d + 1) * 128, :], [], ["WA%d" % d], cast=True)
                WA.append(w_)
            GUP = ab(256)
            dma(GUP, g_up[l * 128:(l + 1) * 128, :], [], ["GUP"], cast=True)
            MUH = af(8); OMU = af(8); OMKA = af(2)
            mu_c = COLT[:, l * NCOL + COLS["mu"]: l * NCOL + COLS["mu"] + 8]
            ka_c = COLT[:, l * NCOL + COLS["ka"]: l * NCOL + COLS["ka"] + 2]
            ts(MUH, mu_c, 0.5, None, ALU.mult, None, ["COLT"], ["MUH"])
            ts(OMU, mu_c, -1.0, 1.0, ALU.mult, ALU.add, ["COLT"], ["OMU"])
            ts(OMKA, ka_c, -1.0, 1.0, ALU.mult, ALU.add, ["COLT"], ["OMKA"])
            RAW = ab(7 * 514); RAWv = RAW.rearrange("p (k t) -> p k t", k=7)
            ZS = af(7 * 512); ZSv = ZS.rearrange("p (k t) -> p k t", k=7)
            T1 = af(512); T2 = af(512)
            post_off = aoff["o"]
            mk = lambda f: [[f() for _ in range(2)] for _ in range(2)]
            KR, BS, KDS = mk(lambda: ab(1024)), mk(lambda: ab(512)), mk(lambda: ab(512))
            KRv = [[KR[d][hp].rearrange("p (c e t) -> p c e t", c=8, e=2) for hp in range(2)] for d in range(2)]
            BKT = mk(lambda: ab(8 * 2 * 128))
            UVA = mk(lambda: ab(8 * 128)); GL = mk(lambda: af(8))
            VT = mk(lambda: ab(8 * 128))
            WTB = ab(512); ADB = ab(512); VB = ab(512)
            LD, AA, KK, SQ, RN, KKN, KD, EP = [af(512) for _ in range(8)]
            BSE, KSE, KDSE = mk(lambda: ab(2 * 512)), mk(lambda: ab(2 * 512)), mk(lambda: ab(2 * 512))
            mset(WTB, 0.0, ["WTB"]); mset(ADB, 0.0, ["ADB"])
            XB = [ab(512), ab(512)]; XTB = [ab(512), ab(512)]; QB = [ab(512), ab(512)]
            AK, RR = [ab(512) for _ in range(2)]
            NL = ab(512); R1 = ab(512)
            ST = ab(8 * 64); STv = ST.rearrange("p (u v) -> p u v", u=8)
            SI = [af(256), af(256)]; SOUT = af(256); SOUTv = SOUT.rearrange("p (q v) -> p q v", v=64)
            kname = lambda n, d, hp: "%s%d%d" % (n, d, hp)
            for d in range(2):
                for hp in range(2):
                    mset(BKT[d][hp], 0.0, [kname("BKT", d, hp)])
                    for (arr_, nm_) in ((BSE, "BSE"), (KSE, "KSE"), (KDSE, "KDSE")):
                        mset(arr_[d][hp], 0.0, [kname(nm_, d, hp)])
            mset(ST, 0.0, ["ST_0", "ST_1"])
            BKTv = [[BKT[d][hp].rearrange("p (c e n) -> p c e n", c=8, e=2) for hp in range(2)] for d in range(2)]
            UVAv = [[UVA[d][hp].rearrange("p (c n) -> p c n", c=8) for hp in range(2)] for d in range(2)]
            VTv = [[VT[d][hp].rearrange("p (c n) -> p c n", c=8) for hp in range(2)] for d in range(2)]

            def tshift(k, w, rev, out3, slot=None):
                ks_ = k
                k = k if slot is None else slot
                tt(T1, RAWv[:, k, 0:512], TSM[:, 0:512], ALU.mult, ["RAW", "TSM"], ["T1"])
                tt(T2, RAWv[:, k, 2:514], TSM[:, 512:1024], ALU.mult, ["RAW", "TSM"], ["T2"])
                tt(T1, T1, T2, ALU.add, ["T1", "T2"], ["T1"])
                ts(T1, T1, MUH[:, ks_:ks_ + 1], None, ALU.mult, None, ["T1", "MUH"], ["T1"])
                zo = out3[:, ::-1] if rev else out3
                stt(zo, RAWv[:, k, 1:513], OMU[:, ks_:ks_ + 1], T1, ALU.mult, ALU.add, ["RAW", "OMU", "T1"], ["ZS"])

            def load_raw(w, k0, k1, s0=None):
                s0 = k0 if s0 is None else s0
                t0 = w * 512
                lo, hi = t0 - 1, t0 + 513
                clo, chi = max(lo, 0), min(hi, T)
                if clo != lo:
                    mset(RAWv[:, :, 0:1], 0.0, ["RAW"])
                if chi != hi:
                    mset(RAWv[:, :, 513:514], 0.0, ["RAW"])
                dma(RAWv[:, s0:s0 + k1 - k0, clo - lo:clo - lo + chi - clo], zAT[k0 * 128:k1 * 128, clo:chi].rearrange("(k p) t -> p k t", p=128),
                    [id(zAT)], ["RAW"])

            def prep(d, w):
                t0 = w * 512
                load_raw(w, 0, 7)
                for k in range(7):
                    tshift(k, w, d == 1, ZSv[:, k, :])
                act(WTB[0:64, :], ZSv[0:64, 6, :], AF.Tanh, ["ZS"], ["WTB"])
                cp(ADB[64:128, :], ZSv[64:128, 6, :], ["ZS"], ["ADB"])
                for hp in range(2):
                    hs = slice(hp * 128, (hp + 1) * 128)
                    kn = lambda n: kname(n, d, hp)
                    ps, pk = bank()
                    mm(ps, WA[d][:, hs], WTB, True, True, ["WA%d" % d, "WTB"], [pk])
                    act(LD, ps, AF.Sigmoid, [pk, "COLT"], ["LD"], bias=col(l, "w0", d * 2 + hp))
                    ts(LD, LD, -math.exp(-0.5), None, ALU.mult, None, ["LD"], ["LD"])
                    ps, pk = bank()
                    mm(ps, WA[d][:, hs], ADB, True, True, ["WA%d" % d, "ADB"], [pk])
                    act(AA, ps, AF.Sigmoid, [pk, "COLT"], ["AA"], bias=col(l, "a0", d * 2 + hp))
                    ts(KK, ZSv[:, 2 + hp, :], col(l, "kk", hp), None, ALU.mult, None, ["ZS", "COLT"], ["KK"])
                    tt(SQ, KK, KK, ALU.mult, ["KK"], ["SQ"])
                    ps, pk = bank()
                    mm(ps, BONES, SQ, True, True, ["CONST", "SQ"], [pk])
                    rsqrt(RN, ps, 1.0, 1e-12, [pk], ["RN"])
                    tt(KKN, KK, RN, ALU.mult, ["KK", "RN"], ["KKN"])
                    ts(RN, AA, col(l, "ka", hp), OMKA[:, hp:hp + 1], ALU.mult, ALU.add, ["AA", "COLT", "OMKA"], ["RN"])
                    tt(KD, ZSv[:, 2 + hp, :], RN, ALU.mult, ["ZS", "RN"], ["KD"])
                    stt(SQ, ZSv[:, hp, :], col(l, "rk", hp), KD, ALU.mult, ALU.mult, ["ZS", "COLT", "KD"], ["SQ"])
                    ps, pk = bank()
                    mm(ps, BONES, SQ, True, True, ["CONST", "SQ"], [pk])
                    bo = KK[:, ::-1] if d == 1 else KK
                    tt(bo, ps, ZSv[:, 4 + hp, :], ALU.mult, [pk, "ZS"], ["KK"])
                    dma(bonT[d * 256 + hp * 128:d * 256 + (hp + 1) * 128, t0:t0 + 512], KK, ["KK"], ["bonT"])
                    scan(RN, RMASK, LD, 0.0, ALU.mult, ALU.add, ["CONST", "LD"], ["RN"])
                    tt(LD, RN, LD, ALU.subtract, ["RN", "LD"], ["LD"])
                    act(GL[d][hp], RN.rearrange("p (c t) -> p c t", t=64)[:, :, 63], AF.Exp, ["RN"], [kn("GL")])
                    act(EP, RN, AF.Exp, ["RN"], ["EP"])
                    c3_ = lambda a_: a_.rearrange("p (c t) -> p c t", t=64)
                    tt(KRv[d][hp][:, :, 1, :], c3_(ZSv[:, hp, :]), c3_(EP), ALU.mult, ["ZS", "EP"], [kn("KR")])
                    act(SQ, RN, AF.Exp, ["RN"], ["SQ"], scale=-1.0)
                    act(EP, LD, AF.Exp, ["LD"], ["EP"])
                    tt(KRv[d][hp][:, :, 0, :], c3_(KKN), c3_(EP), ALU.mult, ["KKN", "EP"], [kn("KR")])
                    tt(AA, AA, KKN, ALU.mult, ["AA", "KKN"], ["AA"])
                    tt(BS[d][hp], AA, SQ, ALU.mult, ["AA", "SQ"], [kn("BS")])
                    tt(KDS[d][hp], KD, SQ, ALU.mult, ["KD", "SQ"], [kn("KDS")])
                    for (src_, dst_, nm_) in ((None, KSE, "KR"), (BS, BSE, "BS"), (KDS, KDSE, "KDS")):
                        dv_ = dst_[d][hp].rearrange("p (e c t) -> p e c t", e=2, t=64)
                        sv_ = KRv[d][hp][:, :, 0, :] if src_ is None else c3_(src_[d][hp])
                        en_ = "KSE" if src_ is None else nm_ + "E"
                        cp(dv_[0:64, 0, :, :], sv_[0:64], [kn(nm_)], [kn(en_)], eng="pool")
                        cp(dv_[64:128, 1, :, :], sv_[64:128], [kn(nm_)], [kn(en_)], eng="pool")
                    cp(VB, ZSv[:, 4 + hp, :], ["ZS"], ["VB"])
                    for (src, sk, kind) in ((BS[d][hp], kn("BS"), 0), (KDS[d][hp], kn("KDS"), 1), (VB, "VB", 2)):
                        ps, pk = bank(); psb = ps.bitcast(BF16)
                        for c in range(8):
                            tr(psb[0:64, c * 128:(c + 1) * 128], src[:, c * 64:(c + 1) * 64], IDB, [sk, "IDB"], [pk])
                        pv = psb[0:64, :].rearrange("p (c n) -> p c n", n=128)
                        if kind == 0:
                            cp(BKTv[d][hp][0:64, :, 0, 0:64], pv[:, :, 0:64], [pk], [kn("BKT")])
                            cp(BKTv[d][hp][0:64, :, 1, 64:128], pv[:, :, 64:128], [pk], [kn("BKT")])
                        elif kind == 1:
                            cp(BKTv[d][hp][64:128, :, 0, 0:64], pv[:, :, 0:64], [pk], [kn("BKT")])
                            cp(BKTv[d][hp][64:128, :, 1, 64:128], pv[:, :, 64:128], [pk], [kn("BKT")])
                        else:
                            cp(UVAv[d][hp][64:128, :, :], pv, [pk], [kn("UVA")])
                            cp(VTv[d][hp][0:64, :, :], pv, [pk], [kn("VT")])

            UNITS = [(d * 4 + h, d, h, h // 2, h % 2) for d in range(2) for h in range(4)]
            psl = lambda hh: slice(hh * 64, hh * 64 + 64)
            ALLK = [kname(n, d, hp) for n in ("KR", "BS", "KDS", "KSE", "BSE", "KDSE") for d in range(2) for hp in range(2)]
            VTK = [kname(n, d, hp) for n in ("UVA", "VT") for d in range(2) for hp in range(2)]
            BTK = [kname("BKT", d, hp) for d in range(2) for hp in range(2)]
            GLK = [kname("GL", d, hp) for d in range(2) for hp in range(2)]

            QF = [ab(512), ab(512)]; ZBB = [ab(512), ab(512)]; BBK = [ab(512), ab(512)]
            WS2 = [af(512), af(512)]; ZA = af(512)

            UD = [[(d * 4 + h, d, h, h // 2, h % 2) for h in range(4)] for d in range(2)]
            H = lambda buf, d: buf[0:64, d * 256:(d + 1) * 256]
            KDk = lambda d: [kname(n, d, hp) for n in ("KR", "BS", "KDS", "KSE", "BSE", "KDSE") for hp in range(2)]
            VDk = lambda d: [kname(n, d, hp) for n in ("UVA", "VT") for hp in range(2)]
            BDk = lambda d: [kname("BKT", d, hp) for hp in range(2)]

            def dmm(d, lf, rf, reads):
                ps, pk = bank()
                for (u, d_, h, hp, hh) in UD[d]:
                    mm(ps[0:64, h * 64:(h + 1) * 64], lf(u, d, hp, hh), rf(u, d, hp, hh), True, True, reads, [pk])
                return ps[0:64, 0:256], pk

            def pre_ops(c8, par):
                tsl = slice(c8 * 64, (c8 + 1) * 64)
                FK = lambda arr: (lambda u, d, hp, hh: arr[d][hp][:, tsl])
                FE = lambda arr: (lambda u, d, hp, hh: arr[d][hp].rearrange("p (e t) -> p e t", e=2)[:, hh, tsl])
                SB = lambda buf: (lambda u, d, hp, hh: buf[0:64, u * 64:(u + 1) * 64])
                VU = lambda u, d, hp, hh: VTv[d][hp][0:64, c8, hh * 64:(hh + 1) * 64]
                KRP = lambda u, d, hp, hh: KRv[d][hp][:, c8, :, :]
                m4 = lambda m_: m_[0:64, 0:256].rearrange("p (u t) -> p u t", t=64)
                h4 = lambda buf, d: H(buf, d).rearrange("p (u t) -> p u t", t=64)
                ops = []

                def pair_mm(d, lf):
                    ps, pk = bank()
                    for (u, d_, h, hp, hh) in UD[d]:
                        mm(ps[0:64, h * 128:(h + 1) * 128], lf(u, d, hp, hh), KRP(u, d, hp, hh), True, True, KDk(d), [pk])
                    return ps[0:64, :].rearrange("p (u e t) -> p u e t", e=2, t=64), pk

                def g1(d):
                    pv_, pk = pair_mm(d, FE(BSE))
                    tt(h4(XTB[0], d), pv_[:, :, 0, :], m4(MSD), ALU.mult, [pk, "CONST"], ["XT0_%d" % d])
                    tt(h4(BBK[par], d), pv_[:, :, 1, :], m4(MI), ALU.mult, [pk, "CONST"], ["BBK%d_%d" % (par, d)])
                    tt(H(QB[0], d), H(I8, d), H(XTB[0], d), ALU.subtract, ["CONST", "XT0_%d" % d], ["Q0_%d" % d])

                def g2(d):
                    ps, pk = dmm(d, FE(KSE), FK(BS), KDk(d))
                    tt(H(XB[0], d), ps, H(MLD, d), ALU.mult, [pk, "CONST"], ["X0_%d" % d])
                    tt(H(NL, d), ps, H(MLL, d), ALU.mult, [pk, "CONST"], ["NL_%d" % d])

                def g3(d):
                    pv_, pk = pair_mm(d, FE(KDSE))
                    tt(h4(AK, d), pv_[:, :, 0, :], m4(MS), ALU.mult, [pk, "CONST"], ["AK_%d" % d])
                    tt(BBK[par][64:128, d * 256:(d + 1) * 256].rearrange("p (u t) -> p u t", t=64), pv_[:, :, 1, :], m4(MI), ALU.mult,
                       [pk, "CONST"], ["BBK%d_%d" % (par, d)])
                ops += [g1, g2, g3]

                def lvl(lev, cur):
                    nxt = 1 - cur

                    def a(d):
                        ps, pk = dmm(d, SB(XTB[cur]), SB(XB[cur]), ["XT%d_%d" % (cur, d), "X%d_%d" % (cur, d)])
                        if lev < 3:
                            ps2, pk2 = dmm(d, SB(XB[cur]), SB(XTB[cur]), ["XT%d_%d" % (cur, d), "X%d_%d" % (cur, d)])
                        act(H(XB[nxt], d), ps, AF.Identity, [pk], ["X%d_%d" % (nxt, d)])
                        if lev < 3:
                            act(H(XTB[nxt], d), ps2, AF.Identity, [pk2], ["XT%d_%d" % (nxt, d)])

                    def b(d):
                        ps, pk = dmm(d, SB(XB[nxt]), SB(QB[cur]), ["X%d_%d" % (nxt, d), "Q%d_%d" % (cur, d)])
                        if lev < 3:
                            tt(H(QB[nxt], d), ps, H(QB[cur], d), ALU.add, [pk, "Q%d_%d" % (cur, d)], ["Q%d_%d" % (nxt, d)])
                        else:
                            tt(H(QF[par], d), ps, H(QB[cur], d), ALU.add, [pk, "Q%d_%d" % (cur, d)], ["QF%d_%d" % (par, d)])
                    return [a, b]
                ops += lvl(1, 0) + lvl(2, 1) + lvl(3, 0)
                YB, YTB, P1B = XB[0], XTB[0], XB[1]

                def z1(d):
                    ps, pk = dmm(d, SB(QF[par]), SB(NL), ["QF%d_%d" % (par, d), "NL_%d" % d])
                    act(H(YB, d), ps, AF.Identity, [pk], ["X0_%d" % d])

                def z2(d):
                    ps, pk = dmm(d, SB(NL), SB(QF[par]), ["QF%d_%d" % (par, d), "NL_%d" % d])
                    cp(H(YTB, d), ps, [pk], ["XT0_%d" % d])
                    stt(H(ZA, d), ps, -1.0, H(I8, d), ALU.mult, ALU.add, ["CONST", pk], ["ZA_%d" % d])

                def z3(d):
                    ps, pk = dmm(d, SB(YB), SB(YTB), ["X0_%d" % d, "XT0_%d" % d])
                    cp(H(P1B, d), ps, [pk], ["X1_%d" % d])
                    tt(H(ZA, d), ps, H(ZA, d), ALU.add, ["ZA_%d" % d, pk], ["ZA_%d" % d])

                def z4(d):
                    ps, pk = dmm(d, SB(YB), SB(P1B), ["X0_%d" % d, "X1_%d" % d])
                    stt(H(ZBB[par], d), ps, -1.0, H(ZA, d), ALU.mult, ALU.add, ["ZA_%d" % d, pk], ["ZBB%d_%d" % (par, d)])

                def w1_(d):
                    ps, pk = dmm(d, SB(AK), VU, ["AK_%d" % d] + VDk(d))
                    act(H(WS2[par], d), ps, AF.Identity, [pk], ["WS%d_%d" % (par, d)])
                ops += [z1, z2, z3, z4, w1_]
                return ops

            def chain_ops(wf, wb, c8, par):
                SB = lambda buf: (lambda u, d, hp, hh: buf[0:64, u * 64:(u + 1) * 64])
                KAP = lambda u, d, hp, hh: KRv[d][hp][:, c8, 0, :]
                UV = lambda u, d, hp, hh: UVAv[d][hp][:, c8, hh * 64:(hh + 1) * 64]
                stk = lambda d: "ST_%d" % d

                def c1(d):
                    ps, pk = dmm(d, KAP, lambda u, d, hp, hh: STv[:, u, :], KDk(d) + [stk(d)])
                    tt(H(RR, d), ps, H(WS2[par], d), ALU.add, [pk, "WS%d_%d" % (par, d)], ["RR_%d" % d])

                def c2(d):
                    ps, pk = dmm(d, SB(QF[par]), SB(RR), ["QF%d_%d" % (par, d), "RR_%d" % d])
                    act(H(R1, d), ps, AF.Identity, [pk], ["R1_%d" % d])

                def c3(d):
                    ps, pk = dmm(d, SB(ZBB[par]), SB(R1), ["ZBB%d_%d" % (par, d), "R1_%d" % d])
                    for hp in range(2):
                        ts(UVAv[d][hp][0:64, c8, :], ps[:, hp * 128:(hp + 1) * 128], -1.0, None, ALU.mult, None, [pk], [kname("UVA", d, hp)])

                def c4(d):
                    tsl_unused = None
                    psy, pky = bank()
                    pss, pks = bank()
                    for (u, d_, h, hp, hh) in UD[d]:
                        cs = slice(h * 64, (h + 1) * 64)
                        mm(psy[0:64, cs], STv[:, u, :], KRv[d][hp][:, c8, 1, :], True, False, KDk(d) + [stk(d)], [pky])
                        mm(psy[0:64, cs], UV(u, d, hp, hh), BBK[par][:, u * 64:(u + 1) * 64], False, True, ["BBK%d_%d" % (par, d)] + VDk(d), [pky])
                    for (u, d_, h, hp, hh) in UD[d]:
                        cs = slice(h * 64, (h + 1) * 64)
                        mm(pss[:, cs], BKTv[d][hp][:, c8, hh, :], UV(u, d, hp, hh), True, False, BDk(d) + VDk(d), [pks])
                        mm(pss[:, cs], IDB, STv[:, u, :], False, True, ["IDB", stk(d)], [pks])
                    for hp in range(2):
                        u0 = d * 4 + hp * 2
                        ts(STv[:, u0:u0 + 2, :], pss[:, hp * 128:(hp + 1) * 128].rearrange("p (u v) -> p u v", v=64),
                           GL[d][hp][:, c8:c8 + 1], None, ALU.mult, None, [pks, kname("GL", d, hp)], [stk(d)])
                    ysb = ZA[0:64, d * 256:(d + 1) * 256].rearrange("p (h t) -> p h t", t=64)
                    act(ysb[:, :, ::-1] if d == 1 else ysb, psy[0:64, 0:256].rearrange("p (h t) -> p h t", t=64), AF.Identity, [pky], ["ZA_%d" % d])
                    tok0 = wf * 512 + c8 * 64 if d == 0 else wb * 512 + 512 - (c8 + 1) * 64
                    dma(yT[d * 256:(d + 1) * 256, tok0:tok0 + 64].rearrange("(h v) t -> v h t", v=64), ysb, ["ZA_%d" % d], ["yT"])
                    if c8 % 4 == 3:
                        sv_ = STv[:, d * 4:(d + 1) * 4, :].rearrange("p (q e) v -> p q e v", e=2)
                        tt(SOUTv[:, d * 2:(d + 1) * 2, :], sv_[:, :, 0, :], sv_[:, :, 1, :], ALU.add, [stk(d)], ["SOUT_%d" % d])
                        seg = (2 * wf + c8 // 4) if d == 0 else (2 * wb + 1 - c8 // 4)
                        base = ((l * NSEG + seg) * 2 + d) * 256
                        dma(st_r[base:base + 256, :].rearrange("(q p) v -> p q v", p=128), SOUTv[:, d * 2:(d + 1) * 2, :], ["SOUT_%d" % d], ["st_r"])
                        segn = seg + 1 if d == 0 else seg - 1
                        if 0 <= segn < NSEG:
                            rb = ((l * NSEG + segn) * 2 + d) * 128
                            dma(SI[d], s0r_d[rb:rb + 128, :], [], ["SI%d" % d])
                            stt(STv[:, d * 4:(d + 1) * 4, :], STv[:, d * 4:(d + 1) * 4, :], CM[:, 0:1],
                                SI[d].rearrange("p (h v) -> p h v", v=64), ALU.mult, ALU.add, [stk(d), "CM", "SI%d" % d], [stk(d)])
                return [c1, c2, c3, c4]

            def run_window(wf, wb):
                for f in pre_ops(0, 0):
                    f(0); f(1)
                for c8 in range(8):
                    ch = chain_ops(wf, wb, c8, c8 % 2)
                    pr = pre_ops(c8 + 1, (c8 + 1) % 2) if c8 < 7 else []
                    n = len(pr); k = len(ch)
                    for i_, cf in enumerate(ch):
                        cf(0); cf(1)
                        for f in pr[(i_ * n) // k:((i_ + 1) * n) // k]:
                            f(0); f(1)

            for d in range(2):
                seg = 0 if d == 0 else NSEG - 1
                rb = ((l * NSEG + seg) * 2 + d) * 128
                dma(SI[d], s0r_d[rb:rb + 128, :], [], ["SI%d" % d])
                cp(STv[:, d * 4:(d + 1) * 4, :], SI[d].rearrange("p (h v) -> p h v", v=64), ["SI%d" % d], ["ST_%d" % d])
            for j in range(8 if RW_STAGE >= 3 else 1):
                prep(0, j)
                prep(1, 7 - j)
                if RW_STAGE >= 2:
                    run_window(j, 7 - j)
            P.barrier()
            aoff["o"] = post_off
            Y0, Y1, Y2, Y3, CEN2, SQ2, RS2, YO = [af(512) for _ in range(8)]
            SGB = ab(512); GZ = af(512); CATA = ab(512)
            for w in range(8):
                t0 = w * 512
                load_raw(w, 7, 8, 0)
                tshift(7, w, False, GZ, slot=0)
                act(SGB, GZ, AF.Sigmoid, ["ZS"], ["SGB"])
                for hp in range(2):
                    rows = lambda d: slice(d * 256 + hp * 128, d * 256 + (hp + 1) * 128)
                    dma(Y0, yT[rows(0), t0:t0 + 512], ["yT"], ["Y0"]); dma(Y1, yT[rows(1), t0:t0 + 512], ["yT"], ["Y1"])
                    dma(Y2, bonT[rows(0), t0:t0 + 512], ["bonT"], ["Y2"]); dma(Y3, bonT[rows(1), t0:t0 + 512], ["bonT"], ["Y3"])
                    tt(Y0, Y0, Y1, ALU.add, ["Y0", "Y1"], ["Y0"])
                    tt(Y2, Y2, Y3, ALU.add, ["Y2", "Y3"], ["Y2"])
                    tt(Y0, Y0, Y2, ALU.add, ["Y0", "Y2"], ["Y0"])
                    ps, pk = bank()
                    mm(ps, BONES, Y0, True, True, ["CONST", "Y0"], [pk])
                    stt(CEN2, ps, -1.0 / 64, Y0, ALU.mult, ALU.add, [pk, "Y0"], ["CEN2"])
                    act(SQ2, CEN2, AF.Square, ["CEN2"], ["SQ2"])
                    ps, pk = bank()
                    mm(ps, BONES, SQ2, True, True, ["CONST", "SQ2"], [pk])
                    rsqrt(RS2, ps, 1.0 / 64, RWKV_LN_EPS, [pk], ["RS2"])
                    tt(YO, CEN2, RS2, ALU.mult, ["CEN2", "RS2"], ["YO"])
                    ts(YO, YO, col(l, "lng", hp), col(l, "lnb", hp), ALU.mult, ALU.add, ["YO", "COLT"], ["YO"])
                    ps, pk = bank()
                    mm(ps, GUP[:, hp * 128:(hp + 1) * 128], SGB, True, True, ["GUP", "SGB"], [pk])
                    tt(CATA, YO, ps, ALU.mult, ["YO", pk], ["CATA"])
                    dma(catT[hp * 128:(hp + 1) * 128, t0:t0 + 512], CATA, ["CATA"], [id(catT)])

        def mlstm_phase(l):
            TG, GI, GF, LI, LF, BCUM, DD, GG, ED, CL, WE, TMPG = [af(256) for _ in range(12)]
            NFB = af(1); EBL = af(4)
            v4 = lambda a: a.rearrange("p (c t) -> p c t", t=64)
            for (gi, dst, dk) in ((0, GI, "GI"), (1, GF, "GF")):
                for d in range(2):
                    dma(TG[d * 64:(d + 1) * 64, :], zG[gi * 8 + d * 4:gi * 8 + (d + 1) * 4, :].rearrange("h (s t) -> (h s) t", t=256), ["zG"], ["TG"])
                cp(dst[0:64, :], TG[0:64, :], ["TG"], [dk])
                cp(dst[64:128, :], TG[64:128, ::-1], ["TG"], [dk])
            ts(LI, GI, GB[:, l * 2:l * 2 + 1], None, ALU.add, None, ["GI", "GB"], ["LI"])
            ts(NFB, GB[:, l * 2 + 1:l * 2 + 2], -1.0, None, ALU.mult, None, ["GB"], ["NFB"])
            act(LF, GF, AF.Exp, ["GF", "NFB"], ["LF"], bias=NFB[:, 0:1], scale=-1.0)
            act(LF, LF, AF.Ln, ["LF", "ONEC"], ["LF"], bias=ONEC[:, 0:1])
            ts(LF, LF, -1.0, None, ALU.mult, None, ["LF"], ["LF"])
            scan(BCUM, RMASK[:, 0:256], LF, 0.0, ALU.mult, ALU.add, ["CONST", "LF"], ["BCUM"])
            tt(DD, LI, BCUM, ALU.subtract, ["LI", "BCUM"], ["DD"])
            scan(GG, RNEG[:, 0:256], DD, -1e30, ALU.add, ALU.max, ["CONST", "DD"], ["GG"])
            CHG = af(64); CHB = af(64); S0M = af(16); MOUT = af(16); MC = af(1); EMI = af(16); EMO = af(16)
            XR = af(128); EMIR = af(128); EMOR = af(128); XA = af(512); ALT = af(512)
            for t_ in (CHG, CHB, S0M, MOUT, MC):
                mset(t_, 0.0, [id(t_)])
            GC4 = af(4); BL4 = af(4)
            cp(GC4, v4(GG)[:, :, 63], ["GG"], ["GC4"]); cp(BL4, v4(BCUM)[:, :, 63], ["BCUM"], ["BL4"])
            dma(gtab[0:128, 0:4], GC4, ["GC4"], ["gtab"])
            dma(gtab[128:256, 0:4], BL4, ["BL4"], ["gtab"])
            for d in range(2):
                rs = slice(d * 32, d * 32 + 4)
                dma(CHG[rs, :].rearrange("p (s c) -> p s c", c=4), gtab[d * 64:(d + 1) * 64, 0:4].rearrange("(h s) c -> h s c", s=16), ["gtab"], [id(CHG)])
                dma(CHB[rs, :].rearrange("p (s c) -> p s c", c=4), gtab[128 + d * 64:128 + (d + 1) * 64, 0:4].rearrange("(h s) c -> h s c", s=16), ["gtab"], [id(CHB)])
                dma(S0M[rs, :], s0m_d[l * 8 + d * 4:l * 8 + (d + 1) * 4, :], [], [id(S0M)])
            for d in range(2):
                rs = slice(d * 32, d * 32 + 4)
                mk_ = "MC%d" % d
                for n in range(64):
                    seg = n // 4 if d == 0 else 15 - n // 4
                    colt = seg * 4 + n % 4
                    if n % 4 == 0:
                        if n == 0:
                            cp(MC[rs, :], S0M[rs, seg:seg + 1], [id(S0M)], [mk_], eng="dve")
                        else:
                            ts(MC[rs, :], MC[rs, :], CM[rs, 0:1], None, ALU.mult, None, [mk_, "CM"], [mk_], eng="dve")
                            tt(MC[rs, :], MC[rs, :], S0M[rs, seg:seg + 1], ALU.add, [mk_, id(S0M)], [mk_], eng="dve")
                    tt(MC[rs, :], MC[rs, :], CHG[rs, colt:colt + 1], ALU.max, [mk_, id(CHG)], [mk_], eng="dve")
                    tt(MC[rs, :], MC[rs, :], CHB[rs, colt:colt + 1], ALU.add, [mk_, id(CHB)], [mk_], eng="dve")
                    if n % 4 == 3:
                        cp(MOUT[rs, seg:seg + 1], MC[rs, :], [mk_], ["MOUT%d" % d], eng="dve")
                dma(st_m[l * 8 + d * 4:l * 8 + (d + 1) * 4, :], MOUT[rs, :], ["MOUT%d" % d, id(MOUT)], ["st_m"])
            act(EMI[0:64, :], S0M[0:64, :], AF.Exp, [id(S0M)], ["EMI"])
            act(EMO[0:64, :], MOUT[0:64, :], AF.Exp, [id(MOUT), "MOUT0", "MOUT1"], ["EMO"], scale=-1.0)
            for (src, sk, dst, dk) in ((EMI, "EMI", EMIR, "EMIR"), (EMO, "EMO", EMOR, "EMOR")):
                tt(XR[0:64, :].rearrange("p (s j) -> p s j", j=8), src[0:64, :].unsqueeze(2).broadcast_to([64, 16, 8]),
                   PK2[0:64, 0:8].unsqueeze(1).broadcast_to([64, 16, 8]), ALU.mult, [sk, "CONST"], ["XR"])
                ps, pk = bank()
                mm(ps[:, 0:128], ONES[0:64, :], XR[0:64, :], True, True, ["CONST", "XR"], [pk])
                cp(dst, ps[:, 0:128], [pk], [dk])
            act(ED, DD, AF.Exp, ["DD"], ["ED"])
            act(CL, BCUM, AF.Exp, ["BCUM"], ["CL"], scale=-1.0)
            tt(v4(TMPG), v4(DD), v4(BCUM)[:, :, 63:64].broadcast_to([128, 4, 64]), ALU.add, ["DD", "BCUM"], ["TMPG"])
            act(WE, TMPG, AF.Exp, ["TMPG", "LN8C"], ["WE"], bias=LN8C[:, 0:1])
            act(EBL, v4(BCUM)[:, :, 63], AF.Exp, ["BCUM"], ["EBL"])
            tt(XA.rearrange("p (s c j) -> p s c j", s=16, c=4), PICKv.unsqueeze(2).broadcast_to([128, 16, 4, 8]),
               EBL.unsqueeze(1).unsqueeze(3).broadcast_to([128, 16, 4, 8]), ALU.mult, ["CONST", "EBL"], ["XA"])
            ps, pk = bank()
            mm(ps, ONES, XA, True, True, ["CONST", "XA"], [pk])
            cp(ALT, ps, [pk], ["ALT"])
            ALTv = ALT.rearrange("p (n j) -> p n j", j=8)
            EDT, WET, CLT = af(512), af(512), af(512)
            for (tab, tk, dst, dk) in ((ED, "ED", EDT, "EDT"), (WE, "WE", WET, "WET"), (CL, "CL", CLT, "CLT")):
                ps, pk = bank()
                for sg in range(16):
                    for c in range(4):
                        n = sg * 4 + c
                        mm(ps[0:64, n * 8:(n + 1) * 8], tab[:, c * 64:(c + 1) * 64], PICKv[:, sg, :], True, True, [tk, "CONST"], [pk])
                cp(dst[0:64, :], ps[0:64, :], [pk], [dk])
            EDTv, WETv, CLTv = [a.rearrange("p (n j) -> p n j", j=8) for a in (EDT, WET, CLT)]
            EMIRv = EMIR.rearrange("p (s j) -> p s j", j=8); EMORv = EMOR.rearrange("p (s j) -> p s j", j=8)
            RAWQ = ab(4 * 514); RAWQv = RAWQ.rearrange("p (k t) -> p k t", k=4)
            Q1, Q2, Q3 = af(512), af(512), af(512)
            QK = [[ab(512) for _ in range(4)] for _ in range(2)]
            KE = [[ab(2 * 512) for _ in range(2)] for _ in range(2)]
            KEv = [[KE[d][hp].rearrange("p (e t) -> p e t", e=2) for hp in range(2)] for d in range(2)]
            KH = [[ab(8 * 2 * 128) for _ in range(2)] for _ in range(2)]
            KHv = [[KH[d][hp].rearrange("p (c e n) -> p c e n", c=8, e=2) for hp in range(2)] for d in range(2)]
            VA = [ab(8 * 4 * 65), ab(8 * 4 * 65)]
            VAv = [VA[d].rearrange("p (c h n) -> p c h n", c=8, h=4) for d in range(2)]
            CT = af(8 * 65); CTv = CT.rearrange("p (u n) -> p u n", n=65)
            CB = ab(8 * 65); CBv = CB.rearrange("p (u n) -> p u n", n=65)
            VN = ab(8 * 256); HMR = af(256)
            TMPS = af(512); HM = af(512); DN = af(8); TMPC = af(260); SIC = af(260); CO = af(130)
            TMPCv = TMPC.rearrange("p (h n) -> p h n", n=65); SICv = SIC.rearrange("p (h n) -> p h n", n=65); COv = CO.rearrange("p (q n) -> p q n", n=65)
            for d in range(2):
                mset(VA[d], 1.0, ["VA%d" % d])
                for hp in range(2):
                    mset(KH[d][hp], 0.0, ["KH%d%d" % (d, hp)])
                    mset(KE[d][hp], 0.0, ["KE%d%d" % (d, hp)])
            mset(CT, 0.0, ["CT0", "CT1"])
            TMPC2 = [TMPC, af(260)]; TMPCv2 = [t_.rearrange("p (h n) -> p h n", n=65) for t_ in TMPC2]

            def seg_of(d, w, c8):
                return (2 * w + c8 // 4) if d == 0 else (2 * w + 1 - c8 // 4)

            def enter_seg(d, seg, first):
                rb = ((l * NSEG + seg) * 2 + d) * 128
                dma(SIC, s0c_d[rb:rb + 128, :], [], ["SIC"])
                tt(SICv, SICv, EMIRv[:, seg, d * 4:(d + 1) * 4].unsqueeze(2).broadcast_to([128, 4, 65]), ALU.mult, ["SIC", "EMIR"], ["SIC"])
                if first:
                    cp(CTv[:, d * 4:(d + 1) * 4, :], SICv, ["SIC"], ["CT%d" % d])
                else:
                    stt(CTv[:, d * 4:(d + 1) * 4, :], CTv[:, d * 4:(d + 1) * 4, :], CM[:, 0:1], SICv, ALU.mult, ALU.add, ["CT%d" % d, "CM", "SIC"], ["CT%d" % d])
                cp(CBv[:, d * 4:(d + 1) * 4, :], CTv[:, d * 4:(d + 1) * 4, :], ["CT%d" % d], ["CB%d" % d])

            def prep(d, w):
                t0 = w * 512
                lo, hi = t0 - 1, t0 + 513
                clo, chi = max(lo, 0), min(hi, T)
                if clo != lo:
                    mset(RAWQv[:, :, 0:1], 0.0, ["RAWQ"])
                if chi != hi:
                    mset(RAWQv[:, :, 513:514], 0.0, ["RAWQ"])
                dma(RAWQv[:, :, clo - lo:clo - lo + chi - clo], zQKT[:, clo:chi].rearrange("(k p) t -> p k t", p=128), [id(zQKT)], ["RAWQ"])
                for k in range(4):
                    stt(Q1, RAWQv[:, k, 0:512], col(l, "qkc", 0 * 4 + k), TSM[:, 0:512], ALU.mult, ALU.mult, ["RAWQ", "COLT", "TSM"], ["Q1"])
                    stt(Q2, RAWQv[:, k, 2:514], col(l, "qkc", 2 * 4 + k), TSM[:, 512:1024], ALU.mult, ALU.mult, ["RAWQ", "COLT", "TSM"], ["Q2"])
                    stt(Q3, RAWQv[:, k, 1:513], col(l, "qkc", 1 * 4 + k), Q1, ALU.mult, ALU.add, ["RAWQ", "COLT", "Q1"], ["Q3"])
                    tt(Q3, Q3, Q2, ALU.add, ["Q3", "Q2"], ["Q3"])
                    qo = QK[d][k][:, ::-1] if d == 1 else QK[d][k]
                    act(qo, Q3, AF.Silu, ["Q3"], ["QK%d%d" % (d, k)])
                    if k >= 2:
                        cp(KEv[d][k - 2][0:64, 0, :], QK[d][k][0:64, :], ["QK%d%d" % (d, k)], ["KE%d%d" % (d, k - 2)], eng="pool")
                        cp(KEv[d][k - 2][64:128, 1, :], QK[d][k][64:128, :], ["QK%d%d" % (d, k)], ["KE%d%d" % (d, k - 2)], eng="pool")
                vsrc = zV[t0:t0 + 512, :]
                if d == 0:
                    for h in range(4):
                        dma(VAv[d][0:64, :, h, 0:64], vsrc[:, h * 64:(h + 1) * 64].rearrange("(c p) v -> p c v", p=64), [id(zV)], ["VA%d" % d])
                else:
                    dma(VN[0:64, :].rearrange("p (c n) -> p c n", c=8), vsrc.rearrange("(c p) n -> p c n", p=64), [id(zV)], ["VN"])
                    for i2 in range(4):
                        ps, pk = bank()
                        mm(ps[0:64, :], JB[0:64, :], VN[0:64, i2 * 512:(i2 + 1) * 512], True, True, ["JB", "VN"], [pk])
                        for cc in range(2):
                            act(VAv[d][0:64, i2 * 2 + cc, :, 0:64], ps[0:64, cc * 256:(cc + 1) * 256].rearrange("p (h v) -> p h v", v=64), AF.Identity, [pk], ["VA%d" % d])
                for hp in range(2):
                    ps, pk = bank(); psb = ps.bitcast(BF16)
                    for c in range(8):
                        tr(psb[0:64, c * 128:(c + 1) * 128], QK[d][2 + hp][:, c * 64:(c + 1) * 64], IDB, ["QK%d%d" % (d, 2 + hp), "IDB"], [pk])
                    pv = psb[0:64, :].rearrange("p (c n) -> p c n", n=128)
                    for half in range(2):
                        seg = seg_of(d, w, half * 4)
                        for hh in range(2):
                            u = d * 4 + hp * 2 + hh
                            tt(KHv[d][hp][0:64, half * 4:(half + 1) * 4, hh, hh * 64:(hh + 1) * 64], pv[:, half * 4:(half + 1) * 4, hh * 64:(hh + 1) * 64],
                               WETv[0:64, seg * 4:seg * 4 + 4, u:u + 1].broadcast_to([64, 4, 64]), ALU.mult, [pk, "WET"], ["KH%d%d" % (d, hp)])

            UNITS = [(d * 4 + h, d, h, h // 2, h % 2) for d in range(2) for h in range(4)]
            psl = lambda hh: slice(hh * 64, hh * 64 + 64)
            QKK = ["QK%d%d" % (d, k) for d in range(2) for k in range(4)] + ["KE%d%d" % (d, hp) for d in range(2) for hp in range(2)]

            SRB = [ab(512), ab(512)]

            def pre_step(wf, wb, c8, par):
                tsl = slice(c8 * 64, (c8 + 1) * 64)
                pss, pks = bank()
                for (u, d, h, hp, hh) in UNITS:
                    cs = slice(u * 64, (u + 1) * 64)
                    mm(pss[0:64, cs], KEv[d][hp][:, hh, tsl], QK[d][hp][:, tsl], True, True, QKK, [pks])
                for d in range(2):
                    w = wf if d == 0 else wb
                    seg = seg_of(d, w, c8); n = seg * 4 + c8 % 4
                    tt(TMPS[0:64, d * 256:(d + 1) * 256].rearrange("p (h t) -> p h t", t=64), pss[0:64, d * 256:(d + 1) * 256].rearrange("p (h t) -> p h t", t=64),
                       EDTv[0:64, n, d * 4:(d + 1) * 4].unsqueeze(2).broadcast_to([64, 4, 64]), ALU.mult, [pks, "EDT"], ["TMPS"])
                tt(SRB[par][0:64, :], TMPS[0:64, :], MI8[0:64, :], ALU.mult, ["TMPS", "CONST"], ["SR%d" % par])

            def chain_step(wf, wb, c8, par, d):
                tsl = slice(c8 * 64, (c8 + 1) * 64)
                if True:
                    w = wf if d == 0 else wb
                    seg = seg_of(d, w, c8); n = seg * 4 + c8 % 4
                    psh, pkh = bank(); psc, pkc = bank()
                    pshv = psh[0:64, 0:260].rearrange("p (h n) -> p h n", n=65)
                    cv = c8 if d == 0 else 7 - c8
                    for h in range(4):
                        u, hp, hh = d * 4 + h, h // 2, h % 2
                        cs = slice(u * 64, (u + 1) * 64)
                        mm(psh[0:64, h * 65:(h + 1) * 65], SRB[par][0:64, cs], VAv[d][0:64, cv, h, :], True, False, ["SR%d" % par, "VA%d" % d], [pkh])
                        mm(psh[0:64, h * 65:(h + 1) * 65], QK[d][hp][:, tsl], CBv[:, u, :], False, True, QKK + ["CB%d" % d], [pkh])
                    for h in range(4):
                        u, hp, hh = d * 4 + h, h // 2, h % 2
                        mm(psc[:, h * 65:(h + 1) * 65], KHv[d][hp][0:64, c8, hh, :], VAv[d][0:64, cv, h, :], True, True, ["KH%d%d" % (d, hp), "VA%d" % d], [pkc])
                    dn = DN[0:64, d * 4:(d + 1) * 4]
                    act(dn, pshv[:, :, 64], AF.Abs, [pkh], ["DN%d" % d])
                    tt(dn, dn, CLTv[0:64, n, d * 4:(d + 1) * 4], ALU.max, ["DN%d" % d, "CLT"], ["DN%d" % d])
                    recip(dn, dn, ["DN%d" % d], ["DN%d" % d])
                    tt(HM[0:64, d * 256:(d + 1) * 256].rearrange("p (h v) -> p h v", v=64), pshv[:, :, 0:64],
                       dn.unsqueeze(2).broadcast_to([64, 4, 64]), ALU.mult, [pkh, "DN%d" % d], ["HM%d" % d])
                    tokb = (wf * 512 + c8 * 64) if d == 0 else (wb * 512 + 512 - (c8 + 1) * 64)
                    hdst = hm[d * T + tokb:d * T + tokb + 64, :]
                    if d == 0:
                        dma(hdst, HM[0:64, 0:256], ["HM0"], ["hm"])
                    else:
                        psr, pkr = bank()
                        mm(psr[0:64, 0:256], JF[0:64, 0:64], HM[0:64, 256:512], True, True, ["CONST", "HM1"], [pkr])
                        act(HMR[0:64, :], psr[0:64, 0:256], AF.Identity, [pkr], ["HMR"])
                        dma(hdst, HMR[0:64, :], ["HMR"], ["hm"])
                    tt(TMPCv2[d], CTv[:, d * 4:(d + 1) * 4, :], ALTv[:, n, d * 4:(d + 1) * 4].unsqueeze(2).broadcast_to([128, 4, 65]), ALU.mult, ["CT%d" % d, "ALT"], ["TMPC%d" % d])
                    tt(CTv[:, d * 4:(d + 1) * 4, :], TMPCv2[d], psc[:, 0:260].rearrange("p (h n) -> p h n", n=65), ALU.add, ["TMPC%d" % d, pkc], ["CT%d" % d])
                    if c8 % 4 == 3:
                        tt(TMPCv2[d], CTv[:, d * 4:(d + 1) * 4, :], EMORv[:, seg, d * 4:(d + 1) * 4].unsqueeze(2).broadcast_to([128, 4, 65]), ALU.mult, ["CT%d" % d, "EMOR"], ["TMPC%d" % d])
                        t4_ = TMPC2[d].rearrange("p (q e n) -> p q e n", e=2, n=65)
                        tt(COv, t4_[:, :, 0, :], t4_[:, :, 1, :], ALU.add, ["TMPC%d" % d], ["CO"])
                        base = ((l * NSEG + seg) * 2 + d) * 256
                        dma(st_c[base:base + 256, :].rearrange("(q p) n -> p q n", p=128), COv, ["CO"], ["st_c"])
                        segn = seg + 1 if d == 0 else seg - 1
                        if 0 <= segn < NSEG:
                            enter_seg(d, segn, False)
                        else:
                            cp(CBv[:, d * 4:(d + 1) * 4, :], CTv[:, d * 4:(d + 1) * 4, :], ["CT%d" % d], ["CB%d" % d])
                    else:
                        cp(CBv[:, d * 4:(d + 1) * 4, :], CTv[:, d * 4:(d + 1) * 4, :], ["CT%d" % d], ["CB%d" % d])


            def run_window(wf, wb):
                pre_step(wf, wb, 0, 0)
                for c8 in range(8):
                    chain_step(wf, wb, c8, c8 % 2, 0)
                    if c8 < 7:
                        pre_step(wf, wb, c8 + 1, (c8 + 1) % 2)
                    chain_step(wf, wb, c8, c8 % 2, 1)

            enter_seg(0, 0, True)
            enter_seg(1, NSEG - 1, True)
            for j in range(8):
                prep(0, j)
                prep(1, 7 - j)
                run_window(j, 7 - j)
            P.barrier()
            H0, H1 = af(256), af(256); ZOB = ab(256); SGO = af(256); HSQ = af(256); SSM = af(4); HN = ab(256)
            CATD = ab(2 * 512); CATDv = CATD.rearrange("p (j t) -> p j t", j=2)
            for i in range(32):
                r0 = i * 128
                dma(H0, hm[r0:r0 + 128, :], ["hm"], ["H0"]); dma(H1, hm[T + r0:T + r0 + 128, :], ["hm"], ["H1"])
                dma(ZOB, zO[r0:r0 + 128, :], [id(zO)], ["ZOB"])
                tt(H0, H0, H1, ALU.add, ["H0", "H1"], ["H0"])
                act(SGO, ZOB, AF.Sigmoid, ["ZOB"], ["SGO"])
                tt(H0, H0, SGO, ALU.mult, ["H0", "SGO"], ["H0"])
                tt(HSQ, H0, H0, ALU.mult, ["H0"], ["HSQ"])
                P.op("dve", lambda e: e.tensor_reduce(SSM, HSQ.rearrange("p (h v) -> p h v", v=64), AX.X, ALU.add), ["HSQ"], ["SSM"])
                rsqrt(SSM, SSM, 1.0 / 64, EPS, ["SSM"], ["SSM"])
                tt(HN.rearrange("p (h v) -> p h v", v=64), H0.rearrange("p (h v) -> p h v", v=64),
                   SSM.unsqueeze(2).broadcast_to([128, 4, 64]), ALU.mult, ["H0", "SSM"], ["HN"])
                ps, pk = bank(); psb = ps.bitcast(BF16)
                for j in range(2):
                    tr(psb[:, j * 128:(j + 1) * 128], HN[:, j * 128:(j + 1) * 128], IDB, ["HN", "IDB"], [pk])
                for j in range(2):
                    ts(CATDv[:, j, (i % 4) * 128:(i % 4 + 1) * 128], psb[:, j * 128:(j + 1) * 128], col(l, "hng", j), None, ALU.mult, None, [pk, "COLT"], ["CATD"])
                if i % 4 == 3:
                    b = i // 4
                    dma(catT[768:1024, b * 512:(b + 1) * 512].rearrange("(j p) t -> p j t", p=128), CATDv, ["CATD"], [id(catT)])

        def make_norm(XN, XNB, SS, HTv):
            def norm_tile(xt, xk, A_, B_, tcol, hk):
                act(XN, xt, AF.Square, [xk], ["XN", "SS"], accum=SS[:, 0:1])
                rsqrt(SS[:, 1:2], SS[:, 0:1], 1.0 / D, EPS, ["SS"], ["SS1"])
                stt(XN, xt, SS[:, 1:2], A_, ALU.mult, ALU.mult, [xk, "SS1", id(A_)], ["XN"])
                tt(XNB, XN, B_, ALU.add, ["XN", id(B_)], ["XNB"])
                ps, pk = bank()
                psb = ps.bitcast(BF16)
                for k in range(8):
                    tr(psb[:, k * 128:(k + 1) * 128], XNB[:, k * 128:(k + 1) * 128], IDB, ["XNB", "IDB"], [pk])
                act(HTv[:, :, tcol * 128:(tcol + 1) * 128], psb.rearrange("p (k t) -> p k t", k=8), AF.Identity, [pk], [hk])
            return norm_tile

        bc = lambda src: src.partition_broadcast(128)
        for l in range(nlayers):
            aoff["o"] = persist_end
            WM = [af(8 * 512), af(8 * 512)]
            MR = af(512); BM = af(512)
            for cb in range(12):
                wt_ = WM[cb % 2]; wk = "WM%d" % (cb % 2)
                dma(wt_.rearrange("p (k n) -> p k n", k=8),
                    w_mod[l * D:(l + 1) * D, cb * 512:(cb + 1) * 512].rearrange("(k p) n -> p k n", p=128), [], [wk])
                dma(BM[0:1, :], b_mod[l:l + 1, cb * 512:(cb + 1) * 512], [], ["BM"])
                ps, pk = bank()
                for k in range(8):
                    mm(ps[0:1, :], SC[:, k:k + 1], wt_[:, k * 512:(k + 1) * 512], k == 0, k == 7, [wk, "SC"], [pk])
                tt(MR[0:1, :], ps[0:1, :], BM[0:1, :], ALU.add, [pk, "BM"], ["MR"])
                dma(modrow[l:l + 1, cb * 512:(cb + 1) * 512], MR[0:1, :], ["MR"], ["modrow"])
            P.barrier()
            aoff["o"] = persist_end
            A2, B2, G1, G2, GB2 = [af(D) for _ in range(5)]
            lay_end = aoff["o"]
            A1, B1, TMPB = af(D), af(D), af(D)
            for (A_, B_, G_, j0, rp) in ((A1, B1, G1, 0, 0), (A2, B2, G2, 3, 1)):
                dma(B_, bc(modrow[l:l + 1, (j0 + 0) * D:(j0 + 1) * D]), ["modrow"], [id(B_)])
                dma(A_, bc(modrow[l:l + 1, (j0 + 1) * D:(j0 + 2) * D]), ["modrow"], [id(A_)])
                dma(G_, bc(modrow[l:l + 1, (j0 + 2) * D:(j0 + 3) * D]), ["modrow"], [id(G_)])
                dma(TMPB, bc(rowp[l:l + 1, rp * D:(rp + 1) * D]), [], ["TMPB"])
                stt(A_, A_, 1.0, TMPB, ALU.add, ALU.mult, [id(A_), "TMPB"], [id(A_)])
            dma(TMPB, bc(rowp[l:l + 1, 2 * D:3 * D]), [], ["TMPB"])
            tt(GB2, G2, TMPB, ALU.mult, [id(G2), "TMPB"], ["GB2"])

            XT = [af(D), af(D)]; XN = af(D); XNB = ab(D); HT = ab(8 * 512); HTv = HT.rearrange("p (k t) -> p k t", k=8)
            SS = af(4)
            norm_tile = make_norm(XN, XNB, SS, HTv)
            WIN = ab(8 * DIN)
            WINv = WIN.rearrange("p (k n) -> p k n", k=8)
            for k in range(8):
                dma(WINv[:, k, :], w_in[l * D + k * 128: l * D + (k + 1) * 128, :], [], ["WIN"], cast=True)
            ZST = [ab(512) for _ in range(4)]; ZSF = af(512)
            CVB = [ab(8192), ab(8192)]
            xsrc = x_in if l == 0 else xs
            ncv = 0
            for b in range(NB):
                cv = CVB[ncv % 2]; ck_ = "CVB%d" % (ncv % 2); ncv += 1
                dma(cv[:, 0:4096].rearrange("p (k n) -> p k n", k=8), w1[l * D:(l + 1) * D, b * 512:(b + 1) * 512].rearrange("(k p) n -> p k n", p=128),
                    [], [ck_], cast=True)
                dma(W1B[b * 128:(b + 1) * 128, :], cv[:, 0:4096], [ck_], ["W1B"], q="pool")
                if b % 2 == 0:
                    p2 = b // 2
                    cv = CVB[ncv % 2]; ck_ = "CVB%d" % (ncv % 2); ncv += 1
                    dma(cv.rearrange("p (f n) -> p f n", f=32), w2[l * DFF:(l + 1) * DFF, p2 * 256:(p2 + 1) * 256].rearrange("(f p) n -> p f n", p=128),
                        [], [ck_], cast=True)
                    dma(W2B[p2 * 128:(p2 + 1) * 128, :], cv, [ck_], ["W2B"], q="pool")
                for t4 in range(4):
                    xt = XT[t4 % 2]; xk = "XT%d" % (t4 % 2)
                    r0 = b * 512 + t4 * 128
                    dma(xt, xsrc[r0:r0 + 128, :], ["xs"], [xk])
                    norm_tile(xt, xk, A1, B1, t4, "HT")
                zi = 0
                for (c0, ntile, dst) in ((C_A, 8, zAT), (C_C, 4, zCT), (C_QK, 4, zQKT)):
                    for ct in range(ntile):
                        ps, pk = bank()
                        for k in range(8):
                            mm(ps, WINv[:, k, c0 + ct * 128:c0 + (ct + 1) * 128], HTv[:, k, :], k == 0, k == 7, ["WIN", "HT"], [pk])
                        zs = ZST[zi % 4]; zk = "ZST%d" % (zi % 4); zi += 1
                        act(zs, ps, AF.Identity, [pk], [zk])
                        dma(dst[ct * 128:(ct + 1) * 128, b * 512:(b + 1) * 512], zs, [zk], [id(dst)])
                ps, pk = bank()
                for k in range(8):
                    mm(ps[0:16, :], WINv[:, k, C_I:C_I + 16], HTv[:, k, :], k == 0, k == 7, ["WIN", "HT"], [pk])
                cp(ZSF[0:16, :], ps[0:16, :], [pk], ["ZSF"])
                dma(zG[:, b * 512:(b + 1) * 512], ZSF[0:16, :], ["ZSF"], ["zG"])
                for t4 in range(4):
                    r0 = b * 512 + t4 * 128
                    for (c0, dst) in ((C_B, zB), (C_V, zV), (C_O, zO)):
                        ps, pk = bank()
                        for k in range(8):
                            mm(ps[:, 0:256], HTv[:, k, t4 * 128:(t4 + 1) * 128], WINv[:, k, c0:c0 + 256], k == 0, k == 7, ["WIN", "HT"], [pk])
                        zs = ZST[zi % 4]; zk = "ZST%d" % (zi % 4); zi += 1
                        act(zs[:, 0:256], ps[:, 0:256], AF.Identity, [pk], [zk])
                        dma(dst[r0:r0 + 128, :], zs[:, 0:256], [zk], [id(dst)])
            P.barrier()

            for nm_, fn_ in (("pool", pool_phase), ("conv", conv_phase), ("rwkv", rwkv_phase), ("mlstm", mlstm_phase)):
                if nm_ in MIX:
                    aoff["o"] = lay_end
                    fn_(l)
                    P.barrier()

            aoff["o"] = lay_end
            XT = [af(D), af(D)]; XN = af(D); XNB = ab(D); HT = ab(8 * 512); HTv = HT.rearrange("p (k t) -> p k t", k=8)
            SS = af(4)
            norm_tile = make_norm(XN, XNB, SS, HTv)
            WOUT = ab(8 * D); WOUTv = WOUT.rearrange("p (k n) -> p k n", k=8)
            for k in range(8):
                dma(WOUTv[:, k, :], w_out[l * D + k * 128: l * D + (k + 1) * 128, :], [], ["WOUT"], cast=True)
            W1P = [ab(8 * 512), ab(8 * 512)]; W2P = [ab(32 * 256), ab(32 * 256)]
            CATT = ab(8 * 512); CATTv = CATT.rearrange("p (k t) -> p k t", k=8)
            X1 = af(4 * D); X1v = X1.rearrange("p (t n) -> p t n", t=4)
            HID = ab(32 * 512); HIDv = HID.rearrange("p (f t) -> p f t", f=32)
            RL = [af(512), af(512)]; TM3 = [af(512), af(512)]
            FGT = None
            if l == nlayers - 1:
                FGT = af(D)
                dma(FGT, bc(rowp[l:l + 1, 3 * D:4 * D]), [], ["FGT"])
            for b in range(NB):
                dma(CATTv, catT[:, b * 512:(b + 1) * 512].rearrange("(k p) t -> p k t", p=128), [id(catT)], ["CATT"])
                for t4 in range(4):
                    xt = XT[t4 % 2]; xk = "XT%d" % (t4 % 2)
                    r0 = b * 512 + t4 * 128
                    dma(xt, xsrc[r0:r0 + 128, :], ["xs"], [xk])
                    for half in range(2):
                        ps, pk = bank()
                        for k in range(8):
                            mm(ps, CATTv[:, k, t4 * 128:(t4 + 1) * 128], WOUTv[:, k, half * 512:(half + 1) * 512], k == 0, k == 7, ["CATT", "WOUT"], [pk])
                        tm = TM3[half]; tk = "TM3%d" % half
                        tt(tm, ps, G1[:, half * 512:(half + 1) * 512], ALU.mult, [pk, id(G1)], [tk])
                        tt(X1v[:, t4, half * 512:(half + 1) * 512], tm, xt[:, half * 512:(half + 1) * 512], ALU.add, [tk, xk], ["X1_%d" % t4])
                    norm_tile(X1v[:, t4, :], "X1_%d" % t4, A2, B2, t4, "HT")
                    tt(X1v[:, t4, :], X1v[:, t4, :], GB2, ALU.add, ["X1_%d" % t4, "GB2"], ["X1_%d" % t4])
                for p1 in range(8):
                    wp = W1P[p1 % 2]; wk = "W1P%d" % (p1 % 2); wpv = wp.rearrange("p (k n) -> p k n", k=8)
                    dma(wp, W1B[p1 * 128:(p1 + 1) * 128, :], ["W1B"], [wk])
                    for ft in range(4):
                        f = p1 * 4 + ft
                        ps, pk = bank()
                        for k in range(8):
                            mm(ps, wpv[:, k, ft * 128:(ft + 1) * 128], HTv[:, k, :], k == 0, k == 7, [wk, "HT"], [pk])
                        rl = RL[f % 2]; rk = "RL%d" % (f % 2)
                        act(rl, ps, AF.Relu, [pk, "COLT"], [rk], bias=col(l, "b1", f))
                        tt(HIDv[:, f, :], rl, rl, ALU.mult, [rk], ["HID"], eng="pool" if f % 2 else "dve")
                for p2 in range(4):
                    wp = W2P[p2 % 2]; wk = "W2P%d" % (p2 % 2); wpv = wp.rearrange("p (f n) -> p f n", f=32)
                    dma(wp, W2B[p2 * 128:(p2 + 1) * 128, :], ["W2B"], [wk])
                    for t4 in range(4):
                        ps, pk = bank()
                        for f in range(32):
                            mm(ps[:, 0:256], HIDv[:, f, t4 * 128:(t4 + 1) * 128], wpv[:, f, :], f == 0, f == 31, ["HID", wk], [pk])
                        tm = TM3[t4 % 2]; tk = "TM3%d" % (t4 % 2)
                        tt(tm[:, 0:256], ps[:, 0:256], G2[:, p2 * 256:(p2 + 1) * 256], ALU.mult, [pk, id(G2)], [tk])
                        tt(X1v[:, t4, p2 * 256:(p2 + 1) * 256], tm[:, 0:256], X1v[:, t4, p2 * 256:(p2 + 1) * 256], ALU.add, [tk, "X1_%d" % t4], ["X1_%d" % t4])
                for t4 in range(4):
                    r0 = b * 512 + t4 * 128
                    xk = "X1_%d" % t4
                    if l < nlayers - 1:
                        dma(xs[r0:r0 + 128, :], X1v[:, t4, :], [xk], ["xs"])
                    else:
                        act(XN, X1v[:, t4, :], AF.Square, [xk], ["XN", "SS"], accum=SS[:, 0:1])
                        rsqrt(SS[:, 1:2], SS[:, 0:1], 1.0 / D, EPS, ["SS"], ["SS1"])
                        stt(XN, X1v[:, t4, :], SS[:, 1:2], FGT, ALU.mult, ALU.mult, [xk, "SS1", "FGT"], ["XN"])
                        dma(y_out[r0:r0 + 128, :], XN, ["XN"], ["yout"])
            P.barrier()

        print('arena peak', aoff.get('max'), 'of', ARENA)
        P.barrier(["sp"])

        @block.tensor
        def _(e):
            P.emit("pe", e)

        @block.scalar
        def _(e):
            P.emit("act", e)

        @block.vector
        def _(e):
            P.emit("dve", e)

        @block.gpsimd
        def _(e):
            P.emit("pool", e)

        @block.sync
        def _(e):
            P.emit("sp", e)
    return nc


MIX = ("pool", "conv", "rwkv", "mlstm")
_NC_CACHE = {}


def kernel(**inp):
    inp = {k: np.asarray(v) for k, v in inp.items()}
    shared = _shared_consts(inp)
    xp = inp["x_prompt"].astype(np.float32); xsm = inp["x_sample"].astype(np.float32)
    cores = []
    for i in range(2):
        cores.append(_prep_core("p", xp[i * 16:(i + 1) * 16], inp["c_ctx"], inp))
    for b in range(2):
        cores.append(_prep_core("s", xsm[b], inp["c"][b], inp, inp["state_rwkv"][b], inp["state_mlstm_C"][b],
                                inp["state_mlstm_n"][b], inp["state_mlstm_m"][b]))
    in_maps = []
    for i in range(8):
        m = dict(shared); m.update(cores[i % 4]); in_maps.append(m)
    if "nc" not in _NC_CACHE:
        _NC_CACHE["nc"] = build()
    res = run_bass_kernel_spmd(_NC_CACHE["nc"], in_maps, core_ids=list(range(8)))
    R = res.results
    y_prompt = np.concatenate([R[0]["y"], R[1]["y"]], 0).reshape(32, 256, D).astype(np.float32)
    y_sample = np.stack([R[2]["y"], R[3]["y"]], 0).reshape(2, T, D).astype(np.float32)
    nr = np.zeros((32, L, 2, 4, 64, 64), np.float32); ncc = np.zeros((32, L, 2, 4, 64, 64), np.float32)
    nn = np.zeros((32, L, 2, 4, 64), np.float32); nm = np.zeros((32, L, 2, 4), np.float32)
    for i in range(2):
        sr = R[i]["st_r"].reshape(L, NSEG, 2, 4, 64, 64)
        scc = R[i]["st_c"].reshape(L, NSEG, 2, 4, 64, 65)
        smm = R[i]["st_m"].reshape(L, 2, 4, NSEG)
        nr[i * 16:(i + 1) * 16] = sr.transpose(1, 0, 2, 3, 5, 4)
        ncc[i * 16:(i + 1) * 16] = scc[..., :64].transpose(1, 0, 2, 3, 5, 4)
        nn[i * 16:(i + 1) * 16] = scc[..., 64].transpose(1, 0, 2, 3, 4)
        nm[i * 16:(i + 1) * 16] = smm.transpose(3, 0, 1, 2)
    return (y_prompt, y_sample, nr, ncc, nn, nm)
```
